# Optimizing a Trainium2 kernel written in Bass

```python
import jax, jax.numpy as jnp
from jax import lax
import numpy as np

D_MODEL = 1024
BATCH = 8
SEQ = 2048
DEPTH = 4
DEC_BATCH = 128
DEC_SEQ = 4
PAST_LEN = 16384
PAGE_SIZE = 128

N_META = 16
N_EVEN = (DEPTH + 1) // 2
N_ODD = DEPTH // 2
D_SSM = D_MODEL // 2
SSM_GROUP = 16
N_SSM_GROUPS = D_SSM // SSM_GROUP
SSM_STATE = 64
D_SCONV = D_MODEL - D_SSM
SCONV_W = 3
D_IN_EVEN = D_SSM + 3 * D_SCONV
D_CONF = D_MODEL
CONF_W = 31
D_FF = ((8 * D_MODEL // 3 + 127) // 128) * 128
FFN_W = 3
EPS = 1e-6

kernel_name = 'hybrid_s5_shortconv_conformer_convffn_step'


def rms_norm(x, g):
    xf = x.astype(jnp.float32)
    y = xf * lax.rsqrt(jnp.mean(xf * xf, axis=-1, keepdims=True) + EPS)
    return (y * g.astype(jnp.float32)).astype(x.dtype)


def layer_norm(x, g, b):
    xf = x.astype(jnp.float32)
    mu = jnp.mean(xf, axis=-1, keepdims=True)
    xc = xf - mu
    y = xc * lax.rsqrt(jnp.mean(xc * xc, axis=-1, keepdims=True) + EPS)
    return (y * g.astype(jnp.float32) + b.astype(jnp.float32)).astype(x.dtype)


def causal_dwconv(x, buf, w):
    k = w.shape[0]
    c = x.shape[-1]
    full = jnp.concatenate([buf.astype(x.dtype), x], axis=1)
    y = lax.conv_general_dilated(full, w[:, None, :].astype(x.dtype), window_strides=(1,),
                                 padding='VALID', dimension_numbers=('NWC', 'WIO', 'NWC'),
                                 feature_group_count=c)
    return y, full[:, -(k - 1):]


def s5_scan(u, h0_re, h0_im, lam_re, lam_im, log_dt, b_re, b_im, c_re, c_im, d_skip):
    f32 = jnp.float32
    n, l, _ = u.shape
    uf = u.astype(f32).reshape(n, l, N_SSM_GROUPS, SSM_GROUP)
    lam = lax.complex(lam_re.astype(f32), lam_im.astype(f32))
    dt = jnp.exp(log_dt.astype(f32))[:, None]
    lam_bar = jnp.exp(lam * dt)
    b_bar = ((lam_bar - 1.0) / lam)[..., None] * lax.complex(b_re.astype(f32), b_im.astype(f32))
    bu = jnp.einsum('nlgi,gpi->nlgp', uf.astype(jnp.complex64), b_bar)
    h0 = lax.complex(h0_re.astype(f32), h0_im.astype(f32))
    bu = bu.at[:, 0].add(lam_bar * h0)
    a = jnp.broadcast_to(lam_bar, bu.shape)

    def combine(e1, e2):
        a1, b1 = e1
        a2, b2 = e2
        return a1 * a2, a2 * b1 + b2

    _, h = lax.associative_scan(combine, (a, bu), axis=1)
    c = lax.complex(c_re.astype(f32), c_im.astype(f32))
    y = jnp.einsum('nlgp,gip->nlgi', h, c).real + d_skip.astype(f32).reshape(N_SSM_GROUPS, SSM_GROUP) * uf
    h_last = h[:, -1]
    return y.reshape(n, l, D_SSM), jnp.real(h_last), jnp.imag(h_last)


def even_mixer(xn, h_re, h_im, sbuf, p, i):
    z = xn @ p['w_in_even'][i]
    u_a, x_b, g_b, g_c = jnp.split(z, [D_SSM, D_SSM + D_SCONV, D_SSM + 2 * D_SCONV], axis=-1)
    y_a, nh_re, nh_im = s5_scan(u_a, h_re, h_im, p['ssm_lam_re'][i], p['ssm_lam_im'][i], p['ssm_log_dt'][i],
                                p['ssm_b_re'][i], p['ssm_b_im'][i], p['ssm_c_re'][i], p['ssm_c_im'][i],
                                p['ssm_d'][i])
    y_a = jax.nn.gelu(y_a)
    y_a = y_a * jax.nn.sigmoid(y_a @ p['w_glu'][i].astype(jnp.float32) + p['b_glu'][i].astype(jnp.float32))
    conv, nbuf = causal_dwconv(g_c * x_b, sbuf, p['w_sconv'][i])
    y_b = g_b * conv
    out = jnp.concatenate([y_a.astype(xn.dtype), y_b], axis=-1) @ p['w_out_even'][i]
    return out, nh_re.astype(h_re.dtype), nh_im.astype(h_im.dtype), nbuf


def odd_mixer(xn, cbuf, p, i):
    h = xn @ p['w_conf_pw1'][i] + p['b_conf_pw1'][i]
    a, g = jnp.split(h, [D_CONF], axis=-1)
    v = a * jax.nn.sigmoid(g)
    conv, nbuf = causal_dwconv(v, cbuf, p['w_conf_dw'][i])
    conv = layer_norm(conv, p['conf_ln_g'][i], p['conf_ln_b'][i])
    out = jax.nn.silu(conv) @ p['w_conf_pw2'][i] + p['b_conf_pw2'][i]
    return out, nbuf


def conv_ffn(xn, fbuf, p, l):
    u = xn @ p['w_ffn_up'][l]
    a, g = jnp.split(u, [D_FF], axis=-1)
    a_c, nbuf = causal_dwconv(a, fbuf, p['w_ffn_conv'][l])
    return (jax.nn.silu(a_c) * g) @ p['w_ffn_down'][l], nbuf


def trunk(x, ssm_re, ssm_im, sconv, cconv, ffn_buf, p):
    n_re, n_im, n_s, n_c, n_f = [], [], [], [], []
    for l in range(DEPTH):
        i = l // 2
        xn = rms_norm(x, p['norm_mix'][l])
        if l % 2 == 0:
            out, r, im, sb = even_mixer(xn, ssm_re[i], ssm_im[i], sconv[i], p, i)
            n_re.append(r)
            n_im.append(im)
            n_s.append(sb)
        else:
            out, cb = odd_mixer(xn, cconv[i], p, i)
            n_c.append(cb)
        x = x + out
        out, fb = conv_ffn(rms_norm(x, p['norm_ffn'][l]), ffn_buf[l], p, l)
        n_f.append(fb)
        x = x + out
    y = rms_norm(x, p['norm_final'])
    return y, jnp.stack(n_re), jnp.stack(n_im), jnp.stack(n_s), jnp.stack(n_c), jnp.stack(n_f)


def setup_inputs(seed: int = 0) -> dict:
    key = jax.random.key(seed)
    ks = iter(jax.random.split(key, 48))

    def nrm(shape, scale):
        return jax.random.normal(next(ks), shape, jnp.float32) * scale

    G, P = N_SSM_GROUPS, SSM_STATE
    lam_im = jnp.broadcast_to(jnp.pi * jnp.arange(P, dtype=jnp.float32), (N_EVEN, G, P))
    return {
        'x_prompt': nrm((BATCH, SEQ, D_MODEL), 1.0),
        'x_sample': nrm((DEC_BATCH, DEC_SEQ, D_MODEL), 1.0),
        'state_ssm_re': nrm((N_EVEN, DEC_BATCH, G, P), 0.5),
        'state_ssm_im': nrm((N_EVEN, DEC_BATCH, G, P), 0.5),
        'state_sconv': nrm((N_EVEN, DEC_BATCH, SCONV_W - 1, D_SCONV), 0.5),
        'state_cconv': nrm((N_ODD, DEC_BATCH, CONF_W - 1, D_CONF), 0.5),
        'state_ffn': nrm((DEPTH, DEC_BATCH, FFN_W - 1, D_FF), 0.5),
        'meta_tokens': nrm((N_META, D_MODEL), 1.0),
        'norm_mix': 1.0 + nrm((DEPTH, D_MODEL), 0.02),
        'norm_ffn': 1.0 + nrm((DEPTH, D_MODEL), 0.02),
        'norm_final': 1.0 + nrm((D_MODEL,), 0.02),
        'w_in_even': nrm((N_EVEN, D_MODEL, D_IN_EVEN), D_MODEL ** -0.5),
        'ssm_lam_re': -0.5 + nrm((N_EVEN, G, P), 0.01),
        'ssm_lam_im': lam_im + nrm((N_EVEN, G, P), 0.01),
        'ssm_log_dt': jax.random.uniform(next(ks), (N_EVEN, G), jnp.float32, np.log(1e-3), np.log(1e-1)),
        'ssm_b_re': nrm((N_EVEN, G, P, SSM_GROUP), (2 * SSM_GROUP) ** -0.5),
        'ssm_b_im': nrm((N_EVEN, G, P, SSM_GROUP), (2 * SSM_GROUP) ** -0.5),
        'ssm_c_re': nrm((N_EVEN, G, SSM_GROUP, P), (2 * P) ** -0.5),
        'ssm_c_im': nrm((N_EVEN, G, SSM_GROUP, P), (2 * P) ** -0.5),
        'ssm_d': nrm((N_EVEN, D_SSM), 1.0),
        'w_glu': nrm((N_EVEN, D_SSM, D_SSM), D_SSM ** -0.5),
        'b_glu': nrm((N_EVEN, D_SSM), 0.02),
        'w_sconv': nrm((N_EVEN, SCONV_W, D_SCONV), SCONV_W ** -0.5),
        'w_out_even': nrm((N_EVEN, D_SSM + D_SCONV, D_MODEL), (D_SSM + D_SCONV) ** -0.5),
        'w_conf_pw1': nrm((N_ODD, D_MODEL, 2 * D_CONF), D_MODEL ** -0.5),
        'b_conf_pw1': nrm((N_ODD, 2 * D_CONF), 0.02),
        'w_conf_dw': nrm((N_ODD, CONF_W, D_CONF), CONF_W ** -0.5),
        'conf_ln_g': 1.0 + nrm((N_ODD, D_CONF), 0.02),
        'conf_ln_b': nrm((N_ODD, D_CONF), 0.02),
        'w_conf_pw2': nrm((N_ODD, D_CONF, D_MODEL), D_CONF ** -0.5),
        'b_conf_pw2': nrm((N_ODD, D_MODEL), 0.02),
        'w_ffn_up': nrm((DEPTH, D_MODEL, 2 * D_FF), D_MODEL ** -0.5),
        'w_ffn_conv': nrm((DEPTH, FFN_W, D_FF), FFN_W ** -0.5),
        'w_ffn_down': nrm((DEPTH, D_FF, D_MODEL), D_FF ** -0.5),
    }


def reference(x_prompt, x_sample, state_ssm_re, state_ssm_im, state_sconv, state_cconv, state_ffn,
              meta_tokens, norm_mix, norm_ffn, norm_final, w_in_even, ssm_lam_re, ssm_lam_im, ssm_log_dt,
              ssm_b_re, ssm_b_im, ssm_c_re, ssm_c_im, ssm_d, w_glu, b_glu, w_sconv, w_out_even,
              w_conf_pw1, b_conf_pw1, w_conf_dw, conf_ln_g, conf_ln_b, w_conf_pw2, b_conf_pw2,
              w_ffn_up, w_ffn_conv, w_ffn_down):
    p = dict(norm_mix=norm_mix, norm_ffn=norm_ffn, norm_final=norm_final, w_in_even=w_in_even,
             ssm_lam_re=ssm_lam_re, ssm_lam_im=ssm_lam_im, ssm_log_dt=ssm_log_dt,
             ssm_b_re=ssm_b_re, ssm_b_im=ssm_b_im, ssm_c_re=ssm_c_re, ssm_c_im=ssm_c_im, ssm_d=ssm_d,
             w_glu=w_glu, b_glu=b_glu, w_sconv=w_sconv, w_out_even=w_out_even,
             w_conf_pw1=w_conf_pw1, b_conf_pw1=b_conf_pw1, w_conf_dw=w_conf_dw,
             conf_ln_g=conf_ln_g, conf_ln_b=conf_ln_b, w_conf_pw2=w_conf_pw2, b_conf_pw2=b_conf_pw2,
             w_ffn_up=w_ffn_up, w_ffn_conv=w_ffn_conv, w_ffn_down=w_ffn_down)

    dt = x_prompt.dtype
    meta = jnp.broadcast_to(meta_tokens.astype(dt)[None], (BATCH, N_META, D_MODEL))
    xp = jnp.concatenate([meta, x_prompt], axis=1)
    zp_re = jnp.zeros((N_EVEN, BATCH, N_SSM_GROUPS, SSM_STATE), dt)
    zp_s = jnp.zeros((N_EVEN, BATCH, SCONV_W - 1, D_SCONV), dt)
    zp_c = jnp.zeros((N_ODD, BATCH, CONF_W - 1, D_CONF), dt)
    zp_f = jnp.zeros((DEPTH, BATCH, FFN_W - 1, D_FF), dt)
    yp, p_ssm_re, p_ssm_im, p_sconv, p_cconv, p_ffn = trunk(xp, zp_re, zp_re, zp_s, zp_c, zp_f, p)
    y_prompt = yp[:, N_META:]

    y_sample, s_ssm_re, s_ssm_im, s_sconv, s_cconv, s_ffn = trunk(
        x_sample, state_ssm_re, state_ssm_im, state_sconv, state_cconv, state_ffn, p)

    return (y_prompt, y_sample, p_ssm_re, p_ssm_im, p_sconv, p_cconv, p_ffn,
            s_ssm_re, s_ssm_im, s_sconv, s_cconv, s_ffn)
```

```python
import os
import numpy as np
from contextlib import ExitStack
import concourse.bass as bass
import concourse.mybir as mybir
from concourse.bass_utils import run_bass_kernel_spmd

F32 = mybir.dt.float32
BF16 = mybir.dt.bfloat16
AF = mybir.ActivationFunctionType
ALU = mybir.AluOpType

NCORES = 8
D = 1024
DEPTH = 4
NCH = 8
DFF = 2816
NFF = 22
TP = 1032
NSEQ = 16
TS = 64
G = 32
EPS = 1e-6
POWS = [1 << k for k in range(11)]
TWO_PI = float(2 * np.pi)
MAGIC = 12582912.0


class Buf:
    __slots__ = ("w", "r")

    def __init__(self):
        self.w = None
        self.r = []


class DSem:
    def __init__(self, name):
        self.name = name
        self.total = 0
        self.sem = None


class Op:
    __slots__ = ("eng", "fn", "deps", "dma", "sem", "cnt", "sig", "dsem")

    def __init__(self, eng, fn, dma):
        self.eng = eng
        self.fn = fn
        self.dma = dma
        self.deps = []
        self.sem = None
        self.cnt = 0
        self.sig = False
        self.dsem = None


class Prog:
    ENGS = ("pe", "act", "dve", "pool", "sp")

    def __init__(self, nc):
        self.nc = nc
        self.ops = {e: [] for e in self.ENGS}
        self.dsems = []
        self.pending = {e: [] for e in self.ENGS}
        self.last = {e: None for e in self.ENGS}
        self.dma_since = []

    def dsem(self, name):
        d = DSem(name)
        self.dsems.append(d)
        return d

    def barrier(self):
        L = [self.last[e] for e in ("pe", "act", "dve", "pool") if self.last[e] is not None]
        L += self.dma_since
        self.dma_since = []
        for e in ("pe", "act", "dve", "sp", "pool"):
            self.pending[e] = self.pending[e] + L

    def op(self, eng, fn, reads=(), writes=(), dma=None, nobar=False):
        isdma = dma is not None
        o = Op(eng, fn, isdma)
        deps = []
        for b in reads:
            if b.w is not None:
                deps.append((b.w, "raw"))
        for b in writes:
            if b.w is not None:
                deps.append((b.w, "waw"))
            for r in b.r:
                deps.append((r, "war"))
        if not nobar:
            for d in self.pending[eng]:
                deps.append((d, "bar"))
            self.pending[eng] = []
        seen = set()
        for d, kind in deps:
            if d is o or id(d) in seen:
                continue
            if (not isdma) and (not d.dma) and d.eng == eng:
                if eng == "pe":
                    continue
            seen.add(id(d))
            if d.dma:
                o.deps.append((d, d.dsem.total))
            else:
                o.deps.append((d, None))
                d.sig = True
        for b in reads:
            b.r.append(o)
        for b in writes:
            b.w = o
            b.r = []
        if isdma:
            dma.total += 16
            o.dsem = dma
            o.cnt = dma.total
            o.sig = True
            if not nobar:
                self.dma_since.append(o)
        else:
            self.last[eng] = o
        self.ops[eng].append(o)
        return o

    def build(self, es):
        nc = self.nc
        sems = {e: es.enter_context(nc.semaphore("s_" + e)) for e in self.ENGS}
        for d in self.dsems:
            if d.total > 0:
                d.sem = es.enter_context(nc.semaphore("d_" + d.name))
        for e in self.ENGS:
            c = 0
            for o in self.ops[e]:
                if o.dma:
                    o.sem = o.dsem.sem
                elif o.sig:
                    c += 1
                    o.sem = sems[e]
                    o.cnt = c
        if os.environ.get("KBAR", "1") == "1":
            allsems = list(sems.values()) + [d.sem for d in self.dsems if d.sem is not None]
            for sm in allsems:
                nc.gpsimd.sem_clear(sm)
            nc.all_engine_barrier()
        block = es.enter_context(nc.Block())
        prog = self

        def run(engine, e):
            seen = {}
            used = {}
            for o in prog.ops[e]:
                need = {}
                for d, thr in o.deps:
                    c = thr if thr is not None else d.cnt
                    s = d.sem
                    if seen.get(s.num, 0) >= c:
                        continue
                    if need.get(s.num, (None, 0))[1] < c:
                        need[s.num] = (s, c)
                for s, c in need.values():
                    seen[s.num] = c
                    engine.wait_ge(s, c)
                ins = o.fn(engine)
                if o.sig:
                    ins.then_inc(o.sem, 16 if o.dma else 1)
                    if o.dma:
                        used[o.dsem.name] = o.dsem
            for d in used.values():
                if seen.get(d.sem.num, 0) < d.total:
                    engine.wait_ge(d.sem, d.total)

        @block.tensor
        def _(eng):
            run(eng, "pe")

        @block.scalar
        def _(eng):
            run(eng, "act")

        @block.vector
        def _(eng):
            run(eng, "dve")

        @block.gpsimd
        def _(eng):
            run(eng, "pool")

        @block.sync
        def _(eng):
            run(eng, "sp")


class Builder:
    def __init__(self):
        self.nc = bass.Bass("TRN2", target_bir_lowering=False)
        self.es = ExitStack()
        self.P = Prog(self.nc)
        self.din = {}
        self.dout = {}
        self.rr = 0

    def inp(self, name, shape):
        t = self.nc.dram_tensor(name, list(shape), F32, kind="ExternalInput").ap()
        self.din[name] = t
        return t

    def outp(self, name, shape):
        t = self.nc.dram_tensor(name, list(shape), F32, kind="ExternalOutput").ap()
        self.dout[name] = t
        return t

    def sb(self, name, shape, dt):
        return self.es.enter_context(self.nc.sbuf_tensor("sb_" + name, list(shape), dt))

    def arena_reset(self):
        self.P.barrier()
        self.aoff = 0

    def alloc(self, cols, dt):
        n32 = cols if dt == F32 else (cols + 1) // 2
        n32 = (n32 + 1) // 2 * 2
        o = self.aoff
        self.aoff += n32
        assert self.aoff <= self.ARENA, ("arena overflow", self.aoff)
        v = self.arena[:, o:o + n32]
        if dt != F32:
            v = v.bitcast(dt)
        return v

    def bank(self):
        i = self.rr % 8
        self.rr += 1
        return self.banks[i], self.bankb[i]

    def act(self, out, in_, func, reads, writes, bias=None, scale=None):
        kw = {}
        if bias is not None:
            kw["bias"] = bias
        if scale is not None:
            kw["scale"] = scale
        return self.P.op("act", lambda e: e.activation(out=out, in_=in_, func=func, **kw), reads, writes)

    def tt(self, eng, out, in0, in1, op, reads, writes):
        return self.P.op(eng, lambda e: e.tensor_tensor(out=out, in0=in0, in1=in1, op=op), reads, writes)

    def ts(self, eng, out, in0, s1, op0, reads, writes, s2=None, op1=None):
        if s2 is None:
            return self.P.op(eng, lambda e: e.tensor_scalar(out=out, in0=in0, scalar1=s1, scalar2=None, op0=op0), reads, writes)
        return self.P.op(eng, lambda e: e.tensor_scalar(out=out, in0=in0, scalar1=s1, scalar2=s2, op0=op0, op1=op1), reads, writes)

    def stt(self, out, in0, scalar, in1, op0, op1, reads, writes):
        return self.P.op("dve", lambda e: e.scalar_tensor_tensor(out=out, in0=in0, scalar=scalar, in1=in1, op0=op0, op1=op1), reads, writes)

    def cp(self, eng, out, in_, reads, writes):
        if eng == "act":
            return self.act(out, in_, AF.Copy, reads, writes)
        return self.P.op(eng, lambda e: e.tensor_copy(out=out, in_=in_), reads, writes)

    def memset(self, eng, ap, val, writes):
        return self.P.op(eng, lambda e: e.memset(ap, val), (), writes)

    def dma(self, eng, out, in_, dsem, reads=(), writes=(), nobar=False):
        return self.P.op(eng, lambda e: e.dma_start(out=out, in_=in_), reads, writes, dma=dsem, nobar=nobar)

    def slab(self, Wd, K, c0, ncols=256):
        KT = K // 128
        i = self.wrr % self.NSLOT
        self.wrr += 1
        t = self.wslots[i][:, 0:KT * ncols].rearrange("p (kt m) -> p kt m", m=ncols)
        src = Wd[:, c0:c0 + ncols].rearrange("(kt p) m -> p kt m", p=128)
        self.dma("pool", t, src, self.wsem[i], writes=[self.wbuf[i]], nobar=True)
        return t, self.wbuf[i]

    def mm_group(self, bank_ap, bankb, lhs_list, rhs_list, reads):
        n = len(lhs_list)

        def fn(e):
            ins = None
            for k in range(n):
                ins = e.matmul(bank_ap, lhs_list[k], rhs_list[k], start=(k == 0), stop=(k == n - 1))
            return ins
        return self.P.op("pe", fn, reads, [bankb])


class _Stop(Exception):
    pass


_CNT = [0]


def _chk(name):
    if os.environ.get("KSTOP", "") == name:
        raise _Stop()


def build_program():
    B = Builder()
    try:
        _build_body(B)
    except _Stop:
        pass
    B.P.build(B.es)
    if os.environ.get("KENDBAR", "1") == "1":
        B.nc.all_engine_barrier()
    return B


def _build_body(B):
    nc, P, es = B.nc, B.P, B.es
    W = TP + TS

    x_d = B.inp("x", [D, 2 * TP + TS])
    h0_d = B.inp("h0", [2, 128, G, NSEQ])
    sch_d = B.inp("sconv_h", [2, 512, 2 * NSEQ])
    cch_d = B.inp("cconv_h", [2, D, 30 * NSEQ])
    ffh_d = B.inp("ffn_h", [DEPTH, DFF, 2 * NSEQ])
    nmix_d = B.inp("nmix", [128, DEPTH, NCH])
    nffn_d = B.inp("nffn", [128, DEPTH, NCH])
    nfin_d = B.inp("nfin", [128, NCH])
    lamre_d = B.inp("lamre", [2, 128, G])
    lamim_d = B.inp("lamim", [2, 128, G])
    logdt_d = B.inp("logdt", [2, 128, G])
    bre_d = B.inp("bre", [2, 64, G * 16])
    bim_d = B.inp("bim", [2, 64, G * 16])
    cn_d = B.inp("cn", [2, 128, 4 * 128])
    ssmd_d = B.inp("ssmd", [2, 128, 4])
    bglu_d = B.inp("bglu", [2, 128, 4])
    wsc_d = B.inp("wsc", [2, 128, 4 * 3])
    bpw1_d = B.inp("bpw1", [2, 128, 16])
    wdw_d = B.inp("wdw", [2, 128, NCH * 31])
    lng_d = B.inp("lng", [2, 128, NCH])
    lnb_d = B.inp("lnb", [2, 128, NCH])
    bpw2_d = B.inp("bpw2", [2, 128, NCH])
    wfc_d = B.inp("wfc", [DEPTH, 128, NFF * 3])
    w_in_d = B.inp("w_in_even", [2, D, 2048])
    w_glu_d = B.inp("w_glu", [2, 512, 512])
    w_out_d = B.inp("w_out_even", [2, D, D])
    w_pw1_d = B.inp("w_conf_pw1", [2, D, 2048])
    w_pw2_d = B.inp("w_conf_pw2", [2, D, D])
    w_up_d = B.inp("w_ffn_up", [DEPTH, D, 2 * DFF])
    w_dn_d = B.inp("w_ffn_down", [DEPTH, DFF, D])

    y_d = B.outp("y", [D, 2 * TP + TS])
    hfin_d = B.outp("hfin", [2, 128, G * 17])
    sco_d = B.outp("sconv_o", [2, 128, 4 * 34])
    cco_d = B.outp("cconv_o", [2, D, 26 * NSEQ])
    cco2_d = B.outp("cconv_o2", [2, 128, NCH * 94])
    ffo_d = B.outp("ffn_o", [DEPTH, 128, NFF * 34])
    KD = os.environ.get("KDUMP", "")
    if KD:
        dbg_d = B.outp("dbg", [D, 2 * TP + TS])
        dbgsem = P.dsem("dbg")

    xs = [B.sb(f"x{c}", [128, W], F32) for c in range(NCH)]
    B.NSLOT = 4
    B.wslots = [B.sb(f"wsl{i}", [128, NFF * 128], BF16) for i in range(B.NSLOT)]
    B.wbuf = [Buf() for _ in range(B.NSLOT)]
    B.wsem = [P.dsem(f"w{i}") for i in range(B.NSLOT)]
    B.wrr = 0
    B.banks = [es.enter_context(nc.psum_tensor(f"bank{i}", [128, 512], F32)) for i in range(8)]
    B.bankb = [Buf() for _ in range(8)]
    B.ARENA = 32000
    B.arena = B.sb("arena", [128, B.ARENA], F32)
    B.aoff = 0

    identf = B.sb("identf", [128, 128], F32)
    identb = B.sb("identb", [128, 128], BF16)
    emat = B.sb("emat", [128, 128], F32)
    onesb = B.sb("onesb", [128, 128], BF16)
    gmask = B.sb("gmask", [128, 8], F32)
    cb = Buf()

    nmix = B.sb("nmix", [128, DEPTH, NCH], F32)
    nffn = B.sb("nffn", [128, DEPTH, NCH], F32)
    nfin = B.sb("nfin", [128, NCH], F32)
    lamre = B.sb("lamre", [128, 2, G], F32)
    lamim = B.sb("lamim", [128, 2, G], F32)
    logdt = B.sb("logdt", [128, 2, G], F32)
    bre = B.alloc(2 * G * 16, F32)[0:64, :].rearrange("p (i k) -> p i k", i=2)
    bim = B.alloc(2 * G * 16, F32)[0:64, :].rearrange("p (i k) -> p i k", i=2)
    pre = B.alloc(12 * G, F32).rearrange("p (a g) -> p a g", g=G)
    preb = B.alloc(2 * G * 16, F32)[0:64, :].rearrange("p (a k) -> p a k", a=2)
    wARG, wANG, wMAG, wT, wR, wSIN, wCOS, wRE, wIM = [B.alloc(11 * G, F32) for _ in range(9)]
    wB1, wB2 = B.alloc(G * 16, F32), B.alloc(G * 16, F32)
    cn = B.alloc(2 * 512, F32).rearrange("p (i k) -> p i k", i=2)
    ssmd = B.sb("ssmd", [128, 2, 4], F32)
    bglu = B.sb("bglu", [128, 2, 4], F32)
    wsc = B.sb("wsc", [128, 2, 12], F32)
    bpw1 = B.sb("bpw1", [128, 2, 16], F32)
    wdw = B.sb("wdw", [128, 2, NCH * 31], F32)
    lng = B.sb("lng", [128, 2, NCH], F32)
    lnb = B.sb("lnb", [128, 2, NCH], F32)
    bpw2 = B.sb("bpw2", [128, 2, NCH], F32)
    wfc = B.sb("wfc", [128, DEPTH, NFF * 3], F32)
    h0 = B.sb("h0", [128, 2, G * NSEQ], F32)
    hfin = B.sb("hfin", [128, 2, G * 17], F32)
    vt = B.sb("vt", [128, 2 * 11, G, 2], F32)
    bmch = B.sb("bmch", [128, 2 * 4, 128], BF16)
    cmch = B.sb("cmch", [128, 2 * 4, 128], BF16)
    car_f = B.sb("car_f", [128, DEPTH, NFF * 2], F32)
    car_s = B.sb("car_s", [128, 2, 4 * 2], F32)
    car_c = B.sb("car_c", [128, 2, NCH * 30], BF16)
    hfb = Buf()
    carb = Buf()

    csem = P.dsem("const")
    for t, d in [(nmix, nmix_d), (nffn, nffn_d), (nfin, nfin_d), (wfc, wfc_d.rearrange("l p k -> p l k"))]:
        B.dma("sp", t[:], d, csem, writes=[cb])
    for t, d in [(lamre, lamre_d), (lamim, lamim_d), (logdt, logdt_d), (bre, bre_d), (bim, bim_d), (cn, cn_d),
                 (ssmd, ssmd_d), (bglu, bglu_d), (wsc, wsc_d), (bpw1, bpw1_d), (wdw, wdw_d), (lng, lng_d),
                 (lnb, lnb_d), (bpw2, bpw2_d), (h0, h0_d.rearrange("i p g s -> i p (g s)"))]:
        B.dma("sp", t[:], d.rearrange("i p k -> p i k"), csem, writes=[cb])
    ddsem = P.dsem("d2d")
    for i in range(2):
        B.dma("sp", cco_d[i], cch_d[i, :, 4 * NSEQ:30 * NSEQ], ddsem)

    B.memset("dve", identf[:], 1.0, [cb])
    P.op("pool", lambda e: e.affine_select(out=identf[:], in_=identf[:], pattern=[[-1, 128]], compare_op=ALU.is_equal,
                                           fill=0.0, base=0, channel_multiplier=1), [cb], [cb])
    B.cp("dve", identb[:], identf[:], [cb], [cb])
    B.memset("dve", onesb[:], 1.0, [cb])
    B.memset("dve", emat[:], 0.0, [cb])
    B.cp("dve", emat[:], identf[:], [cb], [cb])
    B.cp("dve", emat[0:64, 64:128], identf[0:64, 0:64], [cb], [cb])
    B.cp("dve", emat[64:128, 0:64], identf[64:128, 64:128], [cb], [cb])
    B.memset("dve", gmask[:], 0.0, [cb])
    P.op("dve", lambda e: e.tensor_reduce(out=gmask[:], in_=identf[:].rearrange("p (g j) -> p g j", j=16),
                                          axis=mybir.AxisListType.X, op=ALU.add), [cb], [cb])
    B.memset("dve", hfin[:], 0.0, [hfb])
    B.memset("dve", car_f[:], 0.0, [carb])
    B.memset("dve", car_s[:], 0.0, [carb])
    B.memset("dve", car_c[:], 0.0, [carb])

    xsem = P.dsem("x")
    xb0 = [[Buf() for _ in range(3)] for _ in range(NCH)]
    for c in range(NCH):
        B.dma("sp", xs[c][:, 0:TP], x_d[c * 128:(c + 1) * 128, 0:TP], xsem, writes=xb0[c][:3])
    _chk("const")
    pb = Buf()
    R = [cb, pb]

    def pre_(k):
        return pre[:, k, :]

    def range_reduce(out, ang):
        B.ts("dve", pre_(10), ang, float(1 / TWO_PI), ALU.mult, R, [pb], s2=MAGIC, op1=ALU.add)
        B.ts("dve", pre_(10), pre_(10), -MAGIC, ALU.add, R, [pb])
        B.stt(out, pre_(10), -TWO_PI, ang, ALU.mult, ALU.add, R, [pb])
        B.ts("dve", out, out, -float(np.pi), ALU.max, R, [pb], s2=float(np.pi), op1=ALU.min)

    for i in range(2):
        B.act(pre_(0), logdt[:, i, :], AF.Exp, R, [pb])
        B.tt("dve", pre_(1), pre_(0), lamre[:, i, :], ALU.mult, R, [pb])
        B.tt("dve", pre_(2), pre_(0), lamim[:, i, :], ALU.mult, R, [pb])
        NW = 11 * G
        wkb = [Buf() for _ in range(22)]
        for k, d in enumerate(POWS):
            B.ts("dve", wARG[:, k * G:(k + 1) * G], pre_(1), float(d), ALU.mult, R, [wkb[k]])
            B.ts("dve", wANG[:, k * G:(k + 1) * G], pre_(2), float(d), ALU.mult, R, [wkb[11 + k]])
        RW = [cb, pb] + wkb
        B.act(wMAG[:, :], wARG[:, :], AF.Exp, RW, [pb])

        def rr_wide(out, ang):
            B.ts("dve", wT[:, :], ang, float(1 / TWO_PI), ALU.mult, RW, [pb], s2=MAGIC, op1=ALU.add)
            B.ts("dve", wT[:, :], wT[:, :], -MAGIC, ALU.add, RW, [pb])
            B.stt(out, wT[:, :], -TWO_PI, ang, ALU.mult, ALU.add, RW, [pb])
            B.ts("dve", out, out, -float(np.pi), ALU.max, RW, [pb], s2=float(np.pi), op1=ALU.min)
        rr_wide(wR[:, :], wANG[:, :])
        B.act(wSIN[:, :], wR[:, :], AF.Sin, RW, [pb])
        B.ts("dve", wANG[:, :], wANG[:, :], float(np.pi / 2), ALU.add, RW, [pb])
        rr_wide(wR[:, :], wANG[:, :])
        B.act(wCOS[:, :], wR[:, :], AF.Sin, RW, [pb])
        B.tt("dve", wRE[:, :], wMAG[:, :], wCOS[:, :], ALU.mult, RW, [pb])
        B.tt("dve", wIM[:, :], wMAG[:, :], wSIN[:, :], ALU.mult, RW, [pb])
        r3 = lambda t, lo, hi: t[lo:hi, :].rearrange("p (k g) -> p k g", g=G)
        B.cp("dve", vt[0:64, i * 11:(i + 1) * 11, :, 0], r3(wRE, 0, 64), RW, [pb])
        B.ts("dve", vt[64:128, i * 11:(i + 1) * 11, :, 0], r3(wIM, 64, 128), -1.0, ALU.mult, RW, [pb])
        B.cp("dve", vt[0:64, i * 11:(i + 1) * 11, :, 1], r3(wIM, 0, 64), RW, [pb])
        B.cp("dve", vt[64:128, i * 11:(i + 1) * 11, :, 1], r3(wRE, 64, 128), RW, [pb])
        B.cp("dve", pre_(8), wRE[:, 0:G], RW, [pb])
        B.cp("dve", pre_(9), wIM[:, 0:G], RW, [pb])
        for d in (1,):
            if d == 1:
                B.ts("dve", pre_(8), pre_(8), -1.0, ALU.add, R, [pb])
                B.tt("dve", pre_(3), lamre[:, i, :], lamre[:, i, :], ALU.mult, R, [pb])
                B.tt("dve", pre_(4), lamim[:, i, :], lamim[:, i, :], ALU.mult, R, [pb])
                B.tt("dve", pre_(3), pre_(3), pre_(4), ALU.add, R, [pb])
                P.op("dve", lambda e: e.reciprocal(out=pre[:, 3, :], in_=pre[:, 3, :]), R, [pb])
                B.tt("dve", pre_(5), pre_(8), lamre[:, i, :], ALU.mult, R, [pb])
                B.tt("dve", pre_(6), pre_(9), lamim[:, i, :], ALU.mult, R, [pb])
                B.tt("dve", pre_(5), pre_(5), pre_(6), ALU.add, R, [pb])
                B.tt("dve", pre_(11), pre_(5), pre_(3), ALU.mult, R, [pb])
                B.tt("dve", pre_(5), pre_(9), lamre[:, i, :], ALU.mult, R, [pb])
                B.tt("dve", pre_(6), pre_(8), lamim[:, i, :], ALU.mult, R, [pb])
                B.tt("dve", pre_(5), pre_(5), pre_(6), ALU.subtract, R, [pb])
                B.tt("dve", pre_(7), pre_(5), pre_(3), ALU.mult, R, [pb])
                B.ts("dve", pre_(6), pre_(7), -1.0, ALU.mult, R, [pb])
                pgb = [Buf(), Buf()]
                v3 = lambda t: t.rearrange("p (g j) -> p g j", j=16)
                bc = lambda row: pre[0:64, row, :].unsqueeze(2).to_broadcast([64, G, 16])
                B.tt("dve", v3(preb[:, 0, :]), v3(bre[:, i, :]), bc(11), ALU.mult, R, [pgb[0]])
                B.tt("dve", v3(wB1[0:64, :]), v3(bim[:, i, :]), bc(7), ALU.mult, R, [pgb[0]])
                B.tt("dve", preb[:, 0, :], preb[:, 0, :], wB1[0:64, :], ALU.subtract, R + [pgb[0]], [pgb[0]])
                B.tt("dve", v3(preb[:, 1, :]), v3(bim[:, i, :]), bc(11), ALU.mult, R, [pgb[1]])
                B.tt("dve", v3(wB2[0:64, :]), v3(bre[:, i, :]), bc(7), ALU.mult, R, [pgb[1]])
                B.tt("dve", preb[:, 1, :], preb[:, 1, :], wB2[0:64, :], ALU.add, R + [pgb[1]], [pgb[1]])
                for c in range(4):
                    bk, bb = B.bank()
                    P.op("pe", lambda e, c=c, bk=bk: e.transpose(bk[:, 0:64], preb[:, 0, c * 128:(c + 1) * 128], identf[0:64, 0:64]), R + pgb, [bb])
                    P.op("pe", lambda e, c=c, bk=bk: e.transpose(bk[:, 64:128], preb[:, 1, c * 128:(c + 1) * 128], identf[0:64, 0:64]), R + pgb + [bb], [bb])
                    B.cp("dve", bmch[:, i * 4 + c, :], bk[:, 0:128], [bb], [pb])
                    bk2, bb2 = B.bank()
                    P.op("pe", lambda e, c=c, bk2=bk2, i=i: e.transpose(bk2[:, 0:128], cn[:, i, c * 128:(c + 1) * 128], identf[:]), R, [bb2])
                    B.cp("dve", cmch[0:64, i * 4 + c, :], bk2[0:64, 0:128], [bb2], [pb])
                    B.ts("dve", cmch[64:128, i * 4 + c, :], bk2[64:128, 0:128], -1.0, ALU.mult, [bb2], [pb])

    _chk("pre")
    B.arena_reset()
    ysem = P.dsem("y")
    osem = {k: P.dsem(k) for k in ("ffo", "sco", "cco", "hfo", "hst", "hstp")}

    for tile in range(2):
        has_s = (tile == 1)
        ntiles = [(0, 344), (344, 688), (688, 1032)] + ([(TP, TP + TS)] if has_s else [])
        NT = len(ntiles)
        Wt = TP + (TS if has_s else 0)
        if tile == 0:
            xb = xb0
        else:
            xb = [[Buf() for _ in range(NT)] for _ in range(NCH)]
            B.P.barrier()
            for c in range(NCH):
                B.dma("sp", xs[c][:, 0:TP], x_d[c * 128:(c + 1) * 128, tile * TP:(tile + 1) * TP], xsem, writes=xb[c][:3])
                B.dma("sp", xs[c][:, TP:TP + TS], x_d[c * 128:(c + 1) * 128, 2 * TP:2 * TP + TS], xsem, writes=[xb[c][3]])

        def rmsnorm(gain_ap_fn, out_fn, out_bufs, out_dt_f32=False):
            sqs, sqbs, rss, rsbs = B.nrm_tmp
            tails = []
            for nt, (c0, c1) in enumerate(ntiles):
                n = c1 - c0
                sq, sqb, rs, rsb = sqs[nt % 2], sqbs[nt % 2], rss[nt % 2], rsbs[nt % 2]
                for c in range(NCH):
                    B.act(sq[c][:, 0:n], xs[c][:, c0:c1], AF.Square, [xb[c][nt]], [sqb[c]])
                bk, bb = B.bank()
                B.mm_group(bk[:, 0:n], bb, [onesb[:]] * NCH, [sq[c][:, 0:n] for c in range(NCH)], sqb + [cb])

                def _tail(nt=nt, n=n, c0=c0, c1=c1, bk=bk, bb=bb, rs=rs, rsb=rsb):
                    B.act(rs[:, 0:n], bk[:, 0:n], AF.Sqrt, [bb], [rsb], bias=float(EPS), scale=float(1.0 / D))
                    P.op("dve", lambda e, n=n, rs=rs: e.reciprocal(out=rs[:, 0:n], in_=rs[:, 0:n]), [rsb], [rsb])
                    for c in range(NCH):
                        B.stt(out_fn(c)[:, c0:c1], xs[c][:, c0:c1], gain_ap_fn(c), rs[:, 0:n], ALU.mult, ALU.mult,
                              [xb[c][nt], rsb, cb], [out_bufs[c][nt]])
                if tails:
                    tails.pop(0)()
                tails.append(_tail)
            while tails:
                tails.pop(0)()

        def proj(Wd, K, col_list, inp_aps, inp_bufs, evac, ncols=256):
            KT = K // 128
            for si, c0 in enumerate(col_list):
                wt, wb = B.slab(Wd, K, c0, ncols)
                for ms in range(ncols // 128):
                    for nt, (a, b_) in enumerate(ntiles):
                        n = b_ - a
                        bk, bb = B.bank()
                        B.mm_group(bk[:, 0:n], bb, [wt[:, kt, ms * 128:(ms + 1) * 128] for kt in range(KT)],
                                   [inp_aps[kt][:, a:b_] for kt in range(KT)], [wb] + [inp_bufs[kt][nt] for kt in range(KT)])
                        if os.environ.get("KNOEVAC", "") != "1":
                            evac(si, ms, nt, bk[:, 0:n], bb)
                        _CNT[0] += 1
                        _chk("g%d" % _CNT[0])

        for l in range(DEPTH):
            i = l // 2
            B.arena_reset()
            xn = [B.alloc(W, BF16) for _ in range(NCH)]
            xnb = [[Buf() for _ in range(NT)] for _ in range(NCH)]
            B.nrm_tmp = ([[B.alloc(344, BF16) for _ in range(NCH)] for _ in range(2)], [[Buf() for _ in range(NCH)] for _ in range(2)], [B.alloc(344, F32) for _ in range(2)], [Buf(), Buf()])
            rmsnorm(lambda c: nmix[:, l, c:c + 1], lambda c: xn[c], xnb)
            _chk(f"norm{tile}{l}")
            if l % 2 == 0:
                uf = [B.alloc(W, F32) for _ in range(4)]
                ub = [B.alloc(W, BF16) for _ in range(4)]
                ufb = [[Buf() for _ in range(NT)] for _ in range(4)]
                ubb = [[Buf() for _ in range(NT)] for _ in range(4)]
                ycat = [B.alloc(W, BF16) for _ in range(NCH)]
                ycb = [[Buf() for _ in range(NT)] for _ in range(NCH)]
                mark = B.aoff
                CW = 2 + TP + 32 + TS
                cin = [B.alloc(CW, F32) for _ in range(4)]
                cinb = [Buf() for _ in range(4)]
                tmpf = [B.alloc(344, F32) for _ in range(2)]
                tmpb = [Buf() for _ in range(2)]
                accf = [B.alloc(344, F32) for _ in range(2)]
                accb = [Buf() for _ in range(2)]
                sco = B.alloc(4 * 34, F32)
                scob = Buf()
                def ev_u(si, ms, nt, bk, bb):
                    m = si * 2 + ms
                    a, b_ = ntiles[nt]
                    if os.environ.get("KEV", "") != "dve":
                        B.cp("act", uf[m][:, a:b_], bk, [bb], [ufb[m][nt]])
                    if os.environ.get("KEV", "") != "act":
                        B.cp("dve", ub[m][:, a:b_], uf[m][:, a:b_], [ufb[m][nt]], [ubb[m][nt]])
                if os.environ.get("KSTOP", "") == "slab":
                    B.slab(w_in_d[i], D, 0)
                    raise _Stop()
                proj(w_in_d[i], D, [0, 256], xn, xnb, ev_u)
                _chk(f"uproj{tile}{l}")
                for m in range(4):
                    B.cp("dve", cin[m][:, 0:2], car_s[:, i, m * 2:m * 2 + 2], [carb], [cinb[m]])
                    if has_s:
                        B.dma("sp", cin[m][:, 2 + TP:2 + TP + 32], sch_d[i, m * 128:(m + 1) * 128, :], osem["hst"], writes=[cinb[m]])
                cnt = [0]
                for j in range(2):
                    wx, wxb = B.slab(w_in_d[i], D, 512 + 256 * j)
                    wc, wcb = B.slab(w_in_d[i], D, 1536 + 256 * j)
                    wg, wgb = B.slab(w_in_d[i], D, 1024 + 256 * j)
                    for ms in range(2):
                        m = 2 * j + ms
                        for nt, (a, b_) in enumerate(ntiles):
                            n = b_ - a
                            off = 2 + a if a < TP else 2 + TP + 32 + (a - TP)
                            sh = 1 if a < TP else NSEQ
                            rhs = [xn[kt][:, a:b_] for kt in range(NCH)]
                            rb = [xnb[kt][nt] for kt in range(NCH)]
                            bx, bxb = B.bank()
                            B.mm_group(bx[:, 0:n], bxb, [wx[:, kt, ms * 128:(ms + 1) * 128] for kt in range(NCH)], rhs, [wxb] + rb)
                            bc, bcb = B.bank()
                            B.mm_group(bc[:, 0:n], bcb, [wc[:, kt, ms * 128:(ms + 1) * 128] for kt in range(NCH)], rhs, [wcb] + rb)
                            bg, bgb = B.bank()
                            B.mm_group(bg[:, 0:n], bgb, [wg[:, kt, ms * 128:(ms + 1) * 128] for kt in range(NCH)], rhs, [wgb] + rb)
                            q = cnt[0] % 2
                            cnt[0] += 1
                            B.cp("act", tmpf[q][:, 0:n], bx[:, 0:n], [bxb], [tmpb[q]])
                            B.tt("dve", cin[m][:, off:off + n], tmpf[q][:, 0:n], bc[:, 0:n], ALU.mult, [tmpb[q], bcb], [cinb[m]])
                            w0 = wsc[:, i, m * 3 + 0:m * 3 + 1]
                            w1 = wsc[:, i, m * 3 + 1:m * 3 + 2]
                            w2 = wsc[:, i, m * 3 + 2:m * 3 + 3]
                            B.act(accf[q][:, 0:n], cin[m][:, off - 2 * sh:off - 2 * sh + n], AF.Copy, [cinb[m], cb], [accb[q]], scale=w0)
                            B.stt(accf[q][:, 0:n], cin[m][:, off - sh:off - sh + n], w1, accf[q][:, 0:n], ALU.mult, ALU.add, [cinb[m], accb[q], cb], [accb[q]])
                            B.stt(accf[q][:, 0:n], cin[m][:, off:off + n], w2, accf[q][:, 0:n], ALU.mult, ALU.add, [cinb[m], accb[q], cb], [accb[q]])
                            B.tt("dve", ycat[4 + m][:, a:b_], accf[q][:, 0:n], bg[:, 0:n], ALU.mult, [accb[q], bgb], [ycb[4 + m][nt]])
                        if tile == 0:
                            B.cp("dve", car_s[:, i, m * 2:m * 2 + 2], cin[m][:, TP:TP + 2], [cinb[m]], [carb])
                        else:
                            B.cp("dve", sco[:, m * 34:m * 34 + 2], cin[m][:, TP:TP + 2], [cinb[m]], [scob])
                            s0 = 2 + TP + 32 + 32
                            B.cp("dve", sco[:, m * 34 + 2:m * 34 + 34], cin[m][:, s0:s0 + 32], [cinb[m]], [scob])
                if tile == 1:
                    B.dma("sp", sco_d[i], sco[:, :], osem["sco"], reads=[scob])
                _chk(f"sconv{tile}{l}")
                B.arena_reset()
                B.aoff = mark
                XW = 1 + TP + 16 * 5
                S0 = 1 + TP
                X = [B.alloc(XW, BF16) for _ in range(8)]
                Xb = [Buf() for _ in range(8)]
                Am = B.alloc(8 * 11 * 128, BF16)
                Amb2 = [[[Buf(), Buf()] for _ in range(11)] for _ in range(8)]
                Amb = None
                Bm = B.alloc(8 * 128, BF16)
                Cm = B.alloc(8 * 128, BF16)
                Bmb = [Buf() for _ in range(8)]
                Cmb = [Buf() for _ in range(8)]
                yf = [B.alloc(344, F32) for _ in range(2)]
                yfb = [Buf() for _ in range(2)]
                t1 = [B.alloc(344, F32) for _ in range(2)]
                t1b = [Buf() for _ in range(2)]
                B.memset("dve", Cm[:, :], 0.0, Cmb)
                ev = [0]
                ytails = []

                def build_A(c, g8, k, Am=Am, Amb2=Amb2):
                    g = 8 * c + g8
                    row = i * 11 + k
                    o = (g8 * 11 + k) * 128
                    B.tt("dve", Am[:, o:o + 128].rearrange("p (a b) -> p a b", a=2), emat[:, :].rearrange("p (a b) -> p a b", a=2),
                         vt[:, row, g, :].unsqueeze(2).to_broadcast([128, 2, 64]), ALU.mult, [cb, pb], [Amb2[g8][k][0]])
                for c in range(4):
                    for g8 in range(8):
                        g = 8 * c + g8
                        build_A(c, g8, 0)
                        B.ts("dve", Bm[:, g8 * 128:(g8 + 1) * 128], bmch[:, i * 4 + c, :], gmask[:, g8:g8 + 1], ALU.mult, [cb, pb], [Bmb[g8]])
                        if c > 0:
                            B.memset("dve", Cm[:, g8 * 128:(g8 + 1) * 128], 0.0, [Cmb[g8]])
                        B.cp("dve", Cm[:, g8 * 128 + g8 * 16:g8 * 128 + g8 * 16 + 16], cmch[:, i * 4 + c, g8 * 16:g8 * 16 + 16], [cb, pb], [Cmb[g8]])
                        B.cp("dve", X[g8][:, 0:1], hfin[:, i, g * 17:g * 17 + 1], [hfb], [Xb[g8]])
                        if has_s:
                            B.cp("dve", X[g8][:, S0:S0 + 16], h0[:, i, g * NSEQ:(g + 1) * NSEQ], [cb], [Xb[g8]])
                        for nt, (a, b_) in enumerate(ntiles):
                            n = b_ - a
                            bk, bb = B.bank()
                            B.mm_group(bk[:, 0:n], bb, [Bm[:, g8 * 128:(g8 + 1) * 128]], [ub[c][:, a:b_]], [Bmb[g8], ubb[c][nt]])
                            dst = X[g8][:, 1 + a:1 + b_] if a < TP else X[g8][:, S0 + 16:S0 + 16 + TS]
                            ev[0] += 1
                            B.cp("act" if ev[0] % 2 else "dve", dst, bk[:, 0:n], [bb], [Xb[g8]])
                    NX = 1 + TP
                    steps = []
                    d = 1
                    while d < NX:
                        first = 2 * d - 1
                        if first < NX:
                            steps.append((d, first, (NX - 1 - first) // (2 * d) + 1))
                        d *= 2
                    d //= 2
                    while d >= 1:
                        first = 3 * d - 1
                        if first < NX:
                            steps.append((d, first, (NX - 1 - first) // (2 * d) + 1))
                        d //= 2
                    built = {0}
                    for (d, first, cntp) in steps:
                        k = POWS.index(d)
                        for kk in (k, k + 1):
                            if kk <= 10 and kk not in built:
                                built.add(kk)
                                for g8 in range(8):
                                    build_A(c, g8, kk)
                        for p0 in range(0, cntp, 512):
                            pc = min(512, cntp - p0)
                            f0 = first + 2 * d * p0
                            for g8 in range(8):
                                g = 8 * c + g8
                                o = (g8 * 11 + k) * 128
                                dstv = X[g8][:, f0:f0 + 2 * d * (pc - 1) + 1:2 * d]
                                srcv = X[g8][:, f0 - d:f0 - d + 2 * d * (pc - 1) + 1:2 * d]
                                bk, bb = B.bank()
                                B.mm_group(bk[:, 0:pc], bb, [identb[:], Am[:, o:o + 128]], [dstv, srcv], [Xb[g8], Amb2[g8][k][0], cb])
                                ev[0] += 1
                                last = f0 + 2 * d * (pc - 1)
                                en = "dve" if (last == NX - 1 or ev[0] % 2 == 0) else "act"
                                B.cp(en, dstv, bk[:, 0:pc], [bb], [Xb[g8]])
                                if last == NX - 1:
                                    B.cp("dve", hfin[:, i, g * 17:g * 17 + 1], bk[:, pc - 1:pc], [bb], [hfb])
                    if has_s:
                        for t in range(4):
                            for g8 in range(8):
                                g = 8 * c + g8
                                o = (g8 * 11 + 0) * 128
                                dstv = X[g8][:, S0 + 16 * (t + 1):S0 + 16 * (t + 2)]
                                srcv = X[g8][:, S0 + 16 * t:S0 + 16 * (t + 1)]
                                bk, bb = B.bank()
                                B.mm_group(bk[:, 0:16], bb, [identb[:], Am[:, o:o + 128]], [dstv, srcv], [Xb[g8], Amb2[g8][0][0], cb])
                                ev[0] += 1
                                B.cp("dve" if (t == 3 or ev[0] % 2 == 0) else "act", dstv, bk[:, 0:16], [bb], [Xb[g8]])
                                if t == 3:
                                    B.cp("dve", hfin[:, i, g * 17 + 1:g * 17 + 17], bk[:, 0:16], [bb], [hfb])
                    for nt, (a, b_) in enumerate(ntiles):
                        n = b_ - a
                        bk, bb = B.bank()
                        rhs = [(X[g8][:, 1 + a:1 + b_] if a < TP else X[g8][:, S0 + 16:S0 + 16 + TS]) for g8 in range(8)]
                        B.mm_group(bk[:, 0:n], bb, [Cm[:, g8 * 128:(g8 + 1) * 128] for g8 in range(8)], rhs, Xb + Cmb)
                        q = nt % 2
                        B.stt(yf[q][:, 0:n], uf[c][:, a:b_], ssmd[:, i, c:c + 1], bk[:, 0:n], ALU.mult, ALU.add, [ufb[c][nt], bb, cb], [yfb[q]])
                        B.tt("dve", t1[q][:, 0:n], yf[q][:, 0:n], yf[q][:, 0:n], ALU.mult, [yfb[q]], [t1b[q]])
                        B.ts("dve", t1[q][:, 0:n], t1[q][:, 0:n], 0.044715, ALU.mult, [t1b[q]], [t1b[q]], s2=1.0, op1=ALU.add)
                        B.tt("dve", t1[q][:, 0:n], t1[q][:, 0:n], yf[q][:, 0:n], ALU.mult, [t1b[q], yfb[q]], [t1b[q]])
                        B.act(t1[q][:, 0:n], t1[q][:, 0:n], AF.Sigmoid, [t1b[q]], [t1b[q]], scale=1.5957691216057308)

                        def _tail(q=q, n=n, a=a, b_=b_, c=c, nt=nt):
                            B.tt("dve", uf[c][:, a:b_], yf[q][:, 0:n], t1[q][:, 0:n], ALU.mult, [yfb[q], t1b[q]], [ufb[c][nt]])
                            B.cp("act", ub[c][:, a:b_], uf[c][:, a:b_], [ufb[c][nt]], [ubb[c][nt]])
                        if ytails:
                            ytails.pop(0)()
                        ytails.append(_tail)
                    while ytails:
                        ytails.pop(0)()
                _chk(f"scan{tile}{l}")
                if tile == 1:
                    B.dma("sp", hfin_d[i], hfin[:, i, :], osem["hfo"], reads=[hfb])
                sg = [B.alloc(344, F32) for _ in range(2)]
                sgb = [Buf() for _ in range(2)]

                def ev_glu(si, ms, nt, bk, bb):
                    m = si * 2 + ms
                    a, b_ = ntiles[nt]
                    n = b_ - a
                    q = nt % 2
                    B.act(sg[q][:, 0:n], bk, AF.Sigmoid, [bb, cb], [sgb[q]], bias=bglu[:, i, m:m + 1])
                    B.tt("dve", ycat[m][:, a:b_], uf[m][:, a:b_], sg[q][:, 0:n], ALU.mult, [ufb[m][nt], sgb[q]], [ycb[m][nt]])
                proj(w_glu_d[i], 512, [0, 256], ub, ubb, ev_glu)

                def ev_out(si, ms, nt, bk, bb):
                    m = si * 2 + ms
                    a, b_ = ntiles[nt]
                    B.tt("dve", xs[m][:, a:b_], bk, xs[m][:, a:b_], ALU.add, [bb, xb[m][nt]], [xb[m][nt]])
                proj(w_out_d[i], D, [0, 256, 512, 768], ycat, ycb, ev_out)
            else:
                VW = 30 + TP + 30 * NSEQ + TS
                VS = 30 + TP
                vb_ = [B.alloc(VW, BF16) for _ in range(NCH)]
                vbb = [Buf() for _ in range(NCH)]
                cco = B.alloc(NCH * 94, F32)
                ccob = Buf()
                sgm = [B.alloc(344, F32) for _ in range(3)]
                sgmb = [Buf() for _ in range(3)]
                vf = [B.alloc(344, F32) for _ in range(3)]
                vfb = [Buf() for _ in range(3)]
                gtails = []
                for c in range(NCH):
                    B.cp("dve", vb_[c][:, 0:30], car_c[:, i, c * 30:(c + 1) * 30], [carb], [vbb[c]])
                    if has_s:
                        B.dma("pool", vb_[c][:, VS:VS + 30 * NSEQ], cch_d[i, c * 128:(c + 1) * 128, :], osem["hstp"], writes=[vbb[c]])
                cnt = [0]
                for j in range(4):
                    wa, wab = B.slab(w_pw1_d[i], D, 256 * j)
                    wg, wgb = B.slab(w_pw1_d[i], D, 1024 + 256 * j)
                    for ms in range(2):
                        m = 2 * j + ms
                        for nt, (a, b_) in enumerate(ntiles):
                            n = b_ - a
                            off = 30 + a if a < TP else VS + 30 * NSEQ
                            rhs = [xn[kt][:, a:b_] for kt in range(NCH)]
                            rb = [xnb[kt][nt] for kt in range(NCH)]
                            ba, bab = B.bank()
                            B.mm_group(ba[:, 0:n], bab, [wa[:, kt, ms * 128:(ms + 1) * 128] for kt in range(NCH)], rhs, [wab] + rb)
                            bg, bgb = B.bank()
                            B.mm_group(bg[:, 0:n], bgb, [wg[:, kt, ms * 128:(ms + 1) * 128] for kt in range(NCH)], rhs, [wgb] + rb)
                            q = cnt[0] % 3
                            cnt[0] += 1
                            B.act(sgm[q][:, 0:n], bg[:, 0:n], AF.Sigmoid, [bgb, cb], [sgmb[q]], bias=bpw1[:, i, 8 + m:8 + m + 1])
                            B.stt(vf[q][:, 0:n], ba[:, 0:n], bpw1[:, i, m:m + 1], sgm[q][:, 0:n], ALU.add, ALU.mult, [bab, sgmb[q], cb], [vfb[q]])
                            def _tail(q=q, n=n, m=m, off=off, nt=nt):
                                B.cp("act", vb_[m][:, off:off + n], vf[q][:, 0:n], [vfb[q]], [vbb[m]])
                                if tile == 1 and nt == 2:
                                    B.cp("dve", cco[:, m * 94:m * 94 + 30], vf[q][:, n - 30:n], [vfb[q]], [ccob])
                                if tile == 1 and nt == 3:
                                    B.cp("dve", cco[:, m * 94 + 30:m * 94 + 94], vf[q][:, 0:n], [vfb[q]], [ccob])
                            if gtails:
                                gtails.pop(0)()
                            gtails.append(_tail)
                        while gtails:
                            gtails.pop(0)()
                        if tile == 0:
                            B.cp("dve", car_c[:, i, m * 30:(m + 1) * 30], vb_[m][:, TP:TP + 30], [vbb[m]], [carb])
                if tile == 1:
                    B.dma("sp", cco2_d[i], cco[:, :], osem["cco"], reads=[ccob])
                cvf = [B.alloc(W, F32) for _ in range(NCH)]
                cvfb = [[Buf() for _ in range(NT)] for _ in range(NCH)]
                cvb = xn
                cvbb = xnb
                dg = [B.alloc(31 * 128, BF16) for _ in range(2)]
                dgb2 = [[Buf() for _ in range(31)] for _ in range(2)]
                def build_dg(c, dg=dg, dgb2=dgb2):
                    q = c % 2
                    for k in range(31):
                        if k % 2 == 0:
                            B.act(dg[q][:, k * 128:(k + 1) * 128], identf[:], AF.Copy, [cb], [dgb2[q][k]], scale=wdw[:, i, c * 31 + k:c * 31 + k + 1])
                        else:
                            B.ts("dve", dg[q][:, k * 128:(k + 1) * 128], identf[:], wdw[:, i, c * 31 + k:c * 31 + k + 1], ALU.mult, [cb], [dgb2[q][k]])
                build_dg(0)
                for c in range(NCH):
                    q = c % 2
                    if c + 1 < NCH:
                        build_dg(c + 1)
                    for nt, (a, b_) in enumerate(ntiles):
                        n = b_ - a
                        if a < TP:
                            rhs = [vb_[c][:, a + k:a + k + n] for k in range(31)]
                        else:
                            rhs = [vb_[c][:, VS + k * NSEQ:VS + k * NSEQ + n] for k in range(31)]
                        bk, bb = B.bank()
                        B.mm_group(bk[:, 0:n], bb, [dg[q][:, k * 128:(k + 1) * 128] for k in range(31)], rhs, dgb2[q] + [vbb[c]])
                        B.cp("dve", cvf[c][:, a:b_], bk[:, 0:n], [bb], [cvfb[c][nt]])
                        B.cp("act", cvb[c][:, a:b_], cvf[c][:, a:b_], [cvfb[c][nt]], [cvbb[c][nt]])
                mus = [B.alloc(344, F32) for _ in range(2)]
                rsds = [B.alloc(344, F32) for _ in range(2)]
                msqs = [B.alloc(344, F32) for _ in range(2)]
                stbs = [Buf(), Buf()]
                sact = cvb
                sactb = cvbb
                ltails = []
                for nt, (a, b_) in enumerate(ntiles):
                    n = b_ - a
                    pp = nt % 2
                    sq8, sq8b = B.nrm_tmp[0][pp], B.nrm_tmp[1][pp]
                    b1, b1b = B.bank()
                    B.mm_group(b1[:, 0:n], b1b, [onesb[:]] * NCH, [cvb[c][:, a:b_] for c in range(NCH)], [cvbb[c][nt] for c in range(NCH)] + [cb])
                    for c in range(NCH):
                        B.act(sq8[c][:, 0:n], cvf[c][:, a:b_], AF.Square, [cvfb[c][nt]], [sq8b[c]])
                    b2, b2b = B.bank()
                    B.mm_group(b2[:, 0:n], b2b, [onesb[:]] * NCH, [sq8[c][:, 0:n] for c in range(NCH)], sq8b + [cb])

                    def _tail(nt=nt, n=n, a=a, b_=b_, b1=b1, b1b=b1b, b2=b2, b2b=b2b, mu=mus[pp], rsd=rsds[pp], msq=msqs[pp], stb=stbs[pp]):
                        B.ts("dve", mu[:, 0:n], b1[:, 0:n], float(1.0 / D), ALU.mult, [b1b], [stb])
                        B.tt("dve", msq[:, 0:n], mu[:, 0:n], mu[:, 0:n], ALU.mult, [stb], [stb])
                        B.stt(msq[:, 0:n], b2[:, 0:n], float(1.0 / D), msq[:, 0:n], ALU.mult, ALU.subtract, [b2b, stb], [stb])
                        B.act(rsd[:, 0:n], msq[:, 0:n], AF.Sqrt, [stb], [stb], bias=float(EPS), scale=1.0)
                        P.op("dve", lambda e, n=n, rsd=rsd: e.reciprocal(out=rsd[:, 0:n], in_=rsd[:, 0:n]), [stb], [stb])
                        for c in range(NCH):
                            B.tt("dve", cvf[c][:, a:b_], cvf[c][:, a:b_], mu[:, 0:n], ALU.subtract, [cvfb[c][nt], stb], [cvfb[c][nt]])
                            B.tt("dve", cvf[c][:, a:b_], cvf[c][:, a:b_], rsd[:, 0:n], ALU.mult, [cvfb[c][nt], stb], [cvfb[c][nt]])
                            B.act(sact[c][:, a:b_], cvf[c][:, a:b_], AF.Silu, [cvfb[c][nt], cb], [sactb[c][nt]],
                                  bias=lnb[:, i, c:c + 1], scale=lng[:, i, c:c + 1])
                    if ltails:
                        ltails.pop(0)()
                    ltails.append(_tail)
                while ltails:
                    ltails.pop(0)()

                def ev_pw2(si, ms, nt, bk, bb):
                    m = si * 2 + ms
                    a, b_ = ntiles[nt]
                    B.stt(xs[m][:, a:b_], bk, bpw2[:, i, m:m + 1], xs[m][:, a:b_], ALU.add, ALU.add, [bb, xb[m][nt], cb], [xb[m][nt]])
                proj(w_pw2_d[i], D, [0, 256, 512, 768], sact, sactb, ev_pw2)

            if KD == f"mixer{l}":
                for c in range(NCH):
                    B.dma("sp", dbg_d[c * 128:(c + 1) * 128, tile * TP:(tile + 1) * TP], xs[c][:, 0:TP], dbgsem, reads=xb[c][:3])
                    if has_s:
                        B.dma("sp", dbg_d[c * 128:(c + 1) * 128, 2 * TP:2 * TP + TS], xs[c][:, TP:TP + TS], dbgsem, reads=[xb[c][3]])
            _chk(f"mixer{tile}{l}")
            B.arena_reset()
            xn = [B.alloc(W, BF16) for _ in range(NCH)]
            xnb = [[Buf() for _ in range(NT)] for _ in range(NCH)]
            B.nrm_tmp = ([[B.alloc(344, BF16) for _ in range(NCH)] for _ in range(2)], [[Buf() for _ in range(NCH)] for _ in range(2)], [B.alloc(344, F32) for _ in range(2)], [Buf(), Buf()])
            rmsnorm(lambda c: nffn[:, l, c:c + 1], lambda c: xn[c], xnb)
            hh = [B.alloc(W, BF16) for _ in range(NFF)]
            hb = [[Buf() for _ in range(NT)] for _ in range(NFF)]
            AW = 2 + TP + 32 + TS
            asb = [B.alloc(AW, F32) for _ in range(2)]
            asbb = [[Buf() for _ in range(NT + 2)] for _ in range(2)]
            acc = [B.alloc(344, F32) for _ in range(3)]
            accb = [Buf() for _ in range(3)]
            tails = []
            ffo = B.alloc(NFF * 34, F32)
            ffob = Buf()
            fhist = B.alloc(NFF * 32, F32)
            fhb = Buf()
            if has_s:
                B.dma("sp", fhist[:, :].rearrange("p (m k) -> p m k", k=32), ffh_d[l].rearrange("(m p) k -> p m k", p=128), osem["hst"], writes=[fhb])
            cnt = [0]
            for j in range(11):
                wa, wab = B.slab(w_up_d[l], D, 256 * j)
                wg, wgb = B.slab(w_up_d[l], D, DFF + 256 * j)
                for ms in range(2):
                    m = 2 * j + ms
                    s = m % 2
                    B.cp("dve", asb[s][:, 0:2], car_f[:, l, m * 2:m * 2 + 2], [carb], [asbb[s][0]])
                    if has_s:
                        B.cp("dve", asb[s][:, 2 + TP:2 + TP + 32], fhist[:, m * 32:(m + 1) * 32], [fhb], [asbb[s][NT + 1]])
                    for nt, (a, b_) in enumerate(ntiles):
                        n = b_ - a
                        off = 2 + a if a < TP else 2 + TP + 32
                        sh = 1 if a < TP else NSEQ
                        rhs = [xn[kt][:, a:b_] for kt in range(NCH)]
                        rb = [xnb[kt][nt] for kt in range(NCH)]
                        ba, bab = B.bank()
                        B.mm_group(ba[:, 0:n], bab, [wa[:, kt, ms * 128:(ms + 1) * 128] for kt in range(NCH)], rhs, [wab] + rb)
                        bg, bgb = B.bank()
                        B.mm_group(bg[:, 0:n], bgb, [wg[:, kt, ms * 128:(ms + 1) * 128] for kt in range(NCH)], rhs, [wgb] + rb)
                        q = cnt[0] % 3
                        cnt[0] += 1
                        prevb = asbb[s][NT + 1] if a >= TP else asbb[s][nt]
                        curb = asbb[s][1 + nt]
                        B.cp("act", asb[s][:, off:off + n], ba[:, 0:n], [bab], [curb])
                        w0 = wfc[:, l, m * 3 + 0:m * 3 + 1]
                        w1 = wfc[:, l, m * 3 + 1:m * 3 + 2]
                        w2 = wfc[:, l, m * 3 + 2:m * 3 + 3]
                        B.act(acc[q][:, 0:n], asb[s][:, off - 2 * sh:off - 2 * sh + n], AF.Copy, [prevb, curb, cb], [accb[q]], scale=w0)
                        B.stt(acc[q][:, 0:n], asb[s][:, off - sh:off - sh + n], w1, acc[q][:, 0:n], ALU.mult, ALU.add, [prevb, curb, accb[q], cb], [accb[q]])
                        B.stt(acc[q][:, 0:n], asb[s][:, off:off + n], w2, acc[q][:, 0:n], ALU.mult, ALU.add, [curb, accb[q], cb], [accb[q]])
                        def _tail(q=q, n=n, m=m, a=a, b_=b_, bg=bg, bgb=bgb, nt=nt):
                            B.act(acc[q][:, 0:n], acc[q][:, 0:n], AF.Silu, [accb[q]], [accb[q]])
                            B.tt("dve", hh[m][:, a:b_], acc[q][:, 0:n], bg[:, 0:n], ALU.mult, [accb[q], bgb], [hb[m][nt]])
                        if tails:
                            tails.pop(0)()
                        tails.append(_tail)
                    if tile == 0:
                        B.cp("dve", car_f[:, l, m * 2:m * 2 + 2], asb[s][:, TP:TP + 2], [asbb[s][3]], [carb])
                    else:
                        B.cp("dve", ffo[:, m * 34:m * 34 + 2], asb[s][:, TP:TP + 2], [asbb[s][3]], [ffob])
                        s0 = 2 + TP + 32 + 32
                        B.cp("dve", ffo[:, m * 34 + 2:m * 34 + 34], asb[s][:, s0:s0 + 32], [asbb[s][4]], [ffob])
            while tails:
                tails.pop(0)()
            if tile == 1:
                B.dma("sp", ffo_d[l], ffo[:, :], osem["ffo"], reads=[ffob])

            def ev_dn(si, ms, nt, bk, bb):
                m = si
                a, b_ = ntiles[nt]
                B.tt("dve", xs[m][:, a:b_], bk, xs[m][:, a:b_], ALU.add, [bb, xb[m][nt]], [xb[m][nt]])
            proj(w_dn_d[l], DFF, [128 * k for k in range(8)], hh, hb, ev_dn, ncols=128)

            if KD == f"ffn{l}":
                for c in range(NCH):
                    B.dma("sp", dbg_d[c * 128:(c + 1) * 128, tile * TP:(tile + 1) * TP], xs[c][:, 0:TP], dbgsem, reads=xb[c][:3])
                    if has_s:
                        B.dma("sp", dbg_d[c * 128:(c + 1) * 128, 2 * TP:2 * TP + TS], xs[c][:, TP:TP + TS], dbgsem, reads=[xb[c][3]])
            _chk(f"ffn{tile}{l}")
        B.arena_reset()
        yo = [B.alloc(W, F32) for _ in range(NCH)]
        yob = [[Buf() for _ in range(NT)] for _ in range(NCH)]
        B.nrm_tmp = ([[B.alloc(344, BF16) for _ in range(NCH)] for _ in range(2)], [[Buf() for _ in range(NCH)] for _ in range(2)], [B.alloc(344, F32) for _ in range(2)], [Buf(), Buf()])
        rmsnorm(lambda c: nfin[:, c:c + 1], lambda c: yo[c], yob)
        for c in range(NCH):
            B.dma("sp", y_d[c * 128:(c + 1) * 128, tile * TP:(tile + 1) * TP], yo[c][:, 0:TP], ysem, reads=yob[c][:3])
            if has_s:
                B.dma("sp", y_d[c * 128:(c + 1) * 128, 2 * TP:2 * TP + TS], yo[c][:, TP:TP + TS], ysem, reads=[yob[c][3]])
        _chk(f"tile{tile}")

    return B


def _prep_inputs(inp):
    f = lambda a: np.ascontiguousarray(np.asarray(a, dtype=np.float32))
    shared = {}
    vec = lambda a, n: f(np.asarray(a).reshape(-1, n, 128).transpose(2, 0, 1))
    shared["nmix"] = vec(inp["norm_mix"], NCH)
    shared["nffn"] = vec(inp["norm_ffn"], NCH)
    shared["nfin"] = f(np.asarray(inp["norm_final"]).reshape(NCH, 128).T)
    lam_t = lambda a: f(np.concatenate([np.asarray(a).transpose(0, 2, 1)] * 2, axis=1))
    shared["lamre"] = lam_t(inp["ssm_lam_re"])
    shared["lamim"] = lam_t(inp["ssm_lam_im"])
    shared["logdt"] = f(np.broadcast_to(np.asarray(inp["ssm_log_dt"])[:, None, :], (2, 128, G)))
    shared["bre"] = f(np.asarray(inp["ssm_b_re"]).transpose(0, 2, 1, 3).reshape(2, 64, G * 16))
    shared["bim"] = f(np.asarray(inp["ssm_b_im"]).transpose(0, 2, 1, 3).reshape(2, 64, G * 16))
    cl = lambda a: np.asarray(a).reshape(2, 4, 8, 16, 64).transpose(0, 2, 3, 1, 4).reshape(2, 128, 4, 64)
    shared["cn"] = f(np.concatenate([cl(inp["ssm_c_re"]), cl(inp["ssm_c_im"])], axis=3).reshape(2, 128, 512))
    pv = lambda a, n: f(np.asarray(a).reshape(2, n, 128).transpose(0, 2, 1))
    shared["ssmd"] = pv(inp["ssm_d"], 4)
    shared["bglu"] = pv(inp["b_glu"], 4)
    shared["wsc"] = f(np.asarray(inp["w_sconv"]).reshape(2, 3, 4, 128).transpose(0, 3, 2, 1).reshape(2, 128, 12))
    shared["bpw1"] = pv(inp["b_conf_pw1"], 16)
    shared["wdw"] = f(np.asarray(inp["w_conf_dw"]).reshape(2, 31, NCH, 128).transpose(0, 3, 2, 1).reshape(2, 128, NCH * 31))
    shared["lng"] = pv(inp["conf_ln_g"], NCH)
    shared["lnb"] = pv(inp["conf_ln_b"], NCH)
    shared["bpw2"] = pv(inp["b_conf_pw2"], NCH)
    shared["wfc"] = f(np.asarray(inp["w_ffn_conv"]).reshape(DEPTH, 3, NFF, 128).transpose(0, 3, 2, 1).reshape(DEPTH, 128, NFF * 3))
    for k in ("w_in_even", "w_glu", "w_out_even", "w_conf_pw1", "w_conf_pw2", "w_ffn_up", "w_ffn_down"):
        shared[k] = f(inp[k])
    xp = np.asarray(inp["x_prompt"], dtype=np.float32)
    xsmp = np.asarray(inp["x_sample"], dtype=np.float32)
    meta = np.asarray(inp["meta_tokens"], dtype=np.float32)
    maps = []
    for c in range(NCORES):
        m = dict(shared)
        sl = slice(c * NSEQ, (c + 1) * NSEQ)
        xpc = np.concatenate([meta, xp[c]], axis=0)
        xsc = xsmp[sl].transpose(1, 0, 2).reshape(TS, D)
        m["x"] = f(np.concatenate([xpc, xsc], axis=0).T)
        hre = np.asarray(inp["state_ssm_re"])[:, sl].transpose(0, 3, 2, 1)
        him = np.asarray(inp["state_ssm_im"])[:, sl].transpose(0, 3, 2, 1)
        m["h0"] = f(np.concatenate([hre, him], axis=1))
        m["sconv_h"] = f(np.asarray(inp["state_sconv"])[:, sl].transpose(0, 3, 2, 1).reshape(2, 512, 2 * NSEQ))
        m["cconv_h"] = f(np.asarray(inp["state_cconv"])[:, sl].transpose(0, 3, 2, 1).reshape(2, D, 30 * NSEQ))
        m["ffn_h"] = f(np.asarray(inp["state_ffn"])[:, sl].transpose(0, 3, 2, 1).reshape(DEPTH, DFF, 2 * NSEQ))
        maps.append(m)
    return maps


def _unpack_states(r):
    sc = np.asarray(r["sconv_o"]).reshape(2, 128, 4, 34).transpose(0, 2, 1, 3).reshape(2, 512, 34)
    c2 = np.asarray(r["cconv_o2"]).reshape(2, 128, NCH, 94).transpose(0, 2, 1, 3).reshape(2, D, 94)
    hist = np.asarray(r["cconv_o"])
    cc = np.concatenate([c2[:, :, 0:30], hist, c2[:, :, 30:94]], axis=2)
    ff = np.asarray(r["ffn_o"]).reshape(DEPTH, 128, NFF, 34).transpose(0, 2, 1, 3).reshape(DEPTH, DFF, 34)
    return sc, cc, ff


_CACHE = {}


def kernel(**inputs):
    if "B" not in _CACHE:
        _CACHE["B"] = build_program()
    B = _CACHE["B"]
    maps = _prep_inputs(inputs)
    res = run_bass_kernel_spmd(B.nc, maps, core_ids=list(range(NCORES)))
    R = res.results
    BATCH = NCORES
    y_prompt = np.zeros((BATCH, 2048, D), np.float32)
    y_sample = np.zeros((NCORES * NSEQ, 4, D), np.float32)
    p_re = np.zeros((2, BATCH, G, 64), np.float32)
    p_im = np.zeros((2, BATCH, G, 64), np.float32)
    p_sc = np.zeros((2, BATCH, 2, 512), np.float32)
    p_cc = np.zeros((2, BATCH, 30, D), np.float32)
    p_ff = np.zeros((DEPTH, BATCH, 2, DFF), np.float32)
    s_re = np.zeros((2, NCORES * NSEQ, G, 64), np.float32)
    s_im = np.zeros((2, NCORES * NSEQ, G, 64), np.float32)
    s_sc = np.zeros((2, NCORES * NSEQ, 2, 512), np.float32)
    s_cc = np.zeros((2, NCORES * NSEQ, 30, D), np.float32)
    s_ff = np.zeros((DEPTH, NCORES * NSEQ, 2, DFF), np.float32)
    for c in range(NCORES):
        r = R[c]
        sl = slice(c * NSEQ, (c + 1) * NSEQ)
        y = np.asarray(r["y"])
        y_prompt[c] = y[:, 16:2 * TP].T
        y_sample[sl] = y[:, 2 * TP:].T.reshape(4, NSEQ, D).transpose(1, 0, 2)
        hf = np.asarray(r["hfin"]).reshape(2, 128, G, 17)
        p_re[:, c] = hf[:, 0:64, :, 0].transpose(0, 2, 1)
        p_im[:, c] = hf[:, 64:128, :, 0].transpose(0, 2, 1)
        s_re[:, sl] = hf[:, 0:64, :, 1:].transpose(0, 3, 2, 1)
        s_im[:, sl] = hf[:, 64:128, :, 1:].transpose(0, 3, 2, 1)
        sc, cc, ff = _unpack_states(r)
        p_sc[:, c] = sc[:, :, 0:2].transpose(0, 2, 1)
        s_sc[:, sl] = sc[:, :, 2:].reshape(2, 512, 2, NSEQ).transpose(0, 3, 2, 1)
        p_cc[:, c] = cc[:, :, 0:30].transpose(0, 2, 1)
        s_cc[:, sl] = cc[:, :, 30:].reshape(2, D, 30, NSEQ).transpose(0, 3, 2, 1)
        p_ff[:, c] = ff[:, :, 0:2].transpose(0, 2, 1)
        s_ff[:, sl] = ff[:, :, 2:].reshape(DEPTH, DFF, 2, NSEQ).transpose(0, 3, 2, 1)
    return (y_prompt, y_sample, p_re, p_im, p_sc, p_cc, p_ff, s_re, s_im, s_sc, s_cc, s_ff)
```

```python
import os
import numpy as np
from contextlib import ExitStack
import concourse.bass as bass
import concourse.mybir as mybir
from concourse.bass_utils import run_bass_kernel_spmd

F32 = mybir.dt.float32
BF16 = mybir.dt.bfloat16
AF = mybir.ActivationFunctionType
ALU = mybir.AluOpType

NCORES = 8
D = 1024
DEPTH = 4
NCH = 8
DFF = 2816
NFF = 22
TP = 1032
NSEQ = 16
TS = 64
G = 32
EPS = 1e-6
POWS = [1 << k for k in range(11)]
TWO_PI = float(2 * np.pi)
MAGIC = 12582912.0


class Buf:
    __slots__ = ("w", "r")

    def __init__(self):
        self.w = None
        self.r = []


class DSem:
    def __init__(self, name):
        self.name = name
        self.total = 0
        self.sem = None


class Op:
    __slots__ = ("eng", "fn", "deps", "dma", "sem", "cnt", "sig", "dsem")

    def __init__(self, eng, fn, dma):
        self.eng = eng
        self.fn = fn
        self.dma = dma
        self.deps = []
        self.sem = None
        self.cnt = 0
        self.sig = False
        self.dsem = None


class Prog:
    ENGS = ("pe", "act", "dve", "pool", "sp")

    def __init__(self, nc):
        self.nc = nc
        self.ops = {e: [] for e in self.ENGS}
        self.dsems = []
        self.pending = {e: [] for e in self.ENGS}
        self.last = {e: None for e in self.ENGS}
        self.dma_since = []

    def dsem(self, name):
        d = DSem(name)
        self.dsems.append(d)
        return d

    def barrier(self):
        L = [self.last[e] for e in ("pe", "act", "dve", "pool") if self.last[e] is not None]
        L += self.dma_since
        self.dma_since = []
        for e in ("pe", "act", "dve", "sp", "pool"):
            self.pending[e] = self.pending[e] + L

    def op(self, eng, fn, reads=(), writes=(), dma=None, nobar=False):
        isdma = dma is not None
        o = Op(eng, fn, isdma)
        deps = []
        for b in reads:
            if b.w is not None:
                deps.append((b.w, "raw"))
        for b in writes:
            if b.w is not None:
                deps.append((b.w, "waw"))
            for r in b.r:
                deps.append((r, "war"))
        if not nobar:
            for d in self.pending[eng]:
                deps.append((d, "bar"))
            self.pending[eng] = []
        seen = set()
        for d, kind in deps:
            if d is o or id(d) in seen:
                continue
            if (not isdma) and (not d.dma) and d.eng == eng:
                if eng == "pe":
                    continue
            seen.add(id(d))
            if d.dma:
                o.deps.append((d, d.dsem.total))
            else:
                o.deps.append((d, None))
                d.sig = True
        for b in reads:
            b.r.append(o)
        for b in writes:
            b.w = o
            b.r = []
        if isdma:
            dma.total += 16
            o.dsem = dma
            o.cnt = dma.total
            o.sig = True
            if not nobar:
                self.dma_since.append(o)
        else:
            self.last[eng] = o
        self.ops[eng].append(o)
        return o

    def build(self, es):
        nc = self.nc
        sems = {e: es.enter_context(nc.semaphore("s_" + e)) for e in self.ENGS}
        for d in self.dsems:
            if d.total > 0:
                d.sem = es.enter_context(nc.semaphore("d_" + d.name))
        for e in self.ENGS:
            c = 0
            for o in self.ops[e]:
                if o.dma:
                    o.sem = o.dsem.sem
                elif o.sig:
                    c += 1
                    o.sem = sems[e]
                    o.cnt = c
        if os.environ.get("KBAR", "1") == "1":
            allsems = list(sems.values()) + [d.sem for d in self.dsems if d.sem is not None]
            for sm in allsems:
                nc.gpsimd.sem_clear(sm)
            nc.all_engine_barrier()
        block = es.enter_context(nc.Block())
        prog = self

        def run(engine, e):
            seen = {}
            used = {}
            for o in prog.ops[e]:
                need = {}
                for d, thr in o.deps:
                    c = thr if thr is not None else d.cnt
                    s = d.sem
                    if seen.get(s.num, 0) >= c:
                        continue
                    if need.get(s.num, (None, 0))[1] < c:
                        need[s.num] = (s, c)
                for s, c in need.values():
                    seen[s.num] = c
                    engine.wait_ge(s, c)
                ins = o.fn(engine)
                if o.sig:
                    ins.then_inc(o.sem, 16 if o.dma else 1)
                    if o.dma:
                        used[o.dsem.name] = o.dsem
            for d in used.values():
                if seen.get(d.sem.num, 0) < d.total:
                    engine.wait_ge(d.sem, d.total)

        @block.tensor
        def _(eng):
            run(eng, "pe")

        @block.scalar
        def _(eng):
            run(eng, "act")

        @block.vector
        def _(eng):
            run(eng, "dve")

        @block.gpsimd
        def _(eng):
            run(eng, "pool")

        @block.sync
        def _(eng):
            run(eng, "sp")


class Builder:
    def __init__(self):
        self.nc = bass.Bass("TRN2", target_bir_lowering=False)
        self.es = ExitStack()
        self.P = Prog(self.nc)
        self.din = {}
        self.dout = {}
        self.rr = 0

    def inp(self, name, shape):
        t = self.nc.dram_tensor(name, list(shape), F32, kind="ExternalInput").ap()
        self.din[name] = t
        return t

    def outp(self, name, shape):
        t = self.nc.dram_tensor(name, list(shape), F32, kind="ExternalOutput").ap()
        self.dout[name] = t
        return t

    def sb(self, name, shape, dt):
        return self.es.enter_context(self.nc.sbuf_tensor("sb_" + name, list(shape), dt))

    def arena_reset(self):
        self.P.barrier()
        self.aoff = 0

    def alloc(self, cols, dt):
        n32 = cols if dt == F32 else (cols + 1) // 2
        n32 = (n32 + 1) // 2 * 2
        o = self.aoff
        self.aoff += n32
        assert self.aoff <= self.ARENA, ("arena overflow", self.aoff)
        v = self.arena[:, o:o + n32]
        if dt != F32:
            v = v.bitcast(dt)
        return v

    def bank(self):
        i = self.rr % 8
        self.rr += 1
        return self.banks[i], self.bankb[i]

    def act(self, out, in_, func, reads, writes, bias=None, scale=None):
        kw = {}
        if bias is not None:
            kw["bias"] = bias
        if scale is not None:
            kw["scale"] = scale
        return self.P.op("act", lambda e: e.activation(out=out, in_=in_, func=func, **kw), reads, writes)

    def tt(self, eng, out, in0, in1, op, reads, writes):
        return self.P.op(eng, lambda e: e.tensor_tensor(out=out, in0=in0, in1=in1, op=op), reads, writes)

    def ts(self, eng, out, in0, s1, op0, reads, writes, s2=None, op1=None):
        if s2 is None:
            return self.P.op(eng, lambda e: e.tensor_scalar(out=out, in0=in0, scalar1=s1, scalar2=None, op0=op0), reads, writes)
        return self.P.op(eng, lambda e: e.tensor_scalar(out=out, in0=in0, scalar1=s1, scalar2=s2, op0=op0, op1=op1), reads, writes)

    def stt(self, out, in0, scalar, in1, op0, op1, reads, writes):
        return self.P.op("dve", lambda e: e.scalar_tensor_tensor(out=out, in0=in0, scalar=scalar, in1=in1, op0=op0, op1=op1), reads, writes)

    def cp(self, eng, out, in_, reads, writes):
        if eng == "act":
            return self.act(out, in_, AF.Copy, reads, writes)
        return self.P.op(eng, lambda e: e.tensor_copy(out=out, in_=in_), reads, writes)

    def memset(self, eng, ap, val, writes):
        return self.P.op(eng, lambda e: e.memset(ap, val), (), writes)

    def dma(self, eng, out, in_, dsem, reads=(), writes=(), nobar=False):
        return self.P.op(eng, lambda e: e.dma_start(out=out, in_=in_), reads, writes, dma=dsem, nobar=nobar)

    def slab(self, Wd, K, c0, ncols=256):
        KT = K // 128
        i = self.wrr % self.NSLOT
        self.wrr += 1
        t = self.wslots[i][:, 0:KT * ncols].rearrange("p (kt m) -> p kt m", m=ncols)
        src = Wd[:, c0:c0 + ncols].rearrange("(kt p) m -> p kt m", p=128)
        self.dma("pool", t, src, self.wsem[i], writes=[self.wbuf[i]], nobar=True)
        return t, self.wbuf[i]

    def mm_group(self, bank_ap, bankb, lhs_list, rhs_list, reads):
        n = len(lhs_list)

        def fn(e):
            ins = None
            for k in range(n):
                ins = e.matmul(bank_ap, lhs_list[k], rhs_list[k], start=(k == 0), stop=(k == n - 1))
            return ins
        return self.P.op("pe", fn, reads, [bankb])


class _Stop(Exception):
    pass


_CNT = [0]


def _chk(name):
    if os.environ.get("KSTOP", "") == name:
        raise _Stop()


def build_program():
    B = Builder()
    try:
        _build_body(B)
    except _Stop:
        pass
    B.P.build(B.es)
    if os.environ.get("KENDBAR", "1") == "1":
        B.nc.all_engine_barrier()
    return B


def _build_body(B):
    nc, P, es = B.nc, B.P, B.es
    W = TP + TS

    x_d = B.inp("x", [D, 2 * TP + TS])
    h0_d = B.inp("h0", [2, 128, G, NSEQ])
    sch_d = B.inp("sconv_h", [2, 512, 2 * NSEQ])
    cch_d = B.inp("cconv_h", [2, D, 30 * NSEQ])
    ffh_d = B.inp("ffn_h", [DEPTH, DFF, 2 * NSEQ])
    nmix_d = B.inp("nmix", [128, DEPTH, NCH])
    nffn_d = B.inp("nffn", [128, DEPTH, NCH])
    nfin_d = B.inp("nfin", [128, NCH])
    lamre_d = B.inp("lamre", [2, 128, G])
    lamim_d = B.inp("lamim", [2, 128, G])
    logdt_d = B.inp("logdt", [2, 128, G])
    bre_d = B.inp("bre", [2, 64, G * 16])
    bim_d = B.inp("bim", [2, 64, G * 16])
    cn_d = B.inp("cn", [2, 128, 4 * 128])
    ssmd_d = B.inp("ssmd", [2, 128, 4])
    bglu_d = B.inp("bglu", [2, 128, 4])
    wsc_d = B.inp("wsc", [2, 128, 4 * 3])
    bpw1_d = B.inp("bpw1", [2, 128, 16])
    wdw_d = B.inp("wdw", [2, 128, NCH * 31])
    lng_d = B.inp("lng", [2, 128, NCH])
    lnb_d = B.inp("lnb", [2, 128, NCH])
    bpw2_d = B.inp("bpw2", [2, 128, NCH])
    wfc_d = B.inp("wfc", [DEPTH, 128, NFF * 3])
    w_in_d = B.inp("w_in_even", [2, D, 2048])
    w_glu_d = B.inp("w_glu", [2, 512, 512])
    w_out_d = B.inp("w_out_even", [2, D, D])
    w_pw1_d = B.inp("w_conf_pw1", [2, D, 2048])
    w_pw2_d = B.inp("w_conf_pw2", [2, D, D])
    w_up_d = B.inp("w_ffn_up", [DEPTH, D, 2 * DFF])
    w_dn_d = B.inp("w_ffn_down", [DEPTH, DFF, D])

    y_d = B.outp("y", [D, 2 * TP + TS])
    hfin_d = B.outp("hfin", [2, 128, G * 17])
    sco_d = B.outp("sconv_o", [2, 128, 4 * 34])
    cco_d = B.outp("cconv_o", [2, D, 26 * NSEQ])
    cco2_d = B.outp("cconv_o2", [2, 128, NCH * 94])
    ffo_d = B.outp("ffn_o", [DEPTH, 128, NFF * 34])
    KD = os.environ.get("KDUMP", "")
    if KD:
        dbg_d = B.outp("dbg", [D, 2 * TP + TS])
        dbgsem = P.dsem("dbg")

    xs = [B.sb(f"x{c}", [128, W], F32) for c in range(NCH)]
    B.NSLOT = 4
    B.wslots = [B.sb(f"wsl{i}", [128, NFF * 128], BF16) for i in range(B.NSLOT)]
    B.wbuf = [Buf() for _ in range(B.NSLOT)]
    B.wsem = [P.dsem(f"w{i}") for i in range(B.NSLOT)]
    B.wrr = 0
    B.banks = [es.enter_context(nc.psum_tensor(f"bank{i}", [128, 512], F32)) for i in range(8)]
    B.bankb = [Buf() for _ in range(8)]
    B.ARENA = 32000
    B.arena = B.sb("arena", [128, B.ARENA], F32)
    B.aoff = 0

    identf = B.sb("identf", [128, 128], F32)
    identb = B.sb("identb", [128, 128], BF16)
    emat = B.sb("emat", [128, 128], F32)
    onesb = B.sb("onesb", [128, 128], BF16)
    gmask = B.sb("gmask", [128, 8], F32)
    cb = Buf()

    nmix = B.sb("nmix", [128, DEPTH, NCH], F32)
    nffn = B.sb("nffn", [128, DEPTH, NCH], F32)
    nfin = B.sb("nfin", [128, NCH], F32)
    lamre = B.sb("lamre", [128, 2, G], F32)
    lamim = B.sb("lamim", [128, 2, G], F32)
    logdt = B.sb("logdt", [128, 2, G], F32)
    bre = B.alloc(2 * G * 16, F32)[0:64, :].rearrange("p (i k) -> p i k", i=2)
    bim = B.alloc(2 * G * 16, F32)[0:64, :].rearrange("p (i k) -> p i k", i=2)
    pre = B.alloc(12 * G, F32).rearrange("p (a g) -> p a g", g=G)
    preb = B.alloc(2 * G * 16, F32)[0:64, :].rearrange("p (a k) -> p a k", a=2)
    wARG, wANG, wMAG, wT, wR, wSIN, wCOS, wRE, wIM = [B.alloc(11 * G, F32) for _ in range(9)]
    wB1, wB2 = B.alloc(G * 16, F32), B.alloc(G * 16, F32)
    cn = B.alloc(2 * 512, F32).rearrange("p (i k) -> p i k", i=2)
    ssmd = B.sb("ssmd", [128, 2, 4], F32)
    bglu = B.sb("bglu", [128, 2, 4], F32)
    wsc = B.sb("wsc", [128, 2, 12], F32)
    bpw1 = B.sb("bpw1", [128, 2, 16], F32)
    wdw = B.sb("wdw", [128, 2, NCH * 31], F32)
    lng = B.sb("lng", [128, 2, NCH], F32)
    lnb = B.sb("lnb", [128, 2, NCH], F32)
    bpw2 = B.sb("bpw2", [128, 2, NCH], F32)
    wfc = B.sb("wfc", [128, DEPTH, NFF * 3], F32)
    h0 = B.sb("h0", [128, 2, G * NSEQ], F32)
    hfin = B.sb("hfin", [128, 2, G * 17], F32)
    vt = B.sb("vt", [128, 2 * 11, G, 2], F32)
    bmch = B.sb("bmch", [128, 2 * 4, 128], BF16)
    cmch = B.sb("cmch", [128, 2 * 4, 128], BF16)
    car_f = B.sb("car_f", [128, DEPTH, NFF * 2], F32)
    car_s = B.sb("car_s", [128, 2, 4 * 2], F32)
    car_c = B.sb("car_c", [128, 2, NCH * 30], BF16)
    hfb = Buf()
    carb = Buf()

    csem = P.dsem("const")
    for t, d in [(nmix, nmix_d), (nffn, nffn_d), (nfin, nfin_d), (wfc, wfc_d.rearrange("l p k -> p l k"))]:
        B.dma("sp", t[:], d, csem, writes=[cb])
    for t, d in [(lamre, lamre_d), (lamim, lamim_d), (logdt, logdt_d), (bre, bre_d), (bim, bim_d), (cn, cn_d),
                 (ssmd, ssmd_d), (bglu, bglu_d), (wsc, wsc_d), (bpw1, bpw1_d), (wdw, wdw_d), (lng, lng_d),
                 (lnb, lnb_d), (bpw2, bpw2_d), (h0, h0_d.rearrange("i p g s -> i p (g s)"))]:
        B.dma("sp", t[:], d.rearrange("i p k -> p i k"), csem, writes=[cb])
    ddsem = P.dsem("d2d")
    for i in range(2):
        B.dma("sp", cco_d[i], cch_d[i, :, 4 * NSEQ:30 * NSEQ], ddsem)

    B.memset("dve", identf[:], 1.0, [cb])
    P.op("pool", lambda e: e.affine_select(out=identf[:], in_=identf[:], pattern=[[-1, 128]], compare_op=ALU.is_equal,
                                           fill=0.0, base=0, channel_multiplier=1), [cb], [cb])
    B.cp("dve", identb[:], identf[:], [cb], [cb])
    B.memset("dve", onesb[:], 1.0, [cb])
    B.memset("dve", emat[:], 0.0, [cb])
    B.cp("dve", emat[:], identf[:], [cb], [cb])
    B.cp("dve", emat[0:64, 64:128], identf[0:64, 0:64], [cb], [cb])
    B.cp("dve", emat[64:128, 0:64], identf[64:128, 64:128], [cb], [cb])
    B.memset("dve", gmask[:], 0.0, [cb])
    P.op("dve", lambda e: e.tensor_reduce(out=gmask[:], in_=identf[:].rearrange("p (g j) -> p g j", j=16),
                                          axis=mybir.AxisListType.X, op=ALU.add), [cb], [cb])
    B.memset("dve", hfin[:], 0.0, [hfb])
    B.memset("dve", car_f[:], 0.0, [carb])
    B.memset("dve", car_s[:], 0.0, [carb])
    B.memset("dve", car_c[:], 0.0, [carb])

    xsem = P.dsem("x")
    xb0 = [[Buf() for _ in range(3)] for _ in range(NCH)]
    for c in range(NCH):
        B.dma("sp", xs[c][:, 0:TP], x_d[c * 128:(c + 1) * 128, 0:TP], xsem, writes=xb0[c][:3])
    _chk("const")
    pb = Buf()
    R = [cb, pb]

    def pre_(k):
        return pre[:, k, :]

    def range_reduce(out, ang):
        B.ts("dve", pre_(10), ang, float(1 / TWO_PI), ALU.mult, R, [pb], s2=MAGIC, op1=ALU.add)
        B.ts("dve", pre_(10), pre_(10), -MAGIC, ALU.add, R, [pb])
        B.stt(out, pre_(10), -TWO_PI, ang, ALU.mult, ALU.add, R, [pb])
        B.ts("dve", out, out, -float(np.pi), ALU.max, R, [pb], s2=float(np.pi), op1=ALU.min)

    for i in range(2):
        B.act(pre_(0), logdt[:, i, :], AF.Exp, R, [pb])
        B.tt("dve", pre_(1), pre_(0), lamre[:, i, :], ALU.mult, R, [pb])
        B.tt("dve", pre_(2), pre_(0), lamim[:, i, :], ALU.mult, R, [pb])
        NW = 11 * G
        wkb = [Buf() for _ in range(22)]
        for k, d in enumerate(POWS):
            B.ts("dve", wARG[:, k * G:(k + 1) * G], pre_(1), float(d), ALU.mult, R, [wkb[k]])
            B.ts("dve", wANG[:, k * G:(k + 1) * G], pre_(2), float(d), ALU.mult, R, [wkb[11 + k]])
        RW = [cb, pb] + wkb
        B.act(wMAG[:, :], wARG[:, :], AF.Exp, RW, [pb])

        def rr_wide(out, ang):
            B.ts("dve", wT[:, :], ang, float(1 / TWO_PI), ALU.mult, RW, [pb], s2=MAGIC, op1=ALU.add)
            B.ts("dve", wT[:, :], wT[:, :], -MAGIC, ALU.add, RW, [pb])
            B.stt(out, wT[:, :], -TWO_PI, ang, ALU.mult, ALU.add, RW, [pb])
            B.ts("dve", out, out, -float(np.pi), ALU.max, RW, [pb], s2=float(np.pi), op1=ALU.min)
        rr_wide(wR[:, :], wANG[:, :])
        B.act(wSIN[:, :], wR[:, :], AF.Sin, RW, [pb])
        B.ts("dve", wANG[:, :], wANG[:, :], float(np.pi / 2), ALU.add, RW, [pb])
        rr_wide(wR[:, :], wANG[:, :])
        B.act(wCOS[:, :], wR[:, :], AF.Sin, RW, [pb])
        B.tt("dve", wRE[:, :], wMAG[:, :], wCOS[:, :], ALU.mult, RW, [pb])
        B.tt("dve", wIM[:, :], wMAG[:, :], wSIN[:, :], ALU.mult, RW, [pb])
        r3 = lambda t, lo, hi: t[lo:hi, :].rearrange("p (k g) -> p k g", g=G)
        B.cp("dve", vt[0:64, i * 11:(i + 1) * 11, :, 0], r3(wRE, 0, 64), RW, [pb])
        B.ts("dve", vt[64:128, i * 11:(i + 1) * 11, :, 0], r3(wIM, 64, 128), -1.0, ALU.mult, RW, [pb])
        B.cp("dve", vt[0:64, i * 11:(i + 1) * 11, :, 1], r3(wIM, 0, 64), RW, [pb])
        B.cp("dve", vt[64:128, i * 11:(i + 1) * 11, :, 1], r3(wRE, 64, 128), RW, [pb])
        B.cp("dve", pre_(8), wRE[:, 0:G], RW, [pb])
        B.cp("dve", pre_(9), wIM[:, 0:G], RW, [pb])
        for d in (1,):
            if d == 1:
                B.ts("dve", pre_(8), pre_(8), -1.0, ALU.add, R, [pb])
                B.tt("dve", pre_(3), lamre[:, i, :], lamre[:, i, :], ALU.mult, R, [pb])
                B.tt("dve", pre_(4), lamim[:, i, :], lamim[:, i, :], ALU.mult, R, [pb])
                B.tt("dve", pre_(3), pre_(3), pre_(4), ALU.add, R, [pb])
                P.op("dve", lambda e: e.reciprocal(out=pre[:, 3, :], in_=pre[:, 3, :]), R, [pb])
                B.tt("dve", pre_(5), pre_(8), lamre[:, i, :], ALU.mult, R, [pb])
                B.tt("dve", pre_(6), pre_(9), lamim[:, i, :], ALU.mult, R, [pb])
                B.tt("dve", pre_(5), pre_(5), pre_(6), ALU.add, R, [pb])
                B.tt("dve", pre_(11), pre_(5), pre_(3), ALU.mult, R, [pb])
                B.tt("dve", pre_(5), pre_(9), lamre[:, i, :], ALU.mult, R, [pb])
                B.tt("dve", pre_(6), pre_(8), lamim[:, i, :], ALU.mult, R, [pb])
                B.tt("dve", pre_(5), pre_(5), pre_(6), ALU.subtract, R, [pb])
                B.tt("dve", pre_(7), pre_(5), pre_(3), ALU.mult, R, [pb])
                B.ts("dve", pre_(6), pre_(7), -1.0, ALU.mult, R, [pb])
                pgb = [Buf(), Buf()]
                v3 = lambda t: t.rearrange("p (g j) -> p g j", j=16)
                bc = lambda row: pre[0:64, row, :].unsqueeze(2).to_broadcast([64, G, 16])
                B.tt("dve", v3(preb[:, 0, :]), v3(bre[:, i, :]), bc(11), ALU.mult, R, [pgb[0]])
                B.tt("dve", v3(wB1[0:64, :]), v3(bim[:, i, :]), bc(7), ALU.mult, R, [pgb[0]])
                B.tt("dve", preb[:, 0, :], preb[:, 0, :], wB1[0:64, :], ALU.subtract, R + [pgb[0]], [pgb[0]])
                B.tt("dve", v3(preb[:, 1, :]), v3(bim[:, i, :]), bc(11), ALU.mult, R, [pgb[1]])
                B.tt("dve", v3(wB2[0:64, :]), v3(bre[:, i, :]), bc(7), ALU.mult, R, [pgb[1]])
                B.tt("dve", preb[:, 1, :], preb[:, 1, :], wB2[0:64, :], ALU.add, R + [pgb[1]], [pgb[1]])
                for c in range(4):
                    bk, bb = B.bank()
                    P.op("pe", lambda e, c=c, bk=bk: e.transpose(bk[:, 0:64], preb[:, 0, c * 128:(c + 1) * 128], identf[0:64, 0:64]), R + pgb, [bb])
                    P.op("pe", lambda e, c=c, bk=bk: e.transpose(bk[:, 64:128], preb[:, 1, c * 128:(c + 1) * 128], identf[0:64, 0:64]), R + pgb + [bb], [bb])
                    B.cp("dve", bmch[:, i * 4 + c, :], bk[:, 0:128], [bb], [pb])
                    bk2, bb2 = B.bank()
                    P.op("pe", lambda e, c=c, bk2=bk2, i=i: e.transpose(bk2[:, 0:128], cn[:, i, c * 128:(c + 1) * 128], identf[:]), R, [bb2])
                    B.cp("dve", cmch[0:64, i * 4 + c, :], bk2[0:64, 0:128], [bb2], [pb])
                    B.ts("dve", cmch[64:128, i * 4 + c, :], bk2[64:128, 0:128], -1.0, ALU.mult, [bb2], [pb])

    _chk("pre")
    B.arena_reset()
    ysem = P.dsem("y")
    osem = {k: P.dsem(k) for k in ("ffo", "sco", "cco", "hfo", "hst", "hstp")}

    for tile in range(2):
        has_s = (tile == 1)
        ntiles = [(0, 344), (344, 688), (688, 1032)] + ([(TP, TP + TS)] if has_s else [])
        NT = len(ntiles)
        Wt = TP + (TS if has_s else 0)
        if tile == 0:
            xb = xb0
        else:
            xb = [[Buf() for _ in range(NT)] for _ in range(NCH)]
            B.P.barrier()
            for c in range(NCH):
                B.dma("sp", xs[c][:, 0:TP], x_d[c * 128:(c + 1) * 128, tile * TP:(tile + 1) * TP], xsem, writes=xb[c][:3])
                B.dma("sp", xs[c][:, TP:TP + TS], x_d[c * 128:(c + 1) * 128, 2 * TP:2 * TP + TS], xsem, writes=[xb[c][3]])

        def rmsnorm(gain_ap_fn, out_fn, out_bufs, out_dt_f32=False):
            sqs, sqbs, rss, rsbs = B.nrm_tmp
            tails = []
            for nt, (c0, c1) in enumerate(ntiles):
                n = c1 - c0
                sq, sqb, rs, rsb = sqs[nt % 2], sqbs[nt % 2], rss[nt % 2], rsbs[nt % 2]
                for c in range(NCH):
                    B.act(sq[c][:, 0:n], xs[c][:, c0:c1], AF.Square, [xb[c][nt]], [sqb[c]])
                bk, bb = B.bank()
                B.mm_group(bk[:, 0:n], bb, [onesb[:]] * NCH, [sq[c][:, 0:n] for c in range(NCH)], sqb + [cb])

                def _tail(nt=nt, n=n, c0=c0, c1=c1, bk=bk, bb=bb, rs=rs, rsb=rsb):
                    B.act(rs[:, 0:n], bk[:, 0:n], AF.Sqrt, [bb], [rsb], bias=float(EPS), scale=float(1.0 / D))
                    P.op("dve", lambda e, n=n, rs=rs: e.reciprocal(out=rs[:, 0:n], in_=rs[:, 0:n]), [rsb], [rsb])
                    for c in range(NCH):
                        B.stt(out_fn(c)[:, c0:c1], xs[c][:, c0:c1], gain_ap_fn(c), rs[:, 0:n], ALU.mult, ALU.mult,
                              [xb[c][nt], rsb, cb], [out_bufs[c][nt]])
                if tails:
                    tails.pop(0)()
                tails.append(_tail)
            while tails:
                tails.pop(0)()

        def proj(Wd, K, col_list, inp_aps, inp_bufs, evac, ncols=256):
            KT = K // 128
            for si, c0 in enumerate(col_list):
                wt, wb = B.slab(Wd, K, c0, ncols)
                for ms in range(ncols // 128):
                    for nt, (a, b_) in enumerate(ntiles):
                        n = b_ - a
                        bk, bb = B.bank()
                        B.mm_group(bk[:, 0:n], bb, [wt[:, kt, ms * 128:(ms + 1) * 128] for kt in range(KT)],
                                   [inp_aps[kt][:, a:b_] for kt in range(KT)], [wb] + [inp_bufs[kt][nt] for kt in range(KT)])
                        if os.environ.get("KNOEVAC", "") != "1":
                            evac(si, ms, nt, bk[:, 0:n], bb)
                        _CNT[0] += 1
                        _chk("g%d" % _CNT[0])

        for l in range(DEPTH):
            i = l // 2
            B.arena_reset()
            xn = [B.alloc(W, BF16) for _ in range(NCH)]
            xnb = [[Buf() for _ in range(NT)] for _ in range(NCH)]
            B.nrm_tmp = ([[B.alloc(344, BF16) for _ in range(NCH)] for _ in range(2)], [[Buf() for _ in range(NCH)] for _ in range(2)], [B.alloc(344, F32) for _ in range(2)], [Buf(), Buf()])
            rmsnorm(lambda c: nmix[:, l, c:c + 1], lambda c: xn[c], xnb)
            _chk(f"norm{tile}{l}")
            if l % 2 == 0:
                uf = [B.alloc(W, F32) for _ in range(4)]
                ub = [B.alloc(W, BF16) for _ in range(4)]
                ufb = [[Buf() for _ in range(NT)] for _ in range(4)]
                ubb = [[Buf() for _ in range(NT)] for _ in range(4)]
                ycat = [B.alloc(W, BF16) for _ in range(NCH)]
                ycb = [[Buf() for _ in range(NT)] for _ in range(NCH)]
                mark = B.aoff
                CW = 2 + TP + 32 + TS
                cin = [B.alloc(CW, F32) for _ in range(4)]
                cinb = [Buf() for _ in range(4)]
                tmpf = [B.alloc(344, F32) for _ in range(2)]
                tmpb = [Buf() for _ in range(2)]
                accf = [B.alloc(344, F32) for _ in range(2)]
                accb = [Buf() for _ in range(2)]
                sco = B.alloc(4 * 34, F32)
                scob = Buf()
                def ev_u(si, ms, nt, bk, bb):
                    m = si * 2 + ms
                    a, b_ = ntiles[nt]
                    if os.environ.get("KEV", "") != "dve":
                        B.cp("act", uf[m][:, a:b_], bk, [bb], [ufb[m][nt]])
                    if os.environ.get("KEV", "") != "act":
                        B.cp("dve", ub[m][:, a:b_], uf[m][:, a:b_], [ufb[m][nt]], [ubb[m][nt]])
                if os.environ.get("KSTOP", "") == "slab":
                    B.slab(w_in_d[i], D, 0)
                    raise _Stop()
                proj(w_in_d[i], D, [0, 256], xn, xnb, ev_u)
                _chk(f"uproj{tile}{l}")
                for m in range(4):
                    B.cp("dve", cin[m][:, 0:2], car_s[:, i, m * 2:m * 2 + 2], [carb], [cinb[m]])
                    if has_s:
                        B.dma("sp", cin[m][:, 2 + TP:2 + TP + 32], sch_d[i, m * 128:(m + 1) * 128, :], osem["hst"], writes=[cinb[m]])
                cnt = [0]
                for j in range(2):
                    wx, wxb = B.slab(w_in_d[i], D, 512 + 256 * j)
                    wc, wcb = B.slab(w_in_d[i], D, 1536 + 256 * j)
                    wg, wgb = B.slab(w_in_d[i], D, 1024 + 256 * j)
                    for ms in range(2):
                        m = 2 * j + ms
                        for nt, (a, b_) in enumerate(ntiles):
                            n = b_ - a
                            off = 2 + a if a < TP else 2 + TP + 32 + (a - TP)
                            sh = 1 if a < TP else NSEQ
                            rhs = [xn[kt][:, a:b_] for kt in range(NCH)]
                            rb = [xnb[kt][nt] for kt in range(NCH)]
                            bx, bxb = B.bank()
                            B.mm_group(bx[:, 0:n], bxb, [wx[:, kt, ms * 128:(ms + 1) * 128] for kt in range(NCH)], rhs, [wxb] + rb)
                            bc, bcb = B.bank()
                            B.mm_group(bc[:, 0:n], bcb, [wc[:, kt, ms * 128:(ms + 1) * 128] for kt in range(NCH)], rhs, [wcb] + rb)
                            bg, bgb = B.bank()
                            B.mm_group(bg[:, 0:n], bgb, [wg[:, kt, ms * 128:(ms + 1) * 128] for kt in range(NCH)], rhs, [wgb] + rb)
                            q = cnt[0] % 2
                            cnt[0] += 1
                            B.cp("act", tmpf[q][:, 0:n], bx[:, 0:n], [bxb], [tmpb[q]])
                            B.tt("dve", cin[m][:, off:off + n], tmpf[q][:, 0:n], bc[:, 0:n], ALU.mult, [tmpb[q], bcb], [cinb[m]])
                            w0 = wsc[:, i, m * 3 + 0:m * 3 + 1]
                            w1 = wsc[:, i, m * 3 + 1:m * 3 + 2]
                            w2 = wsc[:, i, m * 3 + 2:m * 3 + 3]
                            B.act(accf[q][:, 0:n], cin[m][:, off - 2 * sh:off - 2 * sh + n], AF.Copy, [cinb[m], cb], [accb[q]], scale=w0)
                            B.stt(accf[q][:, 0:n], cin[m][:, off - sh:off - sh + n], w1, accf[q][:, 0:n], ALU.mult, ALU.add, [cinb[m], accb[q], cb], [accb[q]])
                            B.stt(accf[q][:, 0:n], cin[m][:, off:off + n], w2, accf[q][:, 0:n], ALU.mult, ALU.add, [cinb[m], accb[q], cb], [accb[q]])
                            B.tt("dve", ycat[4 + m][:, a:b_], accf[q][:, 0:n], bg[:, 0:n], ALU.mult, [accb[q], bgb], [ycb[4 + m][nt]])
                        if tile == 0:
                            B.cp("dve", car_s[:, i, m * 2:m * 2 + 2], cin[m][:, TP:TP + 2], [cinb[m]], [carb])
                        else:
                            B.cp("dve", sco[:, m * 34:m * 34 + 2], cin[m][:, TP:TP + 2], [cinb[m]], [scob])
                            s0 = 2 + TP + 32 + 32
                            B.cp("dve", sco[:, m * 34 + 2:m * 34 + 34], cin[m][:, s0:s0 + 32], [cinb[m]], [scob])
                if tile == 1:
                    B.dma("sp", sco_d[i], sco[:, :], osem["sco"], reads=[scob])
                _chk(f"sconv{tile}{l}")
                B.arena_reset()
                B.aoff = mark
                XW = 1 + TP + 16 * 5
                S0 = 1 + TP
                X = [B.alloc(XW, BF16) for _ in range(8)]
                Xb = [Buf() for _ in range(8)]
                Am = B.alloc(8 * 11 * 128, BF16)
                Amb2 = [[[Buf(), Buf()] for _ in range(11)] for _ in range(8)]
                Amb = None
                Bm = B.alloc(8 * 128, BF16)
                Cm = B.alloc(8 * 128, BF16)
                Bmb = [Buf() for _ in range(8)]
                Cmb = [Buf() for _ in range(8)]
                yf = [B.alloc(344, F32) for _ in range(2)]
                yfb = [Buf() for _ in range(2)]
                t1 = [B.alloc(344, F32) for _ in range(2)]
                t1b = [Buf() for _ in range(2)]
                B.memset("dve", Cm[:, :], 0.0, Cmb)
                ev = [0]
                ytails = []

                def build_A(c, g8, k, Am=Am, Amb2=Amb2):
                    g = 8 * c + g8
                    row = i * 11 + k
                    o = (g8 * 11 + k) * 128
                    B.tt("dve", Am[:, o:o + 128].rearrange("p (a b) -> p a b", a=2), emat[:, :].rearrange("p (a b) -> p a b", a=2),
                         vt[:, row, g, :].unsqueeze(2).to_broadcast([128, 2, 64]), ALU.mult, [cb, pb], [Amb2[g8][k][0]])
                def prologue_builds(c):
                    for g8 in range(8):
                        g = 8 * c + g8
                        build_A(c, g8, 0)
                        B.ts("dve", Bm[:, g8 * 128:(g8 + 1) * 128], bmch[:, i * 4 + c, :], gmask[:, g8:g8 + 1], ALU.mult, [cb, pb], [Bmb[g8]])
                        B.cp("act", X[g8][:, 0:1], hfin[:, i, g * 17:g * 17 + 1], [hfb], [Xb[g8]])
                        if has_s:
                            B.cp("act", X[g8][:, S0:S0 + 16], h0[:, i, g * NSEQ:(g + 1) * NSEQ], [cb], [Xb[g8]])
                prologue_builds(0)
                for c in range(4):
                    for g8 in range(8):
                        g = 8 * c + g8
                        B.cp("act", Cm[:, g8 * 128 + g8 * 16:g8 * 128 + g8 * 16 + 16], cmch[:, i * 4 + c, g8 * 16:g8 * 16 + 16], [cb, pb], [Cmb[g8]])
                        for nt, (a, b_) in enumerate(ntiles):
                            n = b_ - a
                            bk, bb = B.bank()
                            B.mm_group(bk[:, 0:n], bb, [Bm[:, g8 * 128:(g8 + 1) * 128]], [ub[c][:, a:b_]], [Bmb[g8], ubb[c][nt]])
                            dst = X[g8][:, 1 + a:1 + b_] if a < TP else X[g8][:, S0 + 16:S0 + 16 + TS]
                            ev[0] += 1
                            B.cp("act" if ev[0] % 2 else "dve", dst, bk[:, 0:n], [bb], [Xb[g8]])
                    NX = 1 + TP
                    steps = []
                    d = 1
                    while d < NX:
                        first = 2 * d - 1
                        if first < NX:
                            steps.append((d, first, (NX - 1 - first) // (2 * d) + 1))
                        d *= 2
                    d //= 2
                    while d >= 1:
                        first = 3 * d - 1
                        if first < NX:
                            steps.append((d, first, (NX - 1 - first) // (2 * d) + 1))
                        d //= 2
                    built = {0}
                    for (d, first, cntp) in steps:
                        k = POWS.index(d)
                        for kk in (k, k + 1):
                            if kk <= 10 and kk not in built:
                                built.add(kk)
                                for g8 in range(8):
                                    build_A(c, g8, kk)
                        for p0 in range(0, cntp, 512):
                            pc = min(512, cntp - p0)
                            f0 = first + 2 * d * p0
                            for g8 in range(8):
                                g = 8 * c + g8
                                o = (g8 * 11 + k) * 128
                                dstv = X[g8][:, f0:f0 + 2 * d * (pc - 1) + 1:2 * d]
                                srcv = X[g8][:, f0 - d:f0 - d + 2 * d * (pc - 1) + 1:2 * d]
                                bk, bb = B.bank()
                                B.mm_group(bk[:, 0:pc], bb, [identb[:], Am[:, o:o + 128]], [dstv, srcv], [Xb[g8], Amb2[g8][k][0], cb])
                                ev[0] += 1
                                last = f0 + 2 * d * (pc - 1)
                                en = "dve" if (last == NX - 1 or ev[0] % 2 == 0) else "act"
                                B.cp(en, dstv, bk[:, 0:pc], [bb], [Xb[g8]])
                                if last == NX - 1:
                                    B.cp("dve", hfin[:, i, g * 17:g * 17 + 1], bk[:, pc - 1:pc], [bb], [hfb])
                    if has_s:
                        for t in range(4):
                            for g8 in range(8):
                                g = 8 * c + g8
                                o = (g8 * 11 + 0) * 128
                                dstv = X[g8][:, S0 + 16 * (t + 1):S0 + 16 * (t + 2)]
                                srcv = X[g8][:, S0 + 16 * t:S0 + 16 * (t + 1)]
                                bk, bb = B.bank()
                                B.mm_group(bk[:, 0:16], bb, [identb[:], Am[:, o:o + 128]], [dstv, srcv], [Xb[g8], Amb2[g8][0][0], cb])
                                ev[0] += 1
                                B.cp("dve" if (t == 3 or ev[0] % 2 == 0) else "act", dstv, bk[:, 0:16], [bb], [Xb[g8]])
                                if t == 3:
                                    B.cp("dve", hfin[:, i, g * 17 + 1:g * 17 + 17], bk[:, 0:16], [bb], [hfb])
                    if c + 1 < 4:
                        prologue_builds(c + 1)
                    for nt, (a, b_) in enumerate(ntiles):
                        n = b_ - a
                        bk, bb = B.bank()
                        rhs = [(X[g8][:, 1 + a:1 + b_] if a < TP else X[g8][:, S0 + 16:S0 + 16 + TS]) for g8 in range(8)]
                        B.mm_group(bk[:, 0:n], bb, [Cm[:, g8 * 128:(g8 + 1) * 128] for g8 in range(8)], rhs, Xb + Cmb)
                        q = nt % 2
                        B.stt(yf[q][:, 0:n], uf[c][:, a:b_], ssmd[:, i, c:c + 1], bk[:, 0:n], ALU.mult, ALU.add, [ufb[c][nt], bb, cb], [yfb[q]])
                        B.tt("dve", t1[q][:, 0:n], yf[q][:, 0:n], yf[q][:, 0:n], ALU.mult, [yfb[q]], [t1b[q]])
                        B.ts("dve", t1[q][:, 0:n], t1[q][:, 0:n], 0.044715, ALU.mult, [t1b[q]], [t1b[q]], s2=1.0, op1=ALU.add)
                        B.tt("dve", t1[q][:, 0:n], t1[q][:, 0:n], yf[q][:, 0:n], ALU.mult, [t1b[q], yfb[q]], [t1b[q]])
                        B.act(t1[q][:, 0:n], t1[q][:, 0:n], AF.Sigmoid, [t1b[q]], [t1b[q]], scale=1.5957691216057308)

                        def _tail(q=q, n=n, a=a, b_=b_, c=c, nt=nt):
                            B.tt("dve", uf[c][:, a:b_], yf[q][:, 0:n], t1[q][:, 0:n], ALU.mult, [yfb[q], t1b[q]], [ufb[c][nt]])
                            B.cp("act", ub[c][:, a:b_], uf[c][:, a:b_], [ufb[c][nt]], [ubb[c][nt]])
                        if ytails:
                            ytails.pop(0)()
                        ytails.append(_tail)
                    while ytails:
                        ytails.pop(0)()
                _chk(f"scan{tile}{l}")
                if tile == 1:
                    B.dma("sp", hfin_d[i], hfin[:, i, :], osem["hfo"], reads=[hfb])
                sg = [B.alloc(344, F32) for _ in range(2)]
                sgb = [Buf() for _ in range(2)]

                def ev_glu(si, ms, nt, bk, bb):
                    m = si * 2 + ms
                    a, b_ = ntiles[nt]
                    n = b_ - a
                    q = nt % 2
                    B.act(sg[q][:, 0:n], bk, AF.Sigmoid, [bb, cb], [sgb[q]], bias=bglu[:, i, m:m + 1])
                    B.tt("dve", ycat[m][:, a:b_], uf[m][:, a:b_], sg[q][:, 0:n], ALU.mult, [ufb[m][nt], sgb[q]], [ycb[m][nt]])
                proj(w_glu_d[i], 512, [0, 256], ub, ubb, ev_glu)

                def ev_out(si, ms, nt, bk, bb):
                    m = si * 2 + ms
                    a, b_ = ntiles[nt]
                    B.tt("dve", xs[m][:, a:b_], bk, xs[m][:, a:b_], ALU.add, [bb, xb[m][nt]], [xb[m][nt]])
                proj(w_out_d[i], D, [0, 256, 512, 768], ycat, ycb, ev_out)
            else:
                VW = 30 + TP + 30 * NSEQ + TS
                VS = 30 + TP
                vb_ = [B.alloc(VW, BF16) for _ in range(NCH)]
                vbb = [Buf() for _ in range(NCH)]
                cco = B.alloc(NCH * 94, F32)
                ccob = Buf()
                sgm = [B.alloc(344, F32) for _ in range(3)]
                sgmb = [Buf() for _ in range(3)]
                vf = [B.alloc(344, F32) for _ in range(3)]
                vfb = [Buf() for _ in range(3)]
                gtails = []
                for c in range(NCH):
                    B.cp("dve", vb_[c][:, 0:30], car_c[:, i, c * 30:(c + 1) * 30], [carb], [vbb[c]])
                    if has_s:
                        B.dma("pool", vb_[c][:, VS:VS + 30 * NSEQ], cch_d[i, c * 128:(c + 1) * 128, :], osem["hstp"], writes=[vbb[c]])
                cnt = [0]
                for j in range(4):
                    wa, wab = B.slab(w_pw1_d[i], D, 256 * j)
                    wg, wgb = B.slab(w_pw1_d[i], D, 1024 + 256 * j)
                    for ms in range(2):
                        m = 2 * j + ms
                        for nt, (a, b_) in enumerate(ntiles):
                            n = b_ - a
                            off = 30 + a if a < TP else VS + 30 * NSEQ
                            rhs = [xn[kt][:, a:b_] for kt in range(NCH)]
                            rb = [xnb[kt][nt] for kt in range(NCH)]
                            ba, bab = B.bank()
                            B.mm_group(ba[:, 0:n], bab, [wa[:, kt, ms * 128:(ms + 1) * 128] for kt in range(NCH)], rhs, [wab] + rb)
                            bg, bgb = B.bank()
                            B.mm_group(bg[:, 0:n], bgb, [wg[:, kt, ms * 128:(ms + 1) * 128] for kt in range(NCH)], rhs, [wgb] + rb)
                            q = cnt[0] % 3
                            cnt[0] += 1
                            B.act(sgm[q][:, 0:n], bg[:, 0:n], AF.Sigmoid, [bgb, cb], [sgmb[q]], bias=bpw1[:, i, 8 + m:8 + m + 1])
                            B.stt(vf[q][:, 0:n], ba[:, 0:n], bpw1[:, i, m:m + 1], sgm[q][:, 0:n], ALU.add, ALU.mult, [bab, sgmb[q], cb], [vfb[q]])
                            def _tail(q=q, n=n, m=m, off=off, nt=nt):
                                B.cp("act", vb_[m][:, off:off + n], vf[q][:, 0:n], [vfb[q]], [vbb[m]])
                                if tile == 1 and nt == 2:
                                    B.cp("dve", cco[:, m * 94:m * 94 + 30], vf[q][:, n - 30:n], [vfb[q]], [ccob])
                                if tile == 1 and nt == 3:
                                    B.cp("dve", cco[:, m * 94 + 30:m * 94 + 94], vf[q][:, 0:n], [vfb[q]], [ccob])
                            if gtails:
                                gtails.pop(0)()
                            gtails.append(_tail)
                        while gtails:
                            gtails.pop(0)()
                        if tile == 0:
                            B.cp("dve", car_c[:, i, m * 30:(m + 1) * 30], vb_[m][:, TP:TP + 30], [vbb[m]], [carb])
                if tile == 1:
                    B.dma("sp", cco2_d[i], cco[:, :], osem["cco"], reads=[ccob])
                cvf = [B.alloc(W, F32) for _ in range(NCH)]
                cvfb = [[Buf() for _ in range(NT)] for _ in range(NCH)]
                cvb = xn
                cvbb = xnb
                dg = [B.alloc(31 * 128, BF16) for _ in range(2)]
                dgb2 = [[Buf() for _ in range(31)] for _ in range(2)]
                def build_dg(c, dg=dg, dgb2=dgb2):
                    q = c % 2
                    for k in range(31):
                        if k % 2 == 0:
                            B.act(dg[q][:, k * 128:(k + 1) * 128], identf[:], AF.Copy, [cb], [dgb2[q][k]], scale=wdw[:, i, c * 31 + k:c * 31 + k + 1])
                        else:
                            B.ts("dve", dg[q][:, k * 128:(k + 1) * 128], identf[:], wdw[:, i, c * 31 + k:c * 31 + k + 1], ALU.mult, [cb], [dgb2[q][k]])
                build_dg(0)
                for c in range(NCH):
                    q = c % 2
                    if c + 1 < NCH:
                        build_dg(c + 1)
                    for nt, (a, b_) in enumerate(ntiles):
                        n = b_ - a
                        if a < TP:
                            rhs = [vb_[c][:, a + k:a + k + n] for k in range(31)]
                        else:
                            rhs = [vb_[c][:, VS + k * NSEQ:VS + k * NSEQ + n] for k in range(31)]
                        bk, bb = B.bank()
                        B.mm_group(bk[:, 0:n], bb, [dg[q][:, k * 128:(k + 1) * 128] for k in range(31)], rhs, dgb2[q] + [vbb[c]])
                        B.cp("dve", cvf[c][:, a:b_], bk[:, 0:n], [bb], [cvfb[c][nt]])
                        B.cp("act", cvb[c][:, a:b_], cvf[c][:, a:b_], [cvfb[c][nt]], [cvbb[c][nt]])
                mus = [B.alloc(344, F32) for _ in range(2)]
                rsds = [B.alloc(344, F32) for _ in range(2)]
                msqs = [B.alloc(344, F32) for _ in range(2)]
                stbs = [Buf(), Buf()]
                sact = cvb
                sactb = cvbb
                ltails = []
                for nt, (a, b_) in enumerate(ntiles):
                    n = b_ - a
                    pp = nt % 2
                    sq8, sq8b = B.nrm_tmp[0][pp], B.nrm_tmp[1][pp]
                    b1, b1b = B.bank()
                    B.mm_group(b1[:, 0:n], b1b, [onesb[:]] * NCH, [cvb[c][:, a:b_] for c in range(NCH)], [cvbb[c][nt] for c in range(NCH)] + [cb])
                    for c in range(NCH):
                        B.act(sq8[c][:, 0:n], cvf[c][:, a:b_], AF.Square, [cvfb[c][nt]], [sq8b[c]])
                    b2, b2b = B.bank()
                    B.mm_group(b2[:, 0:n], b2b, [onesb[:]] * NCH, [sq8[c][:, 0:n] for c in range(NCH)], sq8b + [cb])

                    def _tail(nt=nt, n=n, a=a, b_=b_, b1=b1, b1b=b1b, b2=b2, b2b=b2b, mu=mus[pp], rsd=rsds[pp], msq=msqs[pp], stb=stbs[pp]):
                        B.ts("dve", mu[:, 0:n], b1[:, 0:n], float(1.0 / D), ALU.mult, [b1b], [stb])
                        B.tt("dve", msq[:, 0:n], mu[:, 0:n], mu[:, 0:n], ALU.mult, [stb], [stb])
                        B.stt(msq[:, 0:n], b2[:, 0:n], float(1.0 / D), msq[:, 0:n], ALU.mult, ALU.subtract, [b2b, stb], [stb])
                        B.act(rsd[:, 0:n], msq[:, 0:n], AF.Sqrt, [stb], [stb], bias=float(EPS), scale=1.0)
                        P.op("dve", lambda e, n=n, rsd=rsd: e.reciprocal(out=rsd[:, 0:n], in_=rsd[:, 0:n]), [stb], [stb])
                        for c in range(NCH):
                            B.tt("dve", cvf[c][:, a:b_], cvf[c][:, a:b_], mu[:, 0:n], ALU.subtract, [cvfb[c][nt], stb], [cvfb[c][nt]])
                            B.tt("dve", cvf[c][:, a:b_], cvf[c][:, a:b_], rsd[:, 0:n], ALU.mult, [cvfb[c][nt], stb], [cvfb[c][nt]])
                            B.act(sact[c][:, a:b_], cvf[c][:, a:b_], AF.Silu, [cvfb[c][nt], cb], [sactb[c][nt]],
                                  bias=lnb[:, i, c:c + 1], scale=lng[:, i, c:c + 1])
                    if ltails:
                        ltails.pop(0)()
                    ltails.append(_tail)
                while ltails:
                    ltails.pop(0)()

                def ev_pw2(si, ms, nt, bk, bb):
                    m = si * 2 + ms
                    a, b_ = ntiles[nt]
                    B.stt(xs[m][:, a:b_], bk, bpw2[:, i, m:m + 1], xs[m][:, a:b_], ALU.add, ALU.add, [bb, xb[m][nt], cb], [xb[m][nt]])
                proj(w_pw2_d[i], D, [0, 256, 512, 768], sact, sactb, ev_pw2)

            if KD == f"mixer{l}":
                for c in range(NCH):
                    B.dma("sp", dbg_d[c * 128:(c + 1) * 128, tile * TP:(tile + 1) * TP], xs[c][:, 0:TP], dbgsem, reads=xb[c][:3])
                    if has_s:
                        B.dma("sp", dbg_d[c * 128:(c + 1) * 128, 2 * TP:2 * TP + TS], xs[c][:, TP:TP + TS], dbgsem, reads=[xb[c][3]])
            _chk(f"mixer{tile}{l}")
            B.arena_reset()
            xn = [B.alloc(W, BF16) for _ in range(NCH)]
            xnb = [[Buf() for _ in range(NT)] for _ in range(NCH)]
            B.nrm_tmp = ([[B.alloc(344, BF16) for _ in range(NCH)] for _ in range(2)], [[Buf() for _ in range(NCH)] for _ in range(2)], [B.alloc(344, F32) for _ in range(2)], [Buf(), Buf()])
            rmsnorm(lambda c: nffn[:, l, c:c + 1], lambda c: xn[c], xnb)
            hh = [B.alloc(W, BF16) for _ in range(NFF)]
            hb = [[Buf() for _ in range(NT)] for _ in range(NFF)]
            AW = 2 + TP + 32 + TS
            asb = [B.alloc(AW, F32) for _ in range(2)]
            asbb = [[Buf() for _ in range(NT + 2)] for _ in range(2)]
            acc = [B.alloc(344, F32) for _ in range(3)]
            accb = [Buf() for _ in range(3)]
            tails = []
            ffo = B.alloc(NFF * 34, F32)
            ffob = Buf()
            fhist = B.alloc(NFF * 32, F32)
            fhb = Buf()
            if has_s:
                B.dma("sp", fhist[:, :].rearrange("p (m k) -> p m k", k=32), ffh_d[l].rearrange("(m p) k -> p m k", p=128), osem["hst"], writes=[fhb])
            cnt = [0]
            for j in range(11):
                wa, wab = B.slab(w_up_d[l], D, 256 * j)
                wg, wgb = B.slab(w_up_d[l], D, DFF + 256 * j)
                for ms in range(2):
                    m = 2 * j + ms
                    s = m % 2
                    B.cp("dve", asb[s][:, 0:2], car_f[:, l, m * 2:m * 2 + 2], [carb], [asbb[s][0]])
                    if has_s:
                        B.cp("dve", asb[s][:, 2 + TP:2 + TP + 32], fhist[:, m * 32:(m + 1) * 32], [fhb], [asbb[s][NT + 1]])
                    for nt, (a, b_) in enumerate(ntiles):
                        n = b_ - a
                        off = 2 + a if a < TP else 2 + TP + 32
                        sh = 1 if a < TP else NSEQ
                        rhs = [xn[kt][:, a:b_] for kt in range(NCH)]
                        rb = [xnb[kt][nt] for kt in range(NCH)]
                        ba, bab = B.bank()
                        B.mm_group(ba[:, 0:n], bab, [wa[:, kt, ms * 128:(ms + 1) * 128] for kt in range(NCH)], rhs, [wab] + rb)
                        bg, bgb = B.bank()
                        B.mm_group(bg[:, 0:n], bgb, [wg[:, kt, ms * 128:(ms + 1) * 128] for kt in range(NCH)], rhs, [wgb] + rb)
                        q = cnt[0] % 3
                        cnt[0] += 1
                        prevb = asbb[s][NT + 1] if a >= TP else asbb[s][nt]
                        curb = asbb[s][1 + nt]
                        B.cp("act", asb[s][:, off:off + n], ba[:, 0:n], [bab], [curb])
                        w0 = wfc[:, l, m * 3 + 0:m * 3 + 1]
                        w1 = wfc[:, l, m * 3 + 1:m * 3 + 2]
                        w2 = wfc[:, l, m * 3 + 2:m * 3 + 3]
                        B.act(acc[q][:, 0:n], asb[s][:, off - 2 * sh:off - 2 * sh + n], AF.Copy, [prevb, curb, cb], [accb[q]], scale=w0)
                        B.stt(acc[q][:, 0:n], asb[s][:, off - sh:off - sh + n], w1, acc[q][:, 0:n], ALU.mult, ALU.add, [prevb, curb, accb[q], cb], [accb[q]])
                        B.stt(acc[q][:, 0:n], asb[s][:, off:off + n], w2, acc[q][:, 0:n], ALU.mult, ALU.add, [curb, accb[q], cb], [accb[q]])
                        def _tail(q=q, n=n, m=m, a=a, b_=b_, bg=bg, bgb=bgb, nt=nt):
                            B.act(acc[q][:, 0:n], acc[q][:, 0:n], AF.Silu, [accb[q]], [accb[q]])
                            B.tt("dve", hh[m][:, a:b_], acc[q][:, 0:n], bg[:, 0:n], ALU.mult, [accb[q], bgb], [hb[m][nt]])
                        if tails:
                            tails.pop(0)()
                        tails.append(_tail)
                    if tile == 0:
                        B.cp("dve", car_f[:, l, m * 2:m * 2 + 2], asb[s][:, TP:TP + 2], [asbb[s][3]], [carb])
                    else:
                        B.cp("dve", ffo[:, m * 34:m * 34 + 2], asb[s][:, TP:TP + 2], [asbb[s][3]], [ffob])
                        s0 = 2 + TP + 32 + 32
                        B.cp("dve", ffo[:, m * 34 + 2:m * 34 + 34], asb[s][:, s0:s0 + 32], [asbb[s][4]], [ffob])
            while tails:
                tails.pop(0)()
            if tile == 1:
                B.dma("sp", ffo_d[l], ffo[:, :], osem["ffo"], reads=[ffob])

            def ev_dn(si, ms, nt, bk, bb):
                m = si
                a, b_ = ntiles[nt]
                B.tt("dve", xs[m][:, a:b_], bk, xs[m][:, a:b_], ALU.add, [bb, xb[m][nt]], [xb[m][nt]])
            proj(w_dn_d[l], DFF, [128 * k for k in range(8)], hh, hb, ev_dn, ncols=128)

            if KD == f"ffn{l}":
                for c in range(NCH):
                    B.dma("sp", dbg_d[c * 128:(c + 1) * 128, tile * TP:(tile + 1) * TP], xs[c][:, 0:TP], dbgsem, reads=xb[c][:3])
                    if has_s:
                        B.dma("sp", dbg_d[c * 128:(c + 1) * 128, 2 * TP:2 * TP + TS], xs[c][:, TP:TP + TS], dbgsem, reads=[xb[c][3]])
            _chk(f"ffn{tile}{l}")
        B.arena_reset()
        yo = [B.alloc(W, F32) for _ in range(NCH)]
        yob = [[Buf() for _ in range(NT)] for _ in range(NCH)]
        B.nrm_tmp = ([[B.alloc(344, BF16) for _ in range(NCH)] for _ in range(2)], [[Buf() for _ in range(NCH)] for _ in range(2)], [B.alloc(344, F32) for _ in range(2)], [Buf(), Buf()])
        rmsnorm(lambda c: nfin[:, c:c + 1], lambda c: yo[c], yob)
        for c in range(NCH):
            B.dma("sp", y_d[c * 128:(c + 1) * 128, tile * TP:(tile + 1) * TP], yo[c][:, 0:TP], ysem, reads=yob[c][:3])
            if has_s:
                B.dma("sp", y_d[c * 128:(c + 1) * 128, 2 * TP:2 * TP + TS], yo[c][:, TP:TP + TS], ysem, reads=[yob[c][3]])
        _chk(f"tile{tile}")

    return B


def _prep_inputs(inp):
    f = lambda a: np.ascontiguousarray(np.asarray(a, dtype=np.float32))
    shared = {}
    vec = lambda a, n: f(np.asarray(a).reshape(-1, n, 128).transpose(2, 0, 1))
    shared["nmix"] = vec(inp["norm_mix"], NCH)
    shared["nffn"] = vec(inp["norm_ffn"], NCH)
    shared["nfin"] = f(np.asarray(inp["norm_final"]).reshape(NCH, 128).T)
    lam_t = lambda a: f(np.concatenate([np.asarray(a).transpose(0, 2, 1)] * 2, axis=1))
    shared["lamre"] = lam_t(inp["ssm_lam_re"])
    shared["lamim"] = lam_t(inp["ssm_lam_im"])
    shared["logdt"] = f(np.broadcast_to(np.asarray(inp["ssm_log_dt"])[:, None, :], (2, 128, G)))
    shared["bre"] = f(np.asarray(inp["ssm_b_re"]).transpose(0, 2, 1, 3).reshape(2, 64, G * 16))
    shared["bim"] = f(np.asarray(inp["ssm_b_im"]).transpose(0, 2, 1, 3).reshape(2, 64, G * 16))
    cl = lambda a: np.asarray(a).reshape(2, 4, 8, 16, 64).transpose(0, 2, 3, 1, 4).reshape(2, 128, 4, 64)
    shared["cn"] = f(np.concatenate([cl(inp["ssm_c_re"]), cl(inp["ssm_c_im"])], axis=3).reshape(2, 128, 512))
    pv = lambda a, n: f(np.asarray(a).reshape(2, n, 128).transpose(0, 2, 1))
    shared["ssmd"] = pv(inp["ssm_d"], 4)
    shared["bglu"] = pv(inp["b_glu"], 4)
    shared["wsc"] = f(np.asarray(inp["w_sconv"]).reshape(2, 3, 4, 128).transpose(0, 3, 2, 1).reshape(2, 128, 12))
    shared["bpw1"] = pv(inp["b_conf_pw1"], 16)
    shared["wdw"] = f(np.asarray(inp["w_conf_dw"]).reshape(2, 31, NCH, 128).transpose(0, 3, 2, 1).reshape(2, 128, NCH * 31))
    shared["lng"] = pv(inp["conf_ln_g"], NCH)
    shared["lnb"] = pv(inp["conf_ln_b"], NCH)
    shared["bpw2"] = pv(inp["b_conf_pw2"], NCH)
    shared["wfc"] = f(np.asarray(inp["w_ffn_conv"]).reshape(DEPTH, 3, NFF, 128).transpose(0, 3, 2, 1).reshape(DEPTH, 128, NFF * 3))
    for k in ("w_in_even", "w_glu", "w_out_even", "w_conf_pw1", "w_conf_pw2", "w_ffn_up", "w_ffn_down"):
        shared[k] = f(inp[k])
    xp = np.asarray(inp["x_prompt"], dtype=np.float32)
    xsmp = np.asarray(inp["x_sample"], dtype=np.float32)
    meta = np.asarray(inp["meta_tokens"], dtype=np.float32)
    maps = []
    for c in range(NCORES):
        m = dict(shared)
        sl = slice(c * NSEQ, (c + 1) * NSEQ)
        xpc = np.concatenate([meta, xp[c]], axis=0)
        xsc = xsmp[sl].transpose(1, 0, 2).reshape(TS, D)
        m["x"] = f(np.concatenate([xpc, xsc], axis=0).T)
        hre = np.asarray(inp["state_ssm_re"])[:, sl].transpose(0, 3, 2, 1)
        him = np.asarray(inp["state_ssm_im"])[:, sl].transpose(0, 3, 2, 1)
        m["h0"] = f(np.concatenate([hre, him], axis=1))
        m["sconv_h"] = f(np.asarray(inp["state_sconv"])[:, sl].transpose(0, 3, 2, 1).reshape(2, 512, 2 * NSEQ))
        m["cconv_h"] = f(np.asarray(inp["state_cconv"])[:, sl].transpose(0, 3, 2, 1).reshape(2, D, 30 * NSEQ))
        m["ffn_h"] = f(np.asarray(inp["state_ffn"])[:, sl].transpose(0, 3, 2, 1).reshape(DEPTH, DFF, 2 * NSEQ))
        maps.append(m)
    return maps


def _unpack_states(r):
    sc = np.asarray(r["sconv_o"]).reshape(2, 128, 4, 34).transpose(0, 2, 1, 3).reshape(2, 512, 34)
    c2 = np.asarray(r["cconv_o2"]).reshape(2, 128, NCH, 94).transpose(0, 2, 1, 3).reshape(2, D, 94)
    hist = np.asarray(r["cconv_o"])
    cc = np.concatenate([c2[:, :, 0:30], hist, c2[:, :, 30:94]], axis=2)
    ff = np.asarray(r["ffn_o"]).reshape(DEPTH, 128, NFF, 34).transpose(0, 2, 1, 3).reshape(DEPTH, DFF, 34)
    return sc, cc, ff


_CACHE = {}


def kernel(**inputs):
    if "B" not in _CACHE:
        _CACHE["B"] = build_program()
    B = _CACHE["B"]
    maps = _prep_inputs(inputs)
    res = run_bass_kernel_spmd(B.nc, maps, core_ids=list(range(NCORES)))
    R = res.results
    BATCH = NCORES
    y_prompt = np.zeros((BATCH, 2048, D), np.float32)
    y_sample = np.zeros((NCORES * NSEQ, 4, D), np.float32)
    p_re = np.zeros((2, BATCH, G, 64), np.float32)
    p_im = np.zeros((2, BATCH, G, 64), np.float32)
    p_sc = np.zeros((2, BATCH, 2, 512), np.float32)
    p_cc = np.zeros((2, BATCH, 30, D), np.float32)
    p_ff = np.zeros((DEPTH, BATCH, 2, DFF), np.float32)
    s_re = np.zeros((2, NCORES * NSEQ, G, 64), np.float32)
    s_im = np.zeros((2, NCORES * NSEQ, G, 64), np.float32)
    s_sc = np.zeros((2, NCORES * NSEQ, 2, 512), np.float32)
    s_cc = np.zeros((2, NCORES * NSEQ, 30, D), np.float32)
    s_ff = np.zeros((DEPTH, NCORES * NSEQ, 2, DFF), np.float32)
    for c in range(NCORES):
        r = R[c]
        sl = slice(c * NSEQ, (c + 1) * NSEQ)
        y = np.asarray(r["y"])
        y_prompt[c] = y[:, 16:2 * TP].T
        y_sample[sl] = y[:, 2 * TP:].T.reshape(4, NSEQ, D).transpose(1, 0, 2)
        hf = np.asarray(r["hfin"]).reshape(2, 128, G, 17)
        p_re[:, c] = hf[:, 0:64, :, 0].transpose(0, 2, 1)
        p_im[:, c] = hf[:, 64:128, :, 0].transpose(0, 2, 1)
        s_re[:, sl] = hf[:, 0:64, :, 1:].transpose(0, 3, 2, 1)
        s_im[:, sl] = hf[:, 64:128, :, 1:].transpose(0, 3, 2, 1)
        sc, cc, ff = _unpack_states(r)
        p_sc[:, c] = sc[:, :, 0:2].transpose(0, 2, 1)
        s_sc[:, sl] = sc[:, :, 2:].reshape(2, 512, 2, NSEQ).transpose(0, 3, 2, 1)
        p_cc[:, c] = cc[:, :, 0:30].transpose(0, 2, 1)
        s_cc[:, sl] = cc[:, :, 30:].reshape(2, D, 30, NSEQ).transpose(0, 3, 2, 1)
        p_ff[:, c] = ff[:, :, 0:2].transpose(0, 2, 1)
        s_ff[:, sl] = ff[:, :, 2:].reshape(DEPTH, DFF, 2, NSEQ).transpose(0, 3, 2, 1)
    return (y_prompt, y_sample, p_re, p_im, p_sc, p_cc, p_ff, s_re, s_im, s_sc, s_cc, s_ff)
```

```python
import os
import numpy as np
from contextlib import ExitStack
import concourse.bass as bass
import concourse.mybir as mybir
from concourse.bass_utils import run_bass_kernel_spmd

F32 = mybir.dt.float32
BF16 = mybir.dt.bfloat16
AF = mybir.ActivationFunctionType
ALU = mybir.AluOpType

NCORES = 8
D = 1024
DEPTH = 4
NCH = 8
DFF = 2816
NFF = 22
TP = 1032
NSEQ = 16
TS = 64
G = 32
EPS = 1e-6
POWS = [1 << k for k in range(11)]
TWO_PI = float(2 * np.pi)
MAGIC = 12582912.0


class Buf:
    __slots__ = ("w", "r")

    def __init__(self):
        self.w = None
        self.r = []


class DSem:
    def __init__(self, name):
        self.name = name
        self.total = 0
        self.sem = None


class Op:
    __slots__ = ("eng", "fn", "deps", "dma", "sem", "cnt", "sig", "dsem")

    def __init__(self, eng, fn, dma):
        self.eng = eng
        self.fn = fn
        self.dma = dma
        self.deps = []
        self.sem = None
        self.cnt = 0
        self.sig = False
        self.dsem = None


class Prog:
    ENGS = ("pe", "act", "dve", "pool", "sp")

    def __init__(self, nc):
        self.nc = nc
        self.ops = {e: [] for e in self.ENGS}
        self.dsems = []
        self.pending = {e: [] for e in self.ENGS}
        self.last = {e: None for e in self.ENGS}
        self.dma_since = []

    def dsem(self, name):
        d = DSem(name)
        self.dsems.append(d)
        return d

    def barrier(self):
        L = [self.last[e] for e in ("pe", "act", "dve", "pool") if self.last[e] is not None]
        L += self.dma_since
        self.dma_since = []
        for e in ("pe", "act", "dve", "sp", "pool"):
            self.pending[e] = self.pending[e] + L

    def op(self, eng, fn, reads=(), writes=(), dma=None, nobar=False):
        isdma = dma is not None
        o = Op(eng, fn, isdma)
        deps = []
        for b in reads:
            if b.w is not None:
                deps.append((b.w, "raw"))
        for b in writes:
            if b.w is not None:
                deps.append((b.w, "waw"))
            for r in b.r:
                deps.append((r, "war"))
        if not nobar:
            for d in self.pending[eng]:
                deps.append((d, "bar"))
            self.pending[eng] = []
        seen = set()
        for d, kind in deps:
            if d is o or id(d) in seen:
                continue
            if (not isdma) and (not d.dma) and d.eng == eng:
                if eng == "pe":
                    continue
            seen.add(id(d))
            if d.dma:
                o.deps.append((d, d.dsem.total))
            else:
                o.deps.append((d, None))
                d.sig = True
        for b in reads:
            b.r.append(o)
        for b in writes:
            b.w = o
            b.r = []
        if isdma:
            dma.total += 16
            o.dsem = dma
            o.cnt = dma.total
            o.sig = True
            if not nobar:
                self.dma_since.append(o)
        else:
            self.last[eng] = o
        self.ops[eng].append(o)
        return o

    def build(self, es):
        nc = self.nc
        sems = {e: es.enter_context(nc.semaphore("s_" + e)) for e in self.ENGS}
        for d in self.dsems:
            if d.total > 0:
                d.sem = es.enter_context(nc.semaphore("d_" + d.name))
        for e in self.ENGS:
            c = 0
            for o in self.ops[e]:
                if o.dma:
                    o.sem = o.dsem.sem
                elif o.sig:
                    c += 1
                    o.sem = sems[e]
                    o.cnt = c
        if os.environ.get("KBAR", "1") == "1":
            allsems = list(sems.values()) + [d.sem for d in self.dsems if d.sem is not None]
            for sm in allsems:
                nc.gpsimd.sem_clear(sm)
            nc.all_engine_barrier()
        block = es.enter_context(nc.Block())
        prog = self

        def run(engine, e):
            seen = {}
            used = {}
            for o in prog.ops[e]:
                need = {}
                for d, thr in o.deps:
                    c = thr if thr is not None else d.cnt
                    s = d.sem
                    if seen.get(s.num, 0) >= c:
                        continue
                    if need.get(s.num, (None, 0))[1] < c:
                        need[s.num] = (s, c)
                for s, c in need.values():
                    seen[s.num] = c
                    engine.wait_ge(s, c)
                ins = o.fn(engine)
                if o.sig:
                    ins.then_inc(o.sem, 16 if o.dma else 1)
                    if o.dma:
                        used[o.dsem.name] = o.dsem
            for d in used.values():
                if seen.get(d.sem.num, 0) < d.total:
                    engine.wait_ge(d.sem, d.total)

        @block.tensor
        def _(eng):
            run(eng, "pe")

        @block.scalar
        def _(eng):
            run(eng, "act")

        @block.vector
        def _(eng):
            run(eng, "dve")

        @block.gpsimd
        def _(eng):
            run(eng, "pool")

        @block.sync
        def _(eng):
            run(eng, "sp")


class Builder:
    def __init__(self):
        self.nc = bass.Bass("TRN2", target_bir_lowering=False)
        self.es = ExitStack()
        self.P = Prog(self.nc)
        self.din = {}
        self.dout = {}
        self.rr = 0

    def inp(self, name, shape):
        t = self.nc.dram_tensor(name, list(shape), F32, kind="ExternalInput").ap()
        self.din[name] = t
        return t

    def outp(self, name, shape):
        t = self.nc.dram_tensor(name, list(shape), F32, kind="ExternalOutput").ap()
        self.dout[name] = t
        return t

    def sb(self, name, shape, dt):
        return self.es.enter_context(self.nc.sbuf_tensor("sb_" + name, list(shape), dt))

    def arena_reset(self):
        self.P.barrier()
        self.aoff = 0

    def alloc(self, cols, dt):
        n32 = cols if dt == F32 else (cols + 1) // 2
        n32 = (n32 + 1) // 2 * 2
        o = self.aoff
        self.aoff += n32
        assert self.aoff <= self.ARENA, ("arena overflow", self.aoff)
        v = self.arena[:, o:o + n32]
        if dt != F32:
            v = v.bitcast(dt)
        return v

    def bank(self):
        i = self.rr % 8
        self.rr += 1
        return self.banks[i], self.bankb[i]

    def act(self, out, in_, func, reads, writes, bias=None, scale=None):
        kw = {}
        if bias is not None:
            kw["bias"] = bias
        if scale is not None:
            kw["scale"] = scale
        return self.P.op("act", lambda e: e.activation(out=out, in_=in_, func=func, **kw), reads, writes)

    def tt(self, eng, out, in0, in1, op, reads, writes):
        return self.P.op(eng, lambda e: e.tensor_tensor(out=out, in0=in0, in1=in1, op=op), reads, writes)

    def ts(self, eng, out, in0, s1, op0, reads, writes, s2=None, op1=None):
        if s2 is None:
            return self.P.op(eng, lambda e: e.tensor_scalar(out=out, in0=in0, scalar1=s1, scalar2=None, op0=op0), reads, writes)
        return self.P.op(eng, lambda e: e.tensor_scalar(out=out, in0=in0, scalar1=s1, scalar2=s2, op0=op0, op1=op1), reads, writes)

    def stt(self, out, in0, scalar, in1, op0, op1, reads, writes):
        return self.P.op("dve", lambda e: e.scalar_tensor_tensor(out=out, in0=in0, scalar=scalar, in1=in1, op0=op0, op1=op1), reads, writes)

    def cp(self, eng, out, in_, reads, writes):
        if eng == "act":
            return self.act(out, in_, AF.Copy, reads, writes)
        return self.P.op(eng, lambda e: e.tensor_copy(out=out, in_=in_), reads, writes)

    def memset(self, eng, ap, val, writes):
        return self.P.op(eng, lambda e: e.memset(ap, val), (), writes)

    def dma(self, eng, out, in_, dsem, reads=(), writes=(), nobar=False):
        return self.P.op(eng, lambda e: e.dma_start(out=out, in_=in_), reads, writes, dma=dsem, nobar=nobar)

    def slab(self, Wd, K, c0, ncols=256):
        KT = K // 128
        i = self.wrr % self.NSLOT
        self.wrr += 1
        t = self.wslots[i][:, 0:KT * ncols].rearrange("p (kt m) -> p kt m", m=ncols)
        src = Wd[:, c0:c0 + ncols].rearrange("(kt p) m -> p kt m", p=128)
        self.dma("pool", t, src, self.wsem[i], writes=[self.wbuf[i]], nobar=True)
        return t, self.wbuf[i]

    def mm_group(self, bank_ap, bankb, lhs_list, rhs_list, reads):
        n = len(lhs_list)

        def fn(e):
            ins = None
            for k in range(n):
                ins = e.matmul(bank_ap, lhs_list[k], rhs_list[k], start=(k == 0), stop=(k == n - 1))
            return ins
        return self.P.op("pe", fn, reads, [bankb])


class _Stop(Exception):
    pass


_CNT = [0]


def _chk(name):
    if os.environ.get("KSTOP", "") == name:
        raise _Stop()


def build_program():
    B = Builder()
    try:
        _build_body(B)
    except _Stop:
        pass
    B.P.build(B.es)
    if os.environ.get("KENDBAR", "1") == "1":
        B.nc.all_engine_barrier()
    return B


def _build_body(B):
    nc, P, es = B.nc, B.P, B.es
    W = TP + TS

    x_d = B.inp("x", [D, 2 * TP + TS])
    h0_d = B.inp("h0", [2, 128, G, NSEQ])
    sch_d = B.inp("sconv_h", [2, 512, 2 * NSEQ])
    cch_d = B.inp("cconv_h", [2, D, 30 * NSEQ])
    ffh_d = B.inp("ffn_h", [DEPTH, DFF, 2 * NSEQ])
    nmix_d = B.inp("nmix", [128, DEPTH, NCH])
    nffn_d = B.inp("nffn", [128, DEPTH, NCH])
    nfin_d = B.inp("nfin", [128, NCH])
    lamre_d = B.inp("lamre", [2, 128, G])
    lamim_d = B.inp("lamim", [2, 128, G])
    logdt_d = B.inp("logdt", [2, 128, G])
    bre_d = B.inp("bre", [2, 64, G * 16])
    bim_d = B.inp("bim", [2, 64, G * 16])
    cn_d = B.inp("cn", [2, 128, 4 * 128])
    ssmd_d = B.inp("ssmd", [2, 128, 4])
    bglu_d = B.inp("bglu", [2, 128, 4])
    wsc_d = B.inp("wsc", [2, 128, 4 * 3])
    bpw1_d = B.inp("bpw1", [2, 128, 16])
    wdw_d = B.inp("wdw", [2, 128, NCH * 31])
    lng_d = B.inp("lng", [2, 128, NCH])
    lnb_d = B.inp("lnb", [2, 128, NCH])
    bpw2_d = B.inp("bpw2", [2, 128, NCH])
    wfc_d = B.inp("wfc", [DEPTH, 128, NFF * 3])
    w_in_d = B.inp("w_in_even", [2, D, 2048])
    w_glu_d = B.inp("w_glu", [2, 512, 512])
    w_out_d = B.inp("w_out_even", [2, D, D])
    w_pw1_d = B.inp("w_conf_pw1", [2, D, 2048])
    w_pw2_d = B.inp("w_conf_pw2", [2, D, D])
    w_up_d = B.inp("w_ffn_up", [DEPTH, D, 2 * DFF])
    w_dn_d = B.inp("w_ffn_down", [DEPTH, DFF, D])

    y_d = B.outp("y", [D, 2 * TP + TS])
    hfin_d = B.outp("hfin", [2, 128, G * 17])
    sco_d = B.outp("sconv_o", [2, 128, 4 * 34])
    cco_d = B.outp("cconv_o", [2, D, 26 * NSEQ])
    cco2_d = B.outp("cconv_o2", [2, 128, NCH * 94])
    ffo_d = B.outp("ffn_o", [DEPTH, 128, NFF * 34])
    KD = os.environ.get("KDUMP", "")
    if KD:
        dbg_d = B.outp("dbg", [D, 2 * TP + TS])
        dbgsem = P.dsem("dbg")

    xs = [B.sb(f"x{c}", [128, W], F32) for c in range(NCH)]
    B.NSLOT = 4
    B.wslots = [B.sb(f"wsl{i}", [128, NFF * 128], BF16) for i in range(B.NSLOT)]
    B.wbuf = [Buf() for _ in range(B.NSLOT)]
    B.wsem = [P.dsem(f"w{i}") for i in range(B.NSLOT)]
    B.wrr = 0
    B.banks = [es.enter_context(nc.psum_tensor(f"bank{i}", [128, 512], F32)) for i in range(8)]
    B.bankb = [Buf() for _ in range(8)]
    B.ARENA = 32000
    B.arena = B.sb("arena", [128, B.ARENA], F32)
    B.aoff = 0

    identf = B.sb("identf", [128, 128], F32)
    identb = B.sb("identb", [128, 128], BF16)
    emat = B.sb("emat", [128, 128], F32)
    onesb = B.sb("onesb", [128, 128], BF16)
    gmask = B.sb("gmask", [128, 8], F32)
    cb = Buf()

    nmix = B.sb("nmix", [128, DEPTH, NCH], F32)
    nffn = B.sb("nffn", [128, DEPTH, NCH], F32)
    nfin = B.sb("nfin", [128, NCH], F32)
    lamre = B.sb("lamre", [128, 2, G], F32)
    lamim = B.sb("lamim", [128, 2, G], F32)
    logdt = B.sb("logdt", [128, 2, G], F32)
    bre = B.alloc(2 * G * 16, F32)[0:64, :].rearrange("p (i k) -> p i k", i=2)
    bim = B.alloc(2 * G * 16, F32)[0:64, :].rearrange("p (i k) -> p i k", i=2)
    pre = B.alloc(12 * G, F32).rearrange("p (a g) -> p a g", g=G)
    preb = B.alloc(2 * G * 16, F32)[0:64, :].rearrange("p (a k) -> p a k", a=2)
    wARG, wANG, wMAG, wT, wR, wSIN, wCOS, wRE, wIM = [B.alloc(11 * G, F32) for _ in range(9)]
    wB1, wB2 = B.alloc(G * 16, F32), B.alloc(G * 16, F32)
    cn = B.alloc(2 * 512, F32).rearrange("p (i k) -> p i k", i=2)
    ssmd = B.sb("ssmd", [128, 2, 4], F32)
    bglu = B.sb("bglu", [128, 2, 4], F32)
    wsc = B.sb("wsc", [128, 2, 12], F32)
    bpw1 = B.sb("bpw1", [128, 2, 16], F32)
    wdw = B.sb("wdw", [128, 2, NCH * 31], F32)
    lng = B.sb("lng", [128, 2, NCH], F32)
    lnb = B.sb("lnb", [128, 2, NCH], F32)
    bpw2 = B.sb("bpw2", [128, 2, NCH], F32)
    wfc = B.sb("wfc", [128, DEPTH, NFF * 3], F32)
    h0 = B.sb("h0", [128, 2, G * NSEQ], F32)
    hfin = B.sb("hfin", [128, 2, G * 17], F32)
    vt = B.sb("vt", [128, 2 * 11, G, 2], F32)
    bmch = B.sb("bmch", [128, 2 * 4, 128], BF16)
    cmch = B.sb("cmch", [128, 2 * 4, 128], BF16)
    car_f = B.sb("car_f", [128, DEPTH, NFF * 2], F32)
    car_s = B.sb("car_s", [128, 2, 4 * 2], F32)
    car_c = B.sb("car_c", [128, 2, NCH * 30], BF16)
    hfb = Buf()
    carb = Buf()

    csem = P.dsem("const")
    for t, d in [(nmix, nmix_d), (nffn, nffn_d), (nfin, nfin_d), (wfc, wfc_d.rearrange("l p k -> p l k"))]:
        B.dma("sp", t[:], d, csem, writes=[cb])
    for t, d in [(lamre, lamre_d), (lamim, lamim_d), (logdt, logdt_d), (bre, bre_d), (bim, bim_d), (cn, cn_d),
                 (ssmd, ssmd_d), (bglu, bglu_d), (wsc, wsc_d), (bpw1, bpw1_d), (wdw, wdw_d), (lng, lng_d),
                 (lnb, lnb_d), (bpw2, bpw2_d), (h0, h0_d.rearrange("i p g s -> i p (g s)"))]:
        B.dma("sp", t[:], d.rearrange("i p k -> p i k"), csem, writes=[cb])
    ddsem = P.dsem("d2d")
    for i in range(2):
        B.dma("sp", cco_d[i], cch_d[i, :, 4 * NSEQ:30 * NSEQ], ddsem)

    B.memset("dve", identf[:], 1.0, [cb])
    P.op("pool", lambda e: e.affine_select(out=identf[:], in_=identf[:], pattern=[[-1, 128]], compare_op=ALU.is_equal,
                                           fill=0.0, base=0, channel_multiplier=1), [cb], [cb])
    B.cp("dve", identb[:], identf[:], [cb], [cb])
    B.memset("dve", onesb[:], 1.0, [cb])
    B.memset("dve", emat[:], 0.0, [cb])
    B.cp("dve", emat[:], identf[:], [cb], [cb])
    B.cp("dve", emat[0:64, 64:128], identf[0:64, 0:64], [cb], [cb])
    B.cp("dve", emat[64:128, 0:64], identf[64:128, 64:128], [cb], [cb])
    B.memset("dve", gmask[:], 0.0, [cb])
    P.op("dve", lambda e: e.tensor_reduce(out=gmask[:], in_=identf[:].rearrange("p (g j) -> p g j", j=16),
                                          axis=mybir.AxisListType.X, op=ALU.add), [cb], [cb])
    B.memset("dve", hfin[:], 0.0, [hfb])
    B.memset("dve", car_f[:], 0.0, [carb])
    B.memset("dve", car_s[:], 0.0, [carb])
    B.memset("dve", car_c[:], 0.0, [carb])

    xsem = P.dsem("x")
    xb0 = [[Buf() for _ in range(3)] for _ in range(NCH)]
    for c in range(NCH):
        B.dma("sp", xs[c][:, 0:TP], x_d[c * 128:(c + 1) * 128, 0:TP], xsem, writes=xb0[c][:3])
    _chk("const")
    pb = Buf()
    R = [cb, pb]

    def pre_(k):
        return pre[:, k, :]

    def range_reduce(out, ang):
        B.ts("dve", pre_(10), ang, float(1 / TWO_PI), ALU.mult, R, [pb], s2=MAGIC, op1=ALU.add)
        B.ts("dve", pre_(10), pre_(10), -MAGIC, ALU.add, R, [pb])
        B.stt(out, pre_(10), -TWO_PI, ang, ALU.mult, ALU.add, R, [pb])
        B.ts("dve", out, out, -float(np.pi), ALU.max, R, [pb], s2=float(np.pi), op1=ALU.min)

    for i in range(2):
        B.act(pre_(0), logdt[:, i, :], AF.Exp, R, [pb])
        B.tt("dve", pre_(1), pre_(0), lamre[:, i, :], ALU.mult, R, [pb])
        B.tt("dve", pre_(2), pre_(0), lamim[:, i, :], ALU.mult, R, [pb])
        NW = 11 * G
        wkb = [Buf() for _ in range(22)]
        for k, d in enumerate(POWS):
            B.ts("dve", wARG[:, k * G:(k + 1) * G], pre_(1), float(d), ALU.mult, R, [wkb[k]])
            B.ts("dve", wANG[:, k * G:(k + 1) * G], pre_(2), float(d), ALU.mult, R, [wkb[11 + k]])
        RW = [cb, pb] + wkb
        B.act(wMAG[:, :], wARG[:, :], AF.Exp, RW, [pb])

        def rr_wide(out, ang):
            B.ts("dve", wT[:, :], ang, float(1 / TWO_PI), ALU.mult, RW, [pb], s2=MAGIC, op1=ALU.add)
            B.ts("dve", wT[:, :], wT[:, :], -MAGIC, ALU.add, RW, [pb])
            B.stt(out, wT[:, :], -TWO_PI, ang, ALU.mult, ALU.add, RW, [pb])
            B.ts("dve", out, out, -float(np.pi), ALU.max, RW, [pb], s2=float(np.pi), op1=ALU.min)
        rr_wide(wR[:, :], wANG[:, :])
        B.act(wSIN[:, :], wR[:, :], AF.Sin, RW, [pb])
        B.ts("dve", wANG[:, :], wANG[:, :], float(np.pi / 2), ALU.add, RW, [pb])
        rr_wide(wR[:, :], wANG[:, :])
        B.act(wCOS[:, :], wR[:, :], AF.Sin, RW, [pb])
        B.tt("dve", wRE[:, :], wMAG[:, :], wCOS[:, :], ALU.mult, RW, [pb])
        B.tt("dve", wIM[:, :], wMAG[:, :], wSIN[:, :], ALU.mult, RW, [pb])
        r3 = lambda t, lo, hi: t[lo:hi, :].rearrange("p (k g) -> p k g", g=G)
        B.cp("dve", vt[0:64, i * 11:(i + 1) * 11, :, 0], r3(wRE, 0, 64), RW, [pb])
        B.ts("dve", vt[64:128, i * 11:(i + 1) * 11, :, 0], r3(wIM, 64, 128), -1.0, ALU.mult, RW, [pb])
        B.cp("dve", vt[0:64, i * 11:(i + 1) * 11, :, 1], r3(wIM, 0, 64), RW, [pb])
        B.cp("dve", vt[64:128, i * 11:(i + 1) * 11, :, 1], r3(wRE, 64, 128), RW, [pb])
        B.cp("dve", pre_(8), wRE[:, 0:G], RW, [pb])
        B.cp("dve", pre_(9), wIM[:, 0:G], RW, [pb])
        for d in (1,):
            if d == 1:
                B.ts("dve", pre_(8), pre_(8), -1.0, ALU.add, R, [pb])
                B.tt("dve", pre_(3), lamre[:, i, :], lamre[:, i, :], ALU.mult, R, [pb])
                B.tt("dve", pre_(4), lamim[:, i, :], lamim[:, i, :], ALU.mult, R, [pb])
                B.tt("dve", pre_(3), pre_(3), pre_(4), ALU.add, R, [pb])
                P.op("dve", lambda e: e.reciprocal(out=pre[:, 3, :], in_=pre[:, 3, :]), R, [pb])
                B.tt("dve", pre_(5), pre_(8), lamre[:, i, :], ALU.mult, R, [pb])
                B.tt("dve", pre_(6), pre_(9), lamim[:, i, :], ALU.mult, R, [pb])
                B.tt("dve", pre_(5), pre_(5), pre_(6), ALU.add, R, [pb])
                B.tt("dve", pre_(11), pre_(5), pre_(3), ALU.mult, R, [pb])
                B.tt("dve", pre_(5), pre_(9), lamre[:, i, :], ALU.mult, R, [pb])
                B.tt("dve", pre_(6), pre_(8), lamim[:, i, :], ALU.mult, R, [pb])
                B.tt("dve", pre_(5), pre_(5), pre_(6), ALU.subtract, R, [pb])
                B.tt("dve", pre_(7), pre_(5), pre_(3), ALU.mult, R, [pb])
                B.ts("dve", pre_(6), pre_(7), -1.0, ALU.mult, R, [pb])
                pgb = [Buf(), Buf()]
                v3 = lambda t: t.rearrange("p (g j) -> p g j", j=16)
                bc = lambda row: pre[0:64, row, :].unsqueeze(2).to_broadcast([64, G, 16])
                B.tt("dve", v3(preb[:, 0, :]), v3(bre[:, i, :]), bc(11), ALU.mult, R, [pgb[0]])
                B.tt("dve", v3(wB1[0:64, :]), v3(bim[:, i, :]), bc(7), ALU.mult, R, [pgb[0]])
                B.tt("dve", preb[:, 0, :], preb[:, 0, :], wB1[0:64, :], ALU.subtract, R + [pgb[0]], [pgb[0]])
                B.tt("dve", v3(preb[:, 1, :]), v3(bim[:, i, :]), bc(11), ALU.mult, R, [pgb[1]])
                B.tt("dve", v3(wB2[0:64, :]), v3(bre[:, i, :]), bc(7), ALU.mult, R, [pgb[1]])
                B.tt("dve", preb[:, 1, :], preb[:, 1, :], wB2[0:64, :], ALU.add, R + [pgb[1]], [pgb[1]])
                for c in range(4):
                    bk, bb = B.bank()
                    P.op("pe", lambda e, c=c, bk=bk: e.transpose(bk[:, 0:64], preb[:, 0, c * 128:(c + 1) * 128], identf[0:64, 0:64]), R + pgb, [bb])
                    P.op("pe", lambda e, c=c, bk=bk: e.transpose(bk[:, 64:128], preb[:, 1, c * 128:(c + 1) * 128], identf[0:64, 0:64]), R + pgb + [bb], [bb])
                    B.cp("dve", bmch[:, i * 4 + c, :], bk[:, 0:128], [bb], [pb])
                    bk2, bb2 = B.bank()
                    P.op("pe", lambda e, c=c, bk2=bk2, i=i: e.transpose(bk2[:, 0:128], cn[:, i, c * 128:(c + 1) * 128], identf[:]), R, [bb2])
                    B.cp("dve", cmch[0:64, i * 4 + c, :], bk2[0:64, 0:128], [bb2], [pb])
                    B.ts("dve", cmch[64:128, i * 4 + c, :], bk2[64:128, 0:128], -1.0, ALU.mult, [bb2], [pb])

    _chk("pre")
    B.arena_reset()
    ysem = P.dsem("y")
    osem = {k: P.dsem(k) for k in ("ffo", "sco", "cco", "hfo", "hst", "hstp")}

    for tile in range(2):
        has_s = (tile == 1)
        ntiles = [(0, 344), (344, 688), (688, 1032)] + ([(TP, TP + TS)] if has_s else [])
        NT = len(ntiles)
        Wt = TP + (TS if has_s else 0)
        if tile == 0:
            xb = xb0
        else:
            xb = [[Buf() for _ in range(NT)] for _ in range(NCH)]
            B.P.barrier()
            for c in range(NCH):
                B.dma("sp", xs[c][:, 0:TP], x_d[c * 128:(c + 1) * 128, tile * TP:(tile + 1) * TP], xsem, writes=xb[c][:3])
                B.dma("sp", xs[c][:, TP:TP + TS], x_d[c * 128:(c + 1) * 128, 2 * TP:2 * TP + TS], xsem, writes=[xb[c][3]])

        def rmsnorm(gain_ap_fn, out_fn, out_bufs, out_dt_f32=False):
            sqs, sqbs, rss, rsbs = B.nrm_tmp
            tails = []
            for nt, (c0, c1) in enumerate(ntiles):
                n = c1 - c0
                sq, sqb, rs, rsb = sqs[nt % 2], sqbs[nt % 2], rss[nt % 2], rsbs[nt % 2]
                for c in range(NCH):
                    B.act(sq[c][:, 0:n], xs[c][:, c0:c1], AF.Square, [xb[c][nt]], [sqb[c]])
                bk, bb = B.bank()
                B.mm_group(bk[:, 0:n], bb, [onesb[:]] * NCH, [sq[c][:, 0:n] for c in range(NCH)], sqb + [cb])

                def _tail(nt=nt, n=n, c0=c0, c1=c1, bk=bk, bb=bb, rs=rs, rsb=rsb):
                    B.act(rs[:, 0:n], bk[:, 0:n], AF.Sqrt, [bb], [rsb], bias=float(EPS), scale=float(1.0 / D))
                    P.op("dve", lambda e, n=n, rs=rs: e.reciprocal(out=rs[:, 0:n], in_=rs[:, 0:n]), [rsb], [rsb])
                    for c in range(NCH):
                        B.stt(out_fn(c)[:, c0:c1], xs[c][:, c0:c1], gain_ap_fn(c), rs[:, 0:n], ALU.mult, ALU.mult,
                              [xb[c][nt], rsb, cb], [out_bufs[c][nt]])
                if tails:
                    tails.pop(0)()
                tails.append(_tail)
            while tails:
                tails.pop(0)()

        def proj(Wd, K, col_list, inp_aps, inp_bufs, evac, ncols=256):
            KT = K // 128
            for si, c0 in enumerate(col_list):
                wt, wb = B.slab(Wd, K, c0, ncols)
                for ms in range(ncols // 128):
                    for nt, (a, b_) in enumerate(ntiles):
                        n = b_ - a
                        bk, bb = B.bank()
                        B.mm_group(bk[:, 0:n], bb, [wt[:, kt, ms * 128:(ms + 1) * 128] for kt in range(KT)],
                                   [inp_aps[kt][:, a:b_] for kt in range(KT)], [wb] + [inp_bufs[kt][nt] for kt in range(KT)])
                        if os.environ.get("KNOEVAC", "") != "1":
                            evac(si, ms, nt, bk[:, 0:n], bb)
                        _CNT[0] += 1
                        _chk("g%d" % _CNT[0])

        for l in range(DEPTH):
            i = l // 2
            B.arena_reset()
            xn = [B.alloc(W, BF16) for _ in range(NCH)]
            xnb = [[Buf() for _ in range(NT)] for _ in range(NCH)]
            B.nrm_tmp = ([[B.alloc(344, BF16) for _ in range(NCH)] for _ in range(2)], [[Buf() for _ in range(NCH)] for _ in range(2)], [B.alloc(344, F32) for _ in range(2)], [Buf(), Buf()])
            rmsnorm(lambda c: nmix[:, l, c:c + 1], lambda c: xn[c], xnb)
            _chk(f"norm{tile}{l}")
            if l % 2 == 0:
                uf = [B.alloc(W, F32) for _ in range(4)]
                ub = [B.alloc(W, BF16) for _ in range(4)]
                ufb = [[Buf() for _ in range(NT)] for _ in range(4)]
                ubb = [[Buf() for _ in range(NT)] for _ in range(4)]
                ycat = [B.alloc(W, BF16) for _ in range(NCH)]
                ycb = [[Buf() for _ in range(NT)] for _ in range(NCH)]
                mark = B.aoff
                CW = 2 + TP + 32 + TS
                cin = [B.alloc(CW, F32) for _ in range(4)]
                cinb = [Buf() for _ in range(4)]
                tmpf = [B.alloc(344, F32) for _ in range(2)]
                tmpb = [Buf() for _ in range(2)]
                accf = [B.alloc(344, F32) for _ in range(2)]
                accb = [Buf() for _ in range(2)]
                sco = B.alloc(4 * 34, F32)
                scob = Buf()
                def ev_u(si, ms, nt, bk, bb):
                    m = si * 2 + ms
                    a, b_ = ntiles[nt]
                    if os.environ.get("KEV", "") != "dve":
                        B.cp("act", uf[m][:, a:b_], bk, [bb], [ufb[m][nt]])
                    if os.environ.get("KEV", "") != "act":
                        B.cp("dve", ub[m][:, a:b_], uf[m][:, a:b_], [ufb[m][nt]], [ubb[m][nt]])
                if os.environ.get("KSTOP", "") == "slab":
                    B.slab(w_in_d[i], D, 0)
                    raise _Stop()
                proj(w_in_d[i], D, [0, 256], xn, xnb, ev_u)
                _chk(f"uproj{tile}{l}")
                for m in range(4):
                    B.cp("dve", cin[m][:, 0:2], car_s[:, i, m * 2:m * 2 + 2], [carb], [cinb[m]])
                    if has_s:
                        B.dma("sp", cin[m][:, 2 + TP:2 + TP + 32], sch_d[i, m * 128:(m + 1) * 128, :], osem["hst"], writes=[cinb[m]])
                cnt = [0]
                for j in range(2):
                    wx, wxb = B.slab(w_in_d[i], D, 512 + 256 * j)
                    wc, wcb = B.slab(w_in_d[i], D, 1536 + 256 * j)
                    wg, wgb = B.slab(w_in_d[i], D, 1024 + 256 * j)
                    for ms in range(2):
                        m = 2 * j + ms
                        for nt, (a, b_) in enumerate(ntiles):
                            n = b_ - a
                            off = 2 + a if a < TP else 2 + TP + 32 + (a - TP)
                            sh = 1 if a < TP else NSEQ
                            rhs = [xn[kt][:, a:b_] for kt in range(NCH)]
                            rb = [xnb[kt][nt] for kt in range(NCH)]
                            bx, bxb = B.bank()
                            B.mm_group(bx[:, 0:n], bxb, [wx[:, kt, ms * 128:(ms + 1) * 128] for kt in range(NCH)], rhs, [wxb] + rb)
                            bc, bcb = B.bank()
                            B.mm_group(bc[:, 0:n], bcb, [wc[:, kt, ms * 128:(ms + 1) * 128] for kt in range(NCH)], rhs, [wcb] + rb)
                            bg, bgb = B.bank()
                            B.mm_group(bg[:, 0:n], bgb, [wg[:, kt, ms * 128:(ms + 1) * 128] for kt in range(NCH)], rhs, [wgb] + rb)
                            q = cnt[0] % 2
                            cnt[0] += 1
                            B.cp("act", tmpf[q][:, 0:n], bx[:, 0:n], [bxb], [tmpb[q]])
                            B.tt("dve", cin[m][:, off:off + n], tmpf[q][:, 0:n], bc[:, 0:n], ALU.mult, [tmpb[q], bcb], [cinb[m]])
                            w0 = wsc[:, i, m * 3 + 0:m * 3 + 1]
                            w1 = wsc[:, i, m * 3 + 1:m * 3 + 2]
                            w2 = wsc[:, i, m * 3 + 2:m * 3 + 3]
                            B.act(accf[q][:, 0:n], cin[m][:, off - 2 * sh:off - 2 * sh + n], AF.Copy, [cinb[m], cb], [accb[q]], scale=w0)
                            B.stt(accf[q][:, 0:n], cin[m][:, off - sh:off - sh + n], w1, accf[q][:, 0:n], ALU.mult, ALU.add, [cinb[m], accb[q], cb], [accb[q]])
                            B.stt(accf[q][:, 0:n], cin[m][:, off:off + n], w2, accf[q][:, 0:n], ALU.mult, ALU.add, [cinb[m], accb[q], cb], [accb[q]])
                            B.tt("dve", ycat[4 + m][:, a:b_], accf[q][:, 0:n], bg[:, 0:n], ALU.mult, [accb[q], bgb], [ycb[4 + m][nt]])
                        if tile == 0:
                            B.cp("dve", car_s[:, i, m * 2:m * 2 + 2], cin[m][:, TP:TP + 2], [cinb[m]], [carb])
                        else:
                            B.cp("dve", sco[:, m * 34:m * 34 + 2], cin[m][:, TP:TP + 2], [cinb[m]], [scob])
                            s0 = 2 + TP + 32 + 32
                            B.cp("dve", sco[:, m * 34 + 2:m * 34 + 34], cin[m][:, s0:s0 + 32], [cinb[m]], [scob])
                if tile == 1:
                    B.dma("sp", sco_d[i], sco[:, :], osem["sco"], reads=[scob])
                _chk(f"sconv{tile}{l}")
                B.arena_reset()
                B.aoff = mark
                XW = 1 + TP + 16 * 5
                S0 = 1 + TP
                X = [B.alloc(XW, BF16) for _ in range(8)]
                Xb = [Buf() for _ in range(8)]
                Am = B.alloc(8 * 11 * 128, BF16)
                Amb2 = [[[Buf(), Buf()] for _ in range(11)] for _ in range(8)]
                Amb = None
                Bm = B.alloc(8 * 128, BF16)
                Cm = B.alloc(8 * 128, BF16)
                Bmb = [Buf() for _ in range(8)]
                Cmb = [Buf() for _ in range(8)]
                yf = [B.alloc(344, F32) for _ in range(2)]
                yfb = [Buf() for _ in range(2)]
                t1 = [B.alloc(344, F32) for _ in range(2)]
                t1b = [Buf() for _ in range(2)]
                B.memset("dve", Cm[:, :], 0.0, Cmb)
                ev = [0]
                ytails = []

                def build_A(c, g8, k, Am=Am, Amb2=Amb2):
                    g = 8 * c + g8
                    row = i * 11 + k
                    o = (g8 * 11 + k) * 128
                    B.tt("dve", Am[:, o:o + 128].rearrange("p (a b) -> p a b", a=2), emat[:, :].rearrange("p (a b) -> p a b", a=2),
                         vt[:, row, g, :].unsqueeze(2).to_broadcast([128, 2, 64]), ALU.mult, [cb, pb], [Amb2[g8][k][0]])
                def prologue_builds(c):
                    for g8 in range(8):
                        g = 8 * c + g8
                        build_A(c, g8, 0)
                        B.ts("dve", Bm[:, g8 * 128:(g8 + 1) * 128], bmch[:, i * 4 + c, :], gmask[:, g8:g8 + 1], ALU.mult, [cb, pb], [Bmb[g8]])
                        B.cp("act", X[g8][:, 0:1], hfin[:, i, g * 17:g * 17 + 1], [hfb], [Xb[g8]])
                        if has_s:
                            B.cp("act", X[g8][:, S0:S0 + 16], h0[:, i, g * NSEQ:(g + 1) * NSEQ], [cb], [Xb[g8]])
                def prologue_rest(c):
                    for g8 in range(8):
                        g = 8 * c + g8
                        B.cp("act", Cm[:, g8 * 128 + g8 * 16:g8 * 128 + g8 * 16 + 16], cmch[:, i * 4 + c, g8 * 16:g8 * 16 + 16], [cb, pb], [Cmb[g8]])
                        for nt, (a, b_) in enumerate(ntiles):
                            n = b_ - a
                            bk, bb = B.bank()
                            B.mm_group(bk[:, 0:n], bb, [Bm[:, g8 * 128:(g8 + 1) * 128]], [ub[c][:, a:b_]], [Bmb[g8], ubb[c][nt]])
                            dst = X[g8][:, 1 + a:1 + b_] if a < TP else X[g8][:, S0 + 16:S0 + 16 + TS]
                            ev[0] += 1
                            B.cp("act" if ev[0] % 2 else "dve", dst, bk[:, 0:n], [bb], [Xb[g8]])
                yf = yf + [B.arena[:, 4384:4384 + 344], B.arena[:, 4384 + 344:4384 + 688]]
                yfb = yfb + [Buf(), Buf()]
                prologue_builds(0)
                prologue_rest(0)
                for c in range(4):
                    NX = 1 + TP
                    steps = []
                    d = 1
                    while d < NX:
                        first = 2 * d - 1
                        if first < NX:
                            steps.append((d, first, (NX - 1 - first) // (2 * d) + 1))
                        d *= 2
                    d //= 2
                    while d >= 1:
                        first = 3 * d - 1
                        if first < NX:
                            steps.append((d, first, (NX - 1 - first) // (2 * d) + 1))
                        d //= 2
                    built = {0}
                    for (d, first, cntp) in steps:
                        k = POWS.index(d)
                        for kk in (k, k + 1):
                            if kk <= 10 and kk not in built:
                                built.add(kk)
                                for g8 in range(8):
                                    build_A(c, g8, kk)
                        for p0 in range(0, cntp, 512):
                            pc = min(512, cntp - p0)
                            f0 = first + 2 * d * p0
                            for g8 in range(8):
                                g = 8 * c + g8
                                o = (g8 * 11 + k) * 128
                                dstv = X[g8][:, f0:f0 + 2 * d * (pc - 1) + 1:2 * d]
                                srcv = X[g8][:, f0 - d:f0 - d + 2 * d * (pc - 1) + 1:2 * d]
                                bk, bb = B.bank()
                                B.mm_group(bk[:, 0:pc], bb, [identb[:], Am[:, o:o + 128]], [dstv, srcv], [Xb[g8], Amb2[g8][k][0], cb])
                                ev[0] += 1
                                last = f0 + 2 * d * (pc - 1)
                                en = "dve" if (last == NX - 1 or ev[0] % 2 == 0) else "act"
                                B.cp(en, dstv, bk[:, 0:pc], [bb], [Xb[g8]])
                                if last == NX - 1:
                                    B.cp("dve", hfin[:, i, g * 17:g * 17 + 1], bk[:, pc - 1:pc], [bb], [hfb])
                    if has_s:
                        for t in range(4):
                            for g8 in range(8):
                                g = 8 * c + g8
                                o = (g8 * 11 + 0) * 128
                                dstv = X[g8][:, S0 + 16 * (t + 1):S0 + 16 * (t + 2)]
                                srcv = X[g8][:, S0 + 16 * t:S0 + 16 * (t + 1)]
                                bk, bb = B.bank()
                                B.mm_group(bk[:, 0:16], bb, [identb[:], Am[:, o:o + 128]], [dstv, srcv], [Xb[g8], Amb2[g8][0][0], cb])
                                ev[0] += 1
                                B.cp("dve" if (t == 3 or ev[0] % 2 == 0) else "act", dstv, bk[:, 0:16], [bb], [Xb[g8]])
                                if t == 3:
                                    B.cp("dve", hfin[:, i, g * 17 + 1:g * 17 + 17], bk[:, 0:16], [bb], [hfb])
                    if c + 1 < 4:
                        prologue_builds(c + 1)
                    for nt, (a, b_) in enumerate(ntiles):
                        n = b_ - a
                        bk, bb = B.bank()
                        rhs = [(X[g8][:, 1 + a:1 + b_] if a < TP else X[g8][:, S0 + 16:S0 + 16 + TS]) for g8 in range(8)]
                        B.mm_group(bk[:, 0:n], bb, [Cm[:, g8 * 128:(g8 + 1) * 128] for g8 in range(8)], rhs, Xb + Cmb)
                        B.stt(yf[nt][:, 0:n], uf[c][:, a:b_], ssmd[:, i, c:c + 1], bk[:, 0:n], ALU.mult, ALU.add, [ufb[c][nt], bb, cb], [yfb[nt]])
                    if c + 1 < 4:
                        prologue_rest(c + 1)
                    for nt, (a, b_) in enumerate(ntiles):
                        n = b_ - a
                        q = nt % 2
                        B.tt("dve", t1[q][:, 0:n], yf[nt][:, 0:n], yf[nt][:, 0:n], ALU.mult, [yfb[nt]], [t1b[q]])
                        B.ts("dve", t1[q][:, 0:n], t1[q][:, 0:n], 0.044715, ALU.mult, [t1b[q]], [t1b[q]], s2=1.0, op1=ALU.add)
                        B.tt("dve", t1[q][:, 0:n], t1[q][:, 0:n], yf[nt][:, 0:n], ALU.mult, [t1b[q], yfb[nt]], [t1b[q]])
                        B.act(t1[q][:, 0:n], t1[q][:, 0:n], AF.Sigmoid, [t1b[q]], [t1b[q]], scale=1.5957691216057308)

                        def _tail(q=q, n=n, a=a, b_=b_, c=c, nt=nt):
                            B.tt("dve", uf[c][:, a:b_], yf[nt][:, 0:n], t1[q][:, 0:n], ALU.mult, [yfb[nt], t1b[q]], [ufb[c][nt]])
                            B.cp("act", ub[c][:, a:b_], uf[c][:, a:b_], [ufb[c][nt]], [ubb[c][nt]])
                        if ytails:
                            ytails.pop(0)()
                        ytails.append(_tail)
                    while ytails:
                        ytails.pop(0)()
                _chk(f"scan{tile}{l}")
                if tile == 1:
                    B.dma("sp", hfin_d[i], hfin[:, i, :], osem["hfo"], reads=[hfb])
                sg = [B.alloc(344, F32) for _ in range(2)]
                sgb = [Buf() for _ in range(2)]

                def ev_glu(si, ms, nt, bk, bb):
                    m = si * 2 + ms
                    a, b_ = ntiles[nt]
                    n = b_ - a
                    q = nt % 2
                    B.act(sg[q][:, 0:n], bk, AF.Sigmoid, [bb, cb], [sgb[q]], bias=bglu[:, i, m:m + 1])
                    B.tt("dve", ycat[m][:, a:b_], uf[m][:, a:b_], sg[q][:, 0:n], ALU.mult, [ufb[m][nt], sgb[q]], [ycb[m][nt]])
                proj(w_glu_d[i], 512, [0, 256], ub, ubb, ev_glu)

                def ev_out(si, ms, nt, bk, bb):
                    m = si * 2 + ms
                    a, b_ = ntiles[nt]
                    B.tt("dve", xs[m][:, a:b_], bk, xs[m][:, a:b_], ALU.add, [bb, xb[m][nt]], [xb[m][nt]])
                proj(w_out_d[i], D, [0, 256, 512, 768], ycat, ycb, ev_out)
            else:
                VW = 30 + TP + 30 * NSEQ + TS
                VS = 30 + TP
                vb_ = [B.alloc(VW, BF16) for _ in range(NCH)]
                vbb = [Buf() for _ in range(NCH)]
                cco = B.alloc(NCH * 94, F32)
                ccob = Buf()
                sgm = [B.alloc(344, F32) for _ in range(3)]
                sgmb = [Buf() for _ in range(3)]
                vf = [B.alloc(344, F32) for _ in range(3)]
                vfb = [Buf() for _ in range(3)]
                gtails = []
                for c in range(NCH):
                    B.cp("dve", vb_[c][:, 0:30], car_c[:, i, c * 30:(c + 1) * 30], [carb], [vbb[c]])
                    if has_s:
                        B.dma("pool", vb_[c][:, VS:VS + 30 * NSEQ], cch_d[i, c * 128:(c + 1) * 128, :], osem["hstp"], writes=[vbb[c]])
                cnt = [0]
                for j in range(4):
                    wa, wab = B.slab(w_pw1_d[i], D, 256 * j)
                    wg, wgb = B.slab(w_pw1_d[i], D, 1024 + 256 * j)
                    for ms in range(2):
                        m = 2 * j + ms
                        for nt, (a, b_) in enumerate(ntiles):
                            n = b_ - a
                            off = 30 + a if a < TP else VS + 30 * NSEQ
                            rhs = [xn[kt][:, a:b_] for kt in range(NCH)]
                            rb = [xnb[kt][nt] for kt in range(NCH)]
                            ba, bab = B.bank()
                            B.mm_group(ba[:, 0:n], bab, [wa[:, kt, ms * 128:(ms + 1) * 128] for kt in range(NCH)], rhs, [wab] + rb)
                            bg, bgb = B.bank()
                            B.mm_group(bg[:, 0:n], bgb, [wg[:, kt, ms * 128:(ms + 1) * 128] for kt in range(NCH)], rhs, [wgb] + rb)
                            q = cnt[0] % 3
                            cnt[0] += 1
                            B.act(sgm[q][:, 0:n], bg[:, 0:n], AF.Sigmoid, [bgb, cb], [sgmb[q]], bias=bpw1[:, i, 8 + m:8 + m + 1])
                            B.stt(vf[q][:, 0:n], ba[:, 0:n], bpw1[:, i, m:m + 1], sgm[q][:, 0:n], ALU.add, ALU.mult, [bab, sgmb[q], cb], [vfb[q]])
                            def _tail(q=q, n=n, m=m, off=off, nt=nt):
                                B.cp("act", vb_[m][:, off:off + n], vf[q][:, 0:n], [vfb[q]], [vbb[m]])
                                if tile == 1 and nt == 2:
                                    B.cp("dve", cco[:, m * 94:m * 94 + 30], vf[q][:, n - 30:n], [vfb[q]], [ccob])
                                if tile == 1 and nt == 3:
                                    B.cp("dve", cco[:, m * 94 + 30:m * 94 + 94], vf[q][:, 0:n], [vfb[q]], [ccob])
                            if gtails:
                                gtails.pop(0)()
                            gtails.append(_tail)
                        while gtails:
                            gtails.pop(0)()
                        if tile == 0:
                            B.cp("dve", car_c[:, i, m * 30:(m + 1) * 30], vb_[m][:, TP:TP + 30], [vbb[m]], [carb])
                if tile == 1:
                    B.dma("sp", cco2_d[i], cco[:, :], osem["cco"], reads=[ccob])
                cvf = [B.alloc(W, F32) for _ in range(NCH)]
                cvfb = [[Buf() for _ in range(NT)] for _ in range(NCH)]
                cvb = xn
                cvbb = xnb
                dg = [B.alloc(31 * 128, BF16) for _ in range(2)]
                dgb2 = [[Buf() for _ in range(31)] for _ in range(2)]
                def build_dg(c, dg=dg, dgb2=dgb2):
                    q = c % 2
                    for k in range(31):
                        if k % 2 == 0:
                            B.act(dg[q][:, k * 128:(k + 1) * 128], identf[:], AF.Copy, [cb], [dgb2[q][k]], scale=wdw[:, i, c * 31 + k:c * 31 + k + 1])
                        else:
                            B.ts("dve", dg[q][:, k * 128:(k + 1) * 128], identf[:], wdw[:, i, c * 31 + k:c * 31 + k + 1], ALU.mult, [cb], [dgb2[q][k]])
                build_dg(0)
                for c in range(NCH):
                    q = c % 2
                    if c + 1 < NCH:
                        build_dg(c + 1)
                    for nt, (a, b_) in enumerate(ntiles):
                        n = b_ - a
                        if a < TP:
                            rhs = [vb_[c][:, a + k:a + k + n] for k in range(31)]
                        else:
                            rhs = [vb_[c][:, VS + k * NSEQ:VS + k * NSEQ + n] for k in range(31)]
                        bk, bb = B.bank()
                        B.mm_group(bk[:, 0:n], bb, [dg[q][:, k * 128:(k + 1) * 128] for k in range(31)], rhs, dgb2[q] + [vbb[c]])
                        B.cp("dve", cvf[c][:, a:b_], bk[:, 0:n], [bb], [cvfb[c][nt]])
                        B.cp("act", cvb[c][:, a:b_], cvf[c][:, a:b_], [cvfb[c][nt]], [cvbb[c][nt]])
                mus = [B.alloc(344, F32) for _ in range(2)]
                rsds = [B.alloc(344, F32) for _ in range(2)]
                msqs = [B.alloc(344, F32) for _ in range(2)]
                stbs = [Buf(), Buf()]
                sact = cvb
                sactb = cvbb
                ltails = []
                for nt, (a, b_) in enumerate(ntiles):
                    n = b_ - a
                    pp = nt % 2
                    sq8, sq8b = B.nrm_tmp[0][pp], B.nrm_tmp[1][pp]
                    b1, b1b = B.bank()
                    B.mm_group(b1[:, 0:n], b1b, [onesb[:]] * NCH, [cvb[c][:, a:b_] for c in range(NCH)], [cvbb[c][nt] for c in range(NCH)] + [cb])
                    for c in range(NCH):
                        B.act(sq8[c][:, 0:n], cvf[c][:, a:b_], AF.Square, [cvfb[c][nt]], [sq8b[c]])
                    b2, b2b = B.bank()
                    B.mm_group(b2[:, 0:n], b2b, [onesb[:]] * NCH, [sq8[c][:, 0:n] for c in range(NCH)], sq8b + [cb])

                    def _tail(nt=nt, n=n, a=a, b_=b_, b1=b1, b1b=b1b, b2=b2, b2b=b2b, mu=mus[pp], rsd=rsds[pp], msq=msqs[pp], stb=stbs[pp]):
                        B.ts("dve", mu[:, 0:n], b1[:, 0:n], float(1.0 / D), ALU.mult, [b1b], [stb])
                        B.tt("dve", msq[:, 0:n], mu[:, 0:n], mu[:, 0:n], ALU.mult, [stb], [stb])
                        B.stt(msq[:, 0:n], b2[:, 0:n], float(1.0 / D), msq[:, 0:n], ALU.mult, ALU.subtract, [b2b, stb], [stb])
                        B.act(rsd[:, 0:n], msq[:, 0:n], AF.Sqrt, [stb], [stb], bias=float(EPS), scale=1.0)
                        P.op("dve", lambda e, n=n, rsd=rsd: e.reciprocal(out=rsd[:, 0:n], in_=rsd[:, 0:n]), [stb], [stb])
                        for c in range(NCH):
                            B.tt("dve", cvf[c][:, a:b_], cvf[c][:, a:b_], mu[:, 0:n], ALU.subtract, [cvfb[c][nt], stb], [cvfb[c][nt]])
                            B.tt("dve", cvf[c][:, a:b_], cvf[c][:, a:b_], rsd[:, 0:n], ALU.mult, [cvfb[c][nt], stb], [cvfb[c][nt]])
                            B.act(sact[c][:, a:b_], cvf[c][:, a:b_], AF.Silu, [cvfb[c][nt], cb], [sactb[c][nt]],
                                  bias=lnb[:, i, c:c + 1], scale=lng[:, i, c:c + 1])
                    if ltails:
                        ltails.pop(0)()
                    ltails.append(_tail)
                while ltails:
                    ltails.pop(0)()

                def ev_pw2(si, ms, nt, bk, bb):
                    m = si * 2 + ms
                    a, b_ = ntiles[nt]
                    B.stt(xs[m][:, a:b_], bk, bpw2[:, i, m:m + 1], xs[m][:, a:b_], ALU.add, ALU.add, [bb, xb[m][nt], cb], [xb[m][nt]])
                proj(w_pw2_d[i], D, [0, 256, 512, 768], sact, sactb, ev_pw2)

            if KD == f"mixer{l}":
                for c in range(NCH):
                    B.dma("sp", dbg_d[c * 128:(c + 1) * 128, tile * TP:(tile + 1) * TP], xs[c][:, 0:TP], dbgsem, reads=xb[c][:3])
                    if has_s:
                        B.dma("sp", dbg_d[c * 128:(c + 1) * 128, 2 * TP:2 * TP + TS], xs[c][:, TP:TP + TS], dbgsem, reads=[xb[c][3]])
            _chk(f"mixer{tile}{l}")
            B.arena_reset()
            xn = [B.alloc(W, BF16) for _ in range(NCH)]
            xnb = [[Buf() for _ in range(NT)] for _ in range(NCH)]
            B.nrm_tmp = ([[B.alloc(344, BF16) for _ in range(NCH)] for _ in range(2)], [[Buf() for _ in range(NCH)] for _ in range(2)], [B.alloc(344, F32) for _ in range(2)], [Buf(), Buf()])
            rmsnorm(lambda c: nffn[:, l, c:c + 1], lambda c: xn[c], xnb)
            hh = [B.alloc(W, BF16) for _ in range(NFF)]
            hb = [[Buf() for _ in range(NT)] for _ in range(NFF)]
            AW = 2 + TP + 32 + TS
            asb = [B.alloc(AW, F32) for _ in range(2)]
            asbb = [[Buf() for _ in range(NT + 2)] for _ in range(2)]
            acc = [B.alloc(344, F32) for _ in range(3)]
            accb = [Buf() for _ in range(3)]
            tails = []
            ffo = B.alloc(NFF * 34, F32)
            ffob = Buf()
            fhist = B.alloc(NFF * 32, F32)
            fhb = Buf()
            if has_s:
                B.dma("sp", fhist[:, :].rearrange("p (m k) -> p m k", k=32), ffh_d[l].rearrange("(m p) k -> p m k", p=128), osem["hst"], writes=[fhb])
            cnt = [0]
            for j in range(11):
                wa, wab = B.slab(w_up_d[l], D, 256 * j)
                wg, wgb = B.slab(w_up_d[l], D, DFF + 256 * j)
                for ms in range(2):
                    m = 2 * j + ms
                    s = m % 2
                    B.cp("dve", asb[s][:, 0:2], car_f[:, l, m * 2:m * 2 + 2], [carb], [asbb[s][0]])
                    if has_s:
                        B.cp("dve", asb[s][:, 2 + TP:2 + TP + 32], fhist[:, m * 32:(m + 1) * 32], [fhb], [asbb[s][NT + 1]])
                    for nt, (a, b_) in enumerate(ntiles):
                        n = b_ - a
                        off = 2 + a if a < TP else 2 + TP + 32
                        sh = 1 if a < TP else NSEQ
                        rhs = [xn[kt][:, a:b_] for kt in range(NCH)]
                        rb = [xnb[kt][nt] for kt in range(NCH)]
                        ba, bab = B.bank()
                        B.mm_group(ba[:, 0:n], bab, [wa[:, kt, ms * 128:(ms + 1) * 128] for kt in range(NCH)], rhs, [wab] + rb)
                        bg, bgb = B.bank()
                        B.mm_group(bg[:, 0:n], bgb, [wg[:, kt, ms * 128:(ms + 1) * 128] for kt in range(NCH)], rhs, [wgb] + rb)
                        q = cnt[0] % 3
                        cnt[0] += 1
                        prevb = asbb[s][NT + 1] if a >= TP else asbb[s][nt]
                        curb = asbb[s][1 + nt]
                        B.cp("act", asb[s][:, off:off + n], ba[:, 0:n], [bab], [curb])
                        w0 = wfc[:, l, m * 3 + 0:m * 3 + 1]
                        w1 = wfc[:, l, m * 3 + 1:m * 3 + 2]
                        w2 = wfc[:, l, m * 3 + 2:m * 3 + 3]
                        B.act(acc[q][:, 0:n], asb[s][:, off - 2 * sh:off - 2 * sh + n], AF.Copy, [prevb, curb, cb], [accb[q]], scale=w0)
                        B.stt(acc[q][:, 0:n], asb[s][:, off - sh:off - sh + n], w1, acc[q][:, 0:n], ALU.mult, ALU.add, [prevb, curb, accb[q], cb], [accb[q]])
                        B.stt(acc[q][:, 0:n], asb[s][:, off:off + n], w2, acc[q][:, 0:n], ALU.mult, ALU.add, [curb, accb[q], cb], [accb[q]])
                        def _tail(q=q, n=n, m=m, a=a, b_=b_, bg=bg, bgb=bgb, nt=nt):
                            B.act(acc[q][:, 0:n], acc[q][:, 0:n], AF.Silu, [accb[q]], [accb[q]])
                            B.tt("dve", hh[m][:, a:b_], acc[q][:, 0:n], bg[:, 0:n], ALU.mult, [accb[q], bgb], [hb[m][nt]])
                        if tails:
                            tails.pop(0)()
                        tails.append(_tail)
                    if tile == 0:
                        B.cp("dve", car_f[:, l, m * 2:m * 2 + 2], asb[s][:, TP:TP + 2], [asbb[s][3]], [carb])
                    else:
                        B.cp("dve", ffo[:, m * 34:m * 34 + 2], asb[s][:, TP:TP + 2], [asbb[s][3]], [ffob])
                        s0 = 2 + TP + 32 + 32
                        B.cp("dve", ffo[:, m * 34 + 2:m * 34 + 34], asb[s][:, s0:s0 + 32], [asbb[s][4]], [ffob])
            while tails:
                tails.pop(0)()
            if tile == 1:
                B.dma("sp", ffo_d[l], ffo[:, :], osem["ffo"], reads=[ffob])

            def ev_dn(si, ms, nt, bk, bb):
                m = si
                a, b_ = ntiles[nt]
                B.tt("dve", xs[m][:, a:b_], bk, xs[m][:, a:b_], ALU.add, [bb, xb[m][nt]], [xb[m][nt]])
            proj(w_dn_d[l], DFF, [128 * k for k in range(8)], hh, hb, ev_dn, ncols=128)

            if KD == f"ffn{l}":
                for c in range(NCH):
                    B.dma("sp", dbg_d[c * 128:(c + 1) * 128, tile * TP:(tile + 1) * TP], xs[c][:, 0:TP], dbgsem, reads=xb[c][:3])
                    if has_s:
                        B.dma("sp", dbg_d[c * 128:(c + 1) * 128, 2 * TP:2 * TP + TS], xs[c][:, TP:TP + TS], dbgsem, reads=[xb[c][3]])
            _chk(f"ffn{tile}{l}")
        B.arena_reset()
        yo = [B.alloc(W, F32) for _ in range(NCH)]
        yob = [[Buf() for _ in range(NT)] for _ in range(NCH)]
        B.nrm_tmp = ([[B.alloc(344, BF16) for _ in range(NCH)] for _ in range(2)], [[Buf() for _ in range(NCH)] for _ in range(2)], [B.alloc(344, F32) for _ in range(2)], [Buf(), Buf()])
        rmsnorm(lambda c: nfin[:, c:c + 1], lambda c: yo[c], yob)
        for c in range(NCH):
            B.dma("sp", y_d[c * 128:(c + 1) * 128, tile * TP:(tile + 1) * TP], yo[c][:, 0:TP], ysem, reads=yob[c][:3])
            if has_s:
                B.dma("sp", y_d[c * 128:(c + 1) * 128, 2 * TP:2 * TP + TS], yo[c][:, TP:TP + TS], ysem, reads=[yob[c][3]])
        _chk(f"tile{tile}")

    return B


def _prep_inputs(inp):
    f = lambda a: np.ascontiguousarray(np.asarray(a, dtype=np.float32))
    shared = {}
    vec = lambda a, n: f(np.asarray(a).reshape(-1, n, 128).transpose(2, 0, 1))
    shared["nmix"] = vec(inp["norm_mix"], NCH)
    shared["nffn"] = vec(inp["norm_ffn"], NCH)
    shared["nfin"] = f(np.asarray(inp["norm_final"]).reshape(NCH, 128).T)
    lam_t = lambda a: f(np.concatenate([np.asarray(a).transpose(0, 2, 1)] * 2, axis=1))
    shared["lamre"] = lam_t(inp["ssm_lam_re"])
    shared["lamim"] = lam_t(inp["ssm_lam_im"])
    shared["logdt"] = f(np.broadcast_to(np.asarray(inp["ssm_log_dt"])[:, None, :], (2, 128, G)))
    shared["bre"] = f(np.asarray(inp["ssm_b_re"]).transpose(0, 2, 1, 3).reshape(2, 64, G * 16))
    shared["bim"] = f(np.asarray(inp["ssm_b_im"]).transpose(0, 2, 1, 3).reshape(2, 64, G * 16))
    cl = lambda a: np.asarray(a).reshape(2, 4, 8, 16, 64).transpose(0, 2, 3, 1, 4).reshape(2, 128, 4, 64)
    shared["cn"] = f(np.concatenate([cl(inp["ssm_c_re"]), cl(inp["ssm_c_im"])], axis=3).reshape(2, 128, 512))
    pv = lambda a, n: f(np.asarray(a).reshape(2, n, 128).transpose(0, 2, 1))
    shared["ssmd"] = pv(inp["ssm_d"], 4)
    shared["bglu"] = pv(inp["b_glu"], 4)
    shared["wsc"] = f(np.asarray(inp["w_sconv"]).reshape(2, 3, 4, 128).transpose(0, 3, 2, 1).reshape(2, 128, 12))
    shared["bpw1"] = pv(inp["b_conf_pw1"], 16)
    shared["wdw"] = f(np.asarray(inp["w_conf_dw"]).reshape(2, 31, NCH, 128).transpose(0, 3, 2, 1).reshape(2, 128, NCH * 31))
    shared["lng"] = pv(inp["conf_ln_g"], NCH)
    shared["lnb"] = pv(inp["conf_ln_b"], NCH)
    shared["bpw2"] = pv(inp["b_conf_pw2"], NCH)
    shared["wfc"] = f(np.asarray(inp["w_ffn_conv"]).reshape(DEPTH, 3, NFF, 128).transpose(0, 3, 2, 1).reshape(DEPTH, 128, NFF * 3))
    for k in ("w_in_even", "w_glu", "w_out_even", "w_conf_pw1", "w_conf_pw2", "w_ffn_up", "w_ffn_down"):
        shared[k] = f(inp[k])
    xp = np.asarray(inp["x_prompt"], dtype=np.float32)
    xsmp = np.asarray(inp["x_sample"], dtype=np.float32)
    meta = np.asarray(inp["meta_tokens"], dtype=np.float32)
    maps = []
    for c in range(NCORES):
        m = dict(shared)
        sl = slice(c * NSEQ, (c + 1) * NSEQ)
        xpc = np.concatenate([meta, xp[c]], axis=0)
        xsc = xsmp[sl].transpose(1, 0, 2).reshape(TS, D)
        m["x"] = f(np.concatenate([xpc, xsc], axis=0).T)
        hre = np.asarray(inp["state_ssm_re"])[:, sl].transpose(0, 3, 2, 1)
        him = np.asarray(inp["state_ssm_im"])[:, sl].transpose(0, 3, 2, 1)
        m["h0"] = f(np.concatenate([hre, him], axis=1))
        m["sconv_h"] = f(np.asarray(inp["state_sconv"])[:, sl].transpose(0, 3, 2, 1).reshape(2, 512, 2 * NSEQ))
        m["cconv_h"] = f(np.asarray(inp["state_cconv"])[:, sl].transpose(0, 3, 2, 1).reshape(2, D, 30 * NSEQ))
        m["ffn_h"] = f(np.asarray(inp["state_ffn"])[:, sl].transpose(0, 3, 2, 1).reshape(DEPTH, DFF, 2 * NSEQ))
        maps.append(m)
    return maps


def _unpack_states(r):
    sc = np.asarray(r["sconv_o"]).reshape(2, 128, 4, 34).transpose(0, 2, 1, 3).reshape(2, 512, 34)
    c2 = np.asarray(r["cconv_o2"]).reshape(2, 128, NCH, 94).transpose(0, 2, 1, 3).reshape(2, D, 94)
    hist = np.asarray(r["cconv_o"])
    cc = np.concatenate([c2[:, :, 0:30], hist, c2[:, :, 30:94]], axis=2)
    ff = np.asarray(r["ffn_o"]).reshape(DEPTH, 128, NFF, 34).transpose(0, 2, 1, 3).reshape(DEPTH, DFF, 34)
    return sc, cc, ff


_CACHE = {}


def kernel(**inputs):
    if "B" not in _CACHE:
        _CACHE["B"] = build_program()
    B = _CACHE["B"]
    maps = _prep_inputs(inputs)
    res = run_bass_kernel_spmd(B.nc, maps, core_ids=list(range(NCORES)))
    R = res.results
    BATCH = NCORES
    y_prompt = np.zeros((BATCH, 2048, D), np.float32)
    y_sample = np.zeros((NCORES * NSEQ, 4, D), np.float32)
    p_re = np.zeros((2, BATCH, G, 64), np.float32)
    p_im = np.zeros((2, BATCH, G, 64), np.float32)
    p_sc = np.zeros((2, BATCH, 2, 512), np.float32)
    p_cc = np.zeros((2, BATCH, 30, D), np.float32)
    p_ff = np.zeros((DEPTH, BATCH, 2, DFF), np.float32)
    s_re = np.zeros((2, NCORES * NSEQ, G, 64), np.float32)
    s_im = np.zeros((2, NCORES * NSEQ, G, 64), np.float32)
    s_sc = np.zeros((2, NCORES * NSEQ, 2, 512), np.float32)
    s_cc = np.zeros((2, NCORES * NSEQ, 30, D), np.float32)
    s_ff = np.zeros((DEPTH, NCORES * NSEQ, 2, DFF), np.float32)
    for c in range(NCORES):
        r = R[c]
        sl = slice(c * NSEQ, (c + 1) * NSEQ)
        y = np.asarray(r["y"])
        y_prompt[c] = y[:, 16:2 * TP].T
        y_sample[sl] = y[:, 2 * TP:].T.reshape(4, NSEQ, D).transpose(1, 0, 2)
        hf = np.asarray(r["hfin"]).reshape(2, 128, G, 17)
        p_re[:, c] = hf[:, 0:64, :, 0].transpose(0, 2, 1)
        p_im[:, c] = hf[:, 64:128, :, 0].transpose(0, 2, 1)
        s_re[:, sl] = hf[:, 0:64, :, 1:].transpose(0, 3, 2, 1)
        s_im[:, sl] = hf[:, 64:128, :, 1:].transpose(0, 3, 2, 1)
        sc, cc, ff = _unpack_states(r)
        p_sc[:, c] = sc[:, :, 0:2].transpose(0, 2, 1)
        s_sc[:, sl] = sc[:, :, 2:].reshape(2, 512, 2, NSEQ).transpose(0, 3, 2, 1)
        p_cc[:, c] = cc[:, :, 0:30].transpose(0, 2, 1)
        s_cc[:, sl] = cc[:, :, 30:].reshape(2, D, 30, NSEQ).transpose(0, 3, 2, 1)
        p_ff[:, c] = ff[:, :, 0:2].transpose(0, 2, 1)
        s_ff[:, sl] = ff[:, :, 2:].reshape(DEPTH, DFF, 2, NSEQ).transpose(0, 3, 2, 1)
    return (y_prompt, y_sample, p_re, p_im, p_sc, p_cc, p_ff, s_re, s_im, s_sc, s_cc, s_ff)
```

```python
import os
import numpy as np
from contextlib import ExitStack
import concourse.bass as bass
import concourse.mybir as mybir
from concourse.bass_utils import run_bass_kernel_spmd

F32 = mybir.dt.float32
BF16 = mybir.dt.bfloat16
AF = mybir.ActivationFunctionType
ALU = mybir.AluOpType

NCORES = 8
D = 1024
DEPTH = 4
NCH = 8
DFF = 2816
NFF = 22
TP = 1032
NSEQ = 16
TS = 64
G = 32
EPS = 1e-6
POWS = [1 << k for k in range(11)]
TWO_PI = float(2 * np.pi)
MAGIC = 12582912.0


class Buf:
    __slots__ = ("w", "r")

    def __init__(self):
        self.w = None
        self.r = []


class DSem:
    def __init__(self, name):
        self.name = name
        self.total = 0
        self.sem = None


class Op:
    __slots__ = ("eng", "fn", "deps", "dma", "sem", "cnt", "sig", "dsem")

    def __init__(self, eng, fn, dma):
        self.eng = eng
        self.fn = fn
        self.dma = dma
        self.deps = []
        self.sem = None
        self.cnt = 0
        self.sig = False
        self.dsem = None


class Prog:
    ENGS = ("pe", "act", "dve", "pool", "sp")

    def __init__(self, nc):
        self.nc = nc
        self.ops = {e: [] for e in self.ENGS}
        self.dsems = []
        self.pending = {e: [] for e in self.ENGS}
        self.last = {e: None for e in self.ENGS}
        self.dma_since = []

    def dsem(self, name):
        d = DSem(name)
        self.dsems.append(d)
        return d

    def barrier(self):
        L = [self.last[e] for e in ("pe", "act", "dve", "pool") if self.last[e] is not None]
        L += self.dma_since
        self.dma_since = []
        for e in ("pe", "act", "dve", "sp", "pool"):
            self.pending[e] = self.pending[e] + L

    def op(self, eng, fn, reads=(), writes=(), dma=None, nobar=False):
        isdma = dma is not None
        o = Op(eng, fn, isdma)
        deps = []
        for b in reads:
            if b.w is not None:
                deps.append((b.w, "raw"))
        for b in writes:
            if b.w is not None:
                deps.append((b.w, "waw"))
            for r in b.r:
                deps.append((r, "war"))
        if not nobar:
            for d in self.pending[eng]:
                deps.append((d, "bar"))
            self.pending[eng] = []
        seen = set()
        for d, kind in deps:
            if d is o or id(d) in seen:
                continue
            if (not isdma) and (not d.dma) and d.eng == eng:
                if eng == "pe":
                    continue
            seen.add(id(d))
            if d.dma:
                o.deps.append((d, d.dsem.total))
            else:
                o.deps.append((d, None))
                d.sig = True
        for b in reads:
            b.r.append(o)
        for b in writes:
            b.w = o
            b.r = []
        if isdma:
            dma.total += 16
            o.dsem = dma
            o.cnt = dma.total
            o.sig = True
            if not nobar:
                self.dma_since.append(o)
        else:
            self.last[eng] = o
        self.ops[eng].append(o)
        return o

    def build(self, es):
        nc = self.nc
        sems = {e: es.enter_context(nc.semaphore("s_" + e)) for e in self.ENGS}
        for d in self.dsems:
            if d.total > 0:
                d.sem = es.enter_context(nc.semaphore("d_" + d.name))
        for e in self.ENGS:
            c = 0
            for o in self.ops[e]:
                if o.dma:
                    o.sem = o.dsem.sem
                elif o.sig:
                    c += 1
                    o.sem = sems[e]
                    o.cnt = c
        if os.environ.get("KBAR", "1") == "1":
            allsems = list(sems.values()) + [d.sem for d in self.dsems if d.sem is not None]
            for sm in allsems:
                nc.gpsimd.sem_clear(sm)
            nc.all_engine_barrier()
        block = es.enter_context(nc.Block())
        prog = self

        def run(engine, e):
            seen = {}
            used = {}
            for o in prog.ops[e]:
                need = {}
                for d, thr in o.deps:
                    c = thr if thr is not None else d.cnt
                    s = d.sem
                    if seen.get(s.num, 0) >= c:
                        continue
                    if need.get(s.num, (None, 0))[1] < c:
                        need[s.num] = (s, c)
                for s, c in need.values():
                    seen[s.num] = c
                    engine.wait_ge(s, c)
                ins = o.fn(engine)
                if o.sig:
                    ins.then_inc(o.sem, 16 if o.dma else 1)
                    if o.dma:
                        used[o.dsem.name] = o.dsem
            for d in used.values():
                if seen.get(d.sem.num, 0) < d.total:
                    engine.wait_ge(d.sem, d.total)

        @block.tensor
        def _(eng):
            run(eng, "pe")

        @block.scalar
        def _(eng):
            run(eng, "act")

        @block.vector
        def _(eng):
            run(eng, "dve")

        @block.gpsimd
        def _(eng):
            run(eng, "pool")

        @block.sync
        def _(eng):
            run(eng, "sp")


class Builder:
    def __init__(self):
        self.nc = bass.Bass("TRN2", target_bir_lowering=False)
        self.es = ExitStack()
        self.P = Prog(self.nc)
        self.din = {}
        self.dout = {}
        self.rr = 0

    def inp(self, name, shape):
        t = self.nc.dram_tensor(name, list(shape), F32, kind="ExternalInput").ap()
        self.din[name] = t
        return t

    def outp(self, name, shape):
        t = self.nc.dram_tensor(name, list(shape), F32, kind="ExternalOutput").ap()
        self.dout[name] = t
        return t

    def sb(self, name, shape, dt):
        return self.es.enter_context(self.nc.sbuf_tensor("sb_" + name, list(shape), dt))

    def arena_reset(self):
        self.P.barrier()
        self.aoff = 0

    def alloc(self, cols, dt):
        n32 = cols if dt == F32 else (cols + 1) // 2
        n32 = (n32 + 1) // 2 * 2
        o = self.aoff
        self.aoff += n32
        assert self.aoff <= self.ARENA, ("arena overflow", self.aoff)
        v = self.arena[:, o:o + n32]
        if dt != F32:
            v = v.bitcast(dt)
        return v

    def bank(self):
        i = self.rr % 8
        self.rr += 1
        return self.banks[i], self.bankb[i]

    def act(self, out, in_, func, reads, writes, bias=None, scale=None):
        kw = {}
        if bias is not None:
            kw["bias"] = bias
        if scale is not None:
            kw["scale"] = scale
        return self.P.op("act", lambda e: e.activation(out=out, in_=in_, func=func, **kw), reads, writes)

    def tt(self, eng, out, in0, in1, op, reads, writes):
        return self.P.op(eng, lambda e: e.tensor_tensor(out=out, in0=in0, in1=in1, op=op), reads, writes)

    def ts(self, eng, out, in0, s1, op0, reads, writes, s2=None, op1=None):
        if s2 is None:
            return self.P.op(eng, lambda e: e.tensor_scalar(out=out, in0=in0, scalar1=s1, scalar2=None, op0=op0), reads, writes)
        return self.P.op(eng, lambda e: e.tensor_scalar(out=out, in0=in0, scalar1=s1, scalar2=s2, op0=op0, op1=op1), reads, writes)

    def stt(self, out, in0, scalar, in1, op0, op1, reads, writes):
        return self.P.op("dve", lambda e: e.scalar_tensor_tensor(out=out, in0=in0, scalar=scalar, in1=in1, op0=op0, op1=op1), reads, writes)

    def cp(self, eng, out, in_, reads, writes):
        if eng == "act":
            return self.act(out, in_, AF.Copy, reads, writes)
        return self.P.op(eng, lambda e: e.tensor_copy(out=out, in_=in_), reads, writes)

    def memset(self, eng, ap, val, writes):
        return self.P.op(eng, lambda e: e.memset(ap, val), (), writes)

    def dma(self, eng, out, in_, dsem, reads=(), writes=(), nobar=False):
        return self.P.op(eng, lambda e: e.dma_start(out=out, in_=in_), reads, writes, dma=dsem, nobar=nobar)

    def slab(self, Wd, K, c0, ncols=256):
        KT = K // 128
        i = self.wrr % self.NSLOT
        self.wrr += 1
        t = self.wslots[i][:, 0:KT * ncols].rearrange("p (kt m) -> p kt m", m=ncols)
        src = Wd[:, c0:c0 + ncols].rearrange("(kt p) m -> p kt m", p=128)
        self.dma("pool", t, src, self.wsem[i], writes=[self.wbuf[i]], nobar=True)
        return t, self.wbuf[i]

    def mm_group(self, bank_ap, bankb, lhs_list, rhs_list, reads):
        n = len(lhs_list)

        def fn(e):
            ins = None
            for k in range(n):
                ins = e.matmul(bank_ap, lhs_list[k], rhs_list[k], start=(k == 0), stop=(k == n - 1))
            return ins
        return self.P.op("pe", fn, reads, [bankb])


class _Stop(Exception):
    pass


_CNT = [0]


def _chk(name):
    if os.environ.get("KSTOP", "") == name:
        raise _Stop()


def build_program():
    B = Builder()
    try:
        _build_body(B)
    except _Stop:
        pass
    B.P.build(B.es)
    if os.environ.get("KENDBAR", "1") == "1":
        B.nc.all_engine_barrier()
    return B


def _build_body(B):
    nc, P, es = B.nc, B.P, B.es
    W = TP + TS

    x_d = B.inp("x", [D, 2 * TP + TS])
    h0_d = B.inp("h0", [2, 128, G, NSEQ])
    sch_d = B.inp("sconv_h", [2, 512, 2 * NSEQ])
    cch_d = B.inp("cconv_h", [2, D, 30 * NSEQ])
    ffh_d = B.inp("ffn_h", [DEPTH, DFF, 2 * NSEQ])
    nmix_d = B.inp("nmix", [128, DEPTH, NCH])
    nffn_d = B.inp("nffn", [128, DEPTH, NCH])
    nfin_d = B.inp("nfin", [128, NCH])
    lamre_d = B.inp("lamre", [2, 128, G])
    lamim_d = B.inp("lamim", [2, 128, G])
    logdt_d = B.inp("logdt", [2, 128, G])
    bre_d = B.inp("bre", [2, 64, G * 16])
    bim_d = B.inp("bim", [2, 64, G * 16])
    cn_d = B.inp("cn", [2, 128, 4 * 128])
    ssmd_d = B.inp("ssmd", [2, 128, 4])
    bglu_d = B.inp("bglu", [2, 128, 4])
    wsc_d = B.inp("wsc", [2, 128, 4 * 3])
    bpw1_d = B.inp("bpw1", [2, 128, 16])
    wdw_d = B.inp("wdw", [2, 128, NCH * 31])
    lng_d = B.inp("lng", [2, 128, NCH])
    lnb_d = B.inp("lnb", [2, 128, NCH])
    bpw2_d = B.inp("bpw2", [2, 128, NCH])
    wfc_d = B.inp("wfc", [DEPTH, 128, NFF * 3])
    w_in_d = B.inp("w_in_even", [2, D, 2048])
    w_glu_d = B.inp("w_glu", [2, 512, 512])
    w_out_d = B.inp("w_out_even", [2, D, D])
    w_pw1_d = B.inp("w_conf_pw1", [2, D, 2048])
    w_pw2_d = B.inp("w_conf_pw2", [2, D, D])
    w_up_d = B.inp("w_ffn_up", [DEPTH, D, 2 * DFF])
    w_dn_d = B.inp("w_ffn_down", [DEPTH, DFF, D])

    y_d = B.outp("y", [D, 2 * TP + TS])
    hfin_d = B.outp("hfin", [2, 128, G * 17])
    sco_d = B.outp("sconv_o", [2, 128, 4 * 34])
    cco_d = B.outp("cconv_o", [2, D, 26 * NSEQ])
    cco2_d = B.outp("cconv_o2", [2, 128, NCH * 94])
    ffo_d = B.outp("ffn_o", [DEPTH, 128, NFF * 34])
    KD = os.environ.get("KDUMP", "")
    if KD:
        dbg_d = B.outp("dbg", [D, 2 * TP + TS])
        dbgsem = P.dsem("dbg")

    xs = [B.sb(f"x{c}", [128, W], F32) for c in range(NCH)]
    B.NSLOT = 4
    B.wslots = [B.sb(f"wsl{i}", [128, NFF * 128], BF16) for i in range(B.NSLOT)]
    B.wbuf = [Buf() for _ in range(B.NSLOT)]
    B.wsem = [P.dsem(f"w{i}") for i in range(B.NSLOT)]
    B.wrr = 0
    B.banks = [es.enter_context(nc.psum_tensor(f"bank{i}", [128, 512], F32)) for i in range(8)]
    B.bankb = [Buf() for _ in range(8)]
    B.ARENA = 32000
    B.arena = B.sb("arena", [128, B.ARENA], F32)
    B.aoff = 0

    identf = B.sb("identf", [128, 128], F32)
    identb = B.sb("identb", [128, 128], BF16)
    emat = B.sb("emat", [128, 128], F32)
    onesb = B.sb("onesb", [128, 128], BF16)
    gmask = B.sb("gmask", [128, 8], F32)
    cb = Buf()

    nmix = B.sb("nmix", [128, DEPTH, NCH], F32)
    nffn = B.sb("nffn", [128, DEPTH, NCH], F32)
    nfin = B.sb("nfin", [128, NCH], F32)
    lamre = B.sb("lamre", [128, 2, G], F32)
    lamim = B.sb("lamim", [128, 2, G], F32)
    logdt = B.sb("logdt", [128, 2, G], F32)
    bre = B.alloc(2 * G * 16, F32)[0:64, :].rearrange("p (i k) -> p i k", i=2)
    bim = B.alloc(2 * G * 16, F32)[0:64, :].rearrange("p (i k) -> p i k", i=2)
    pre = B.alloc(12 * G, F32).rearrange("p (a g) -> p a g", g=G)
    preb = B.alloc(2 * G * 16, F32)[0:64, :].rearrange("p (a k) -> p a k", a=2)
    wARG, wANG, wMAG, wT, wR, wSIN, wCOS, wRE, wIM = [B.alloc(11 * G, F32) for _ in range(9)]
    wB1, wB2 = B.alloc(G * 16, F32), B.alloc(G * 16, F32)
    cn = B.alloc(2 * 512, F32).rearrange("p (i k) -> p i k", i=2)
    ssmd = B.sb("ssmd", [128, 2, 4], F32)
    bglu = B.sb("bglu", [128, 2, 4], F32)
    wsc = B.sb("wsc", [128, 2, 12], F32)
    bpw1 = B.sb("bpw1", [128, 2, 16], F32)
    wdw = B.sb("wdw", [128, 2, NCH * 31], F32)
    lng = B.sb("lng", [128, 2, NCH], F32)
    lnb = B.sb("lnb", [128, 2, NCH], F32)
    bpw2 = B.sb("bpw2", [128, 2, NCH], F32)
    wfc = B.sb("wfc", [128, DEPTH, NFF * 3], F32)
    h0 = B.sb("h0", [128, 2, G * NSEQ], F32)
    hfin = B.sb("hfin", [128, 2, G * 17], F32)
    vt = B.sb("vt", [128, 2 * 11, G, 2], F32)
    bmch = B.sb("bmch", [128, 2 * 4, 128], BF16)
    cmch = B.sb("cmch", [128, 2 * 4, 128], BF16)
    car_f = B.sb("car_f", [128, DEPTH, NFF * 2], F32)
    car_s = B.sb("car_s", [128, 2, 4 * 2], F32)
    car_c = B.sb("car_c", [128, 2, NCH * 30], BF16)
    hfb = Buf()
    carb = Buf()

    csem = P.dsem("const")
    for t, d in [(nmix, nmix_d), (nffn, nffn_d), (nfin, nfin_d), (wfc, wfc_d.rearrange("l p k -> p l k"))]:
        B.dma("sp", t[:], d, csem, writes=[cb])
    for t, d in [(lamre, lamre_d), (lamim, lamim_d), (logdt, logdt_d), (bre, bre_d), (bim, bim_d), (cn, cn_d),
                 (ssmd, ssmd_d), (bglu, bglu_d), (wsc, wsc_d), (bpw1, bpw1_d), (wdw, wdw_d), (lng, lng_d),
                 (lnb, lnb_d), (bpw2, bpw2_d), (h0, h0_d.rearrange("i p g s -> i p (g s)"))]:
        B.dma("sp", t[:], d.rearrange("i p k -> p i k"), csem, writes=[cb])
    ddsem = P.dsem("d2d")
    for i in range(2):
        B.dma("sp", cco_d[i], cch_d[i, :, 4 * NSEQ:30 * NSEQ], ddsem)

    B.memset("dve", identf[:], 1.0, [cb])
    P.op("pool", lambda e: e.affine_select(out=identf[:], in_=identf[:], pattern=[[-1, 128]], compare_op=ALU.is_equal,
                                           fill=0.0, base=0, channel_multiplier=1), [cb], [cb])
    B.cp("dve", identb[:], identf[:], [cb], [cb])
    B.memset("dve", onesb[:], 1.0, [cb])
    B.memset("dve", emat[:], 0.0, [cb])
    B.cp("dve", emat[:], identf[:], [cb], [cb])
    B.cp("dve", emat[0:64, 64:128], identf[0:64, 0:64], [cb], [cb])
    B.cp("dve", emat[64:128, 0:64], identf[64:128, 64:128], [cb], [cb])
    B.memset("dve", gmask[:], 0.0, [cb])
    P.op("dve", lambda e: e.tensor_reduce(out=gmask[:], in_=identf[:].rearrange("p (g j) -> p g j", j=16),
                                          axis=mybir.AxisListType.X, op=ALU.add), [cb], [cb])
    B.memset("dve", hfin[:], 0.0, [hfb])
    B.memset("dve", car_f[:], 0.0, [carb])
    B.memset("dve", car_s[:], 0.0, [carb])
    B.memset("dve", car_c[:], 0.0, [carb])

    xsem = P.dsem("x")
    xb0 = [[Buf() for _ in range(3)] for _ in range(NCH)]
    for c in range(NCH):
        B.dma("sp", xs[c][:, 0:TP], x_d[c * 128:(c + 1) * 128, 0:TP], xsem, writes=xb0[c][:3])
    _chk("const")
    pb = Buf()
    R = [cb, pb]

    def pre_(k):
        return pre[:, k, :]

    def range_reduce(out, ang):
        B.ts("dve", pre_(10), ang, float(1 / TWO_PI), ALU.mult, R, [pb], s2=MAGIC, op1=ALU.add)
        B.ts("dve", pre_(10), pre_(10), -MAGIC, ALU.add, R, [pb])
        B.stt(out, pre_(10), -TWO_PI, ang, ALU.mult, ALU.add, R, [pb])
        B.ts("dve", out, out, -float(np.pi), ALU.max, R, [pb], s2=float(np.pi), op1=ALU.min)

    for i in range(2):
        B.act(pre_(0), logdt[:, i, :], AF.Exp, R, [pb])
        B.tt("dve", pre_(1), pre_(0), lamre[:, i, :], ALU.mult, R, [pb])
        B.tt("dve", pre_(2), pre_(0), lamim[:, i, :], ALU.mult, R, [pb])
        NW = 11 * G
        wkb = [Buf() for _ in range(22)]
        for k, d in enumerate(POWS):
            B.ts("dve", wARG[:, k * G:(k + 1) * G], pre_(1), float(d), ALU.mult, R, [wkb[k]])
            B.ts("dve", wANG[:, k * G:(k + 1) * G], pre_(2), float(d), ALU.mult, R, [wkb[11 + k]])
        RW = [cb, pb] + wkb
        B.act(wMAG[:, :], wARG[:, :], AF.Exp, RW, [pb])

        def rr_wide(out, ang):
            B.ts("dve", wT[:, :], ang, float(1 / TWO_PI), ALU.mult, RW, [pb], s2=MAGIC, op1=ALU.add)
            B.ts("dve", wT[:, :], wT[:, :], -MAGIC, ALU.add, RW, [pb])
            B.stt(out, wT[:, :], -TWO_PI, ang, ALU.mult, ALU.add, RW, [pb])
            B.ts("dve", out, out, -float(np.pi), ALU.max, RW, [pb], s2=float(np.pi), op1=ALU.min)
        rr_wide(wR[:, :], wANG[:, :])
        B.act(wSIN[:, :], wR[:, :], AF.Sin, RW, [pb])
        B.ts("dve", wANG[:, :], wANG[:, :], float(np.pi / 2), ALU.add, RW, [pb])
        rr_wide(wR[:, :], wANG[:, :])
        B.act(wCOS[:, :], wR[:, :], AF.Sin, RW, [pb])
        B.tt("dve", wRE[:, :], wMAG[:, :], wCOS[:, :], ALU.mult, RW, [pb])
        B.tt("dve", wIM[:, :], wMAG[:, :], wSIN[:, :], ALU.mult, RW, [pb])
        r3 = lambda t, lo, hi: t[lo:hi, :].rearrange("p (k g) -> p k g", g=G)
        B.cp("dve", vt[0:64, i * 11:(i + 1) * 11, :, 0], r3(wRE, 0, 64), RW, [pb])
        B.ts("dve", vt[64:128, i * 11:(i + 1) * 11, :, 0], r3(wIM, 64, 128), -1.0, ALU.mult, RW, [pb])
        B.cp("dve", vt[0:64, i * 11:(i + 1) * 11, :, 1], r3(wIM, 0, 64), RW, [pb])
        B.cp("dve", vt[64:128, i * 11:(i + 1) * 11, :, 1], r3(wRE, 64, 128), RW, [pb])
        B.cp("dve", pre_(8), wRE[:, 0:G], RW, [pb])
        B.cp("dve", pre_(9), wIM[:, 0:G], RW, [pb])
        for d in (1,):
            if d == 1:
                B.ts("dve", pre_(8), pre_(8), -1.0, ALU.add, R, [pb])
                B.tt("dve", pre_(3), lamre[:, i, :], lamre[:, i, :], ALU.mult, R, [pb])
                B.tt("dve", pre_(4), lamim[:, i, :], lamim[:, i, :], ALU.mult, R, [pb])
                B.tt("dve", pre_(3), pre_(3), pre_(4), ALU.add, R, [pb])
                P.op("dve", lambda e: e.reciprocal(out=pre[:, 3, :], in_=pre[:, 3, :]), R, [pb])
                B.tt("dve", pre_(5), pre_(8), lamre[:, i, :], ALU.mult, R, [pb])
                B.tt("dve", pre_(6), pre_(9), lamim[:, i, :], ALU.mult, R, [pb])
                B.tt("dve", pre_(5), pre_(5), pre_(6), ALU.add, R, [pb])
                B.tt("dve", pre_(11), pre_(5), pre_(3), ALU.mult, R, [pb])
                B.tt("dve", pre_(5), pre_(9), lamre[:, i, :], ALU.mult, R, [pb])
                B.tt("dve", pre_(6), pre_(8), lamim[:, i, :], ALU.mult, R, [pb])
                B.tt("dve", pre_(5), pre_(5), pre_(6), ALU.subtract, R, [pb])
                B.tt("dve", pre_(7), pre_(5), pre_(3), ALU.mult, R, [pb])
                B.ts("dve", pre_(6), pre_(7), -1.0, ALU.mult, R, [pb])
                pgb = [Buf(), Buf()]
                v3 = lambda t: t.rearrange("p (g j) -> p g j", j=16)
                bc = lambda row: pre[0:64, row, :].unsqueeze(2).to_broadcast([64, G, 16])
                B.tt("dve", v3(preb[:, 0, :]), v3(bre[:, i, :]), bc(11), ALU.mult, R, [pgb[0]])
                B.tt("dve", v3(wB1[0:64, :]), v3(bim[:, i, :]), bc(7), ALU.mult, R, [pgb[0]])
                B.tt("dve", preb[:, 0, :], preb[:, 0, :], wB1[0:64, :], ALU.subtract, R + [pgb[0]], [pgb[0]])
                B.tt("dve", v3(preb[:, 1, :]), v3(bim[:, i, :]), bc(11), ALU.mult, R, [pgb[1]])
                B.tt("dve", v3(wB2[0:64, :]), v3(bre[:, i, :]), bc(7), ALU.mult, R, [pgb[1]])
                B.tt("dve", preb[:, 1, :], preb[:, 1, :], wB2[0:64, :], ALU.add, R + [pgb[1]], [pgb[1]])
                for c in range(4):
                    bk, bb = B.bank()
                    P.op("pe", lambda e, c=c, bk=bk: e.transpose(bk[:, 0:64], preb[:, 0, c * 128:(c + 1) * 128], identf[0:64, 0:64]), R + pgb, [bb])
                    P.op("pe", lambda e, c=c, bk=bk: e.transpose(bk[:, 64:128], preb[:, 1, c * 128:(c + 1) * 128], identf[0:64, 0:64]), R + pgb + [bb], [bb])
                    B.cp("dve", bmch[:, i * 4 + c, :], bk[:, 0:128], [bb], [pb])
                    bk2, bb2 = B.bank()
                    P.op("pe", lambda e, c=c, bk2=bk2, i=i: e.transpose(bk2[:, 0:128], cn[:, i, c * 128:(c + 1) * 128], identf[:]), R, [bb2])
                    B.cp("dve", cmch[0:64, i * 4 + c, :], bk2[0:64, 0:128], [bb2], [pb])
                    B.ts("dve", cmch[64:128, i * 4 + c, :], bk2[64:128, 0:128], -1.0, ALU.mult, [bb2], [pb])

    _chk("pre")
    B.arena_reset()
    ysem = P.dsem("y")
    osem = {k: P.dsem(k) for k in ("ffo", "sco", "cco", "hfo", "hst", "hstp")}

    for tile in range(2):
        has_s = (tile == 1)
        ntiles = [(0, 344), (344, 688), (688, 1032)] + ([(TP, TP + TS)] if has_s else [])
        NT = len(ntiles)
        Wt = TP + (TS if has_s else 0)
        if tile == 0:
            xb = xb0
        else:
            xb = [[Buf() for _ in range(NT)] for _ in range(NCH)]
            B.P.barrier()
            for c in range(NCH):
                B.dma("sp", xs[c][:, 0:TP], x_d[c * 128:(c + 1) * 128, tile * TP:(tile + 1) * TP], xsem, writes=xb[c][:3])
                B.dma("sp", xs[c][:, TP:TP + TS], x_d[c * 128:(c + 1) * 128, 2 * TP:2 * TP + TS], xsem, writes=[xb[c][3]])

        def rmsnorm(gain_ap_fn, out_fn, out_bufs, out_dt_f32=False):
            sqs, sqbs, rss, rsbs = B.nrm_tmp
            tails = []
            for nt, (c0, c1) in enumerate(ntiles):
                n = c1 - c0
                sq, sqb, rs, rsb = sqs[nt % 2], sqbs[nt % 2], rss[nt % 2], rsbs[nt % 2]
                for c in range(NCH):
                    B.act(sq[c][:, 0:n], xs[c][:, c0:c1], AF.Square, [xb[c][nt]], [sqb[c]])
                bk, bb = B.bank()
                B.mm_group(bk[:, 0:n], bb, [onesb[:]] * NCH, [sq[c][:, 0:n] for c in range(NCH)], sqb + [cb])

                def _tail(nt=nt, n=n, c0=c0, c1=c1, bk=bk, bb=bb, rs=rs, rsb=rsb):
                    B.act(rs[:, 0:n], bk[:, 0:n], AF.Sqrt, [bb], [rsb], bias=float(EPS), scale=float(1.0 / D))
                    P.op("dve", lambda e, n=n, rs=rs: e.reciprocal(out=rs[:, 0:n], in_=rs[:, 0:n]), [rsb], [rsb])
                    for c in range(NCH):
                        B.stt(out_fn(c)[:, c0:c1], xs[c][:, c0:c1], gain_ap_fn(c), rs[:, 0:n], ALU.mult, ALU.mult,
                              [xb[c][nt], rsb, cb], [out_bufs[c][nt]])
                if tails:
                    tails.pop(0)()
                tails.append(_tail)
            while tails:
                tails.pop(0)()

        def proj(Wd, K, col_list, inp_aps, inp_bufs, evac, ncols=256):
            KT = K // 128
            for si, c0 in enumerate(col_list):
                wt, wb = B.slab(Wd, K, c0, ncols)
                for ms in range(ncols // 128):
                    for nt, (a, b_) in enumerate(ntiles):
                        n = b_ - a
                        bk, bb = B.bank()
                        B.mm_group(bk[:, 0:n], bb, [wt[:, kt, ms * 128:(ms + 1) * 128] for kt in range(KT)],
                                   [inp_aps[kt][:, a:b_] for kt in range(KT)], [wb] + [inp_bufs[kt][nt] for kt in range(KT)])
                        if os.environ.get("KNOEVAC", "") != "1":
                            evac(si, ms, nt, bk[:, 0:n], bb)
                        _CNT[0] += 1
                        _chk("g%d" % _CNT[0])

        for l in range(DEPTH):
            i = l // 2
            B.arena_reset()
            xn = [B.alloc(W, BF16) for _ in range(NCH)]
            xnb = [[Buf() for _ in range(NT)] for _ in range(NCH)]
            B.nrm_tmp = ([[B.alloc(344, BF16) for _ in range(NCH)] for _ in range(2)], [[Buf() for _ in range(NCH)] for _ in range(2)], [B.alloc(344, F32) for _ in range(2)], [Buf(), Buf()])
            rmsnorm(lambda c: nmix[:, l, c:c + 1], lambda c: xn[c], xnb)
            _chk(f"norm{tile}{l}")
            if l % 2 == 0:
                uf = [B.alloc(W, F32) for _ in range(4)]
                ub = [B.alloc(W, BF16) for _ in range(4)]
                ufb = [[Buf() for _ in range(NT)] for _ in range(4)]
                ubb = [[Buf() for _ in range(NT)] for _ in range(4)]
                ycat = [B.alloc(W, BF16) for _ in range(NCH)]
                ycb = [[Buf() for _ in range(NT)] for _ in range(NCH)]
                mark = B.aoff
                CW = 2 + TP + 32 + TS
                cin = [B.alloc(CW, F32) for _ in range(4)]
                cinb = [Buf() for _ in range(4)]
                tmpf = [B.alloc(344, F32) for _ in range(2)]
                tmpb = [Buf() for _ in range(2)]
                accf = [B.alloc(344, F32) for _ in range(2)]
                accb = [Buf() for _ in range(2)]
                sco = B.alloc(4 * 34, F32)
                scob = Buf()
                def ev_u(si, ms, nt, bk, bb):
                    m = si * 2 + ms
                    a, b_ = ntiles[nt]
                    if os.environ.get("KEV", "") != "dve":
                        B.cp("act", uf[m][:, a:b_], bk, [bb], [ufb[m][nt]])
                    if os.environ.get("KEV", "") != "act":
                        B.cp("dve", ub[m][:, a:b_], uf[m][:, a:b_], [ufb[m][nt]], [ubb[m][nt]])
                if os.environ.get("KSTOP", "") == "slab":
                    B.slab(w_in_d[i], D, 0)
                    raise _Stop()
                proj(w_in_d[i], D, [0, 256], xn, xnb, ev_u)
                _chk(f"uproj{tile}{l}")
                for m in range(4):
                    B.cp("dve", cin[m][:, 0:2], car_s[:, i, m * 2:m * 2 + 2], [carb], [cinb[m]])
                    if has_s:
                        B.dma("sp", cin[m][:, 2 + TP:2 + TP + 32], sch_d[i, m * 128:(m + 1) * 128, :], osem["hst"], writes=[cinb[m]])
                cnt = [0]
                for j in range(2):
                    wx, wxb = B.slab(w_in_d[i], D, 512 + 256 * j)
                    wc, wcb = B.slab(w_in_d[i], D, 1536 + 256 * j)
                    wg, wgb = B.slab(w_in_d[i], D, 1024 + 256 * j)
                    for ms in range(2):
                        m = 2 * j + ms
                        for nt, (a, b_) in enumerate(ntiles):
                            n = b_ - a
                            off = 2 + a if a < TP else 2 + TP + 32 + (a - TP)
                            sh = 1 if a < TP else NSEQ
                            rhs = [xn[kt][:, a:b_] for kt in range(NCH)]
                            rb = [xnb[kt][nt] for kt in range(NCH)]
                            bx, bxb = B.bank()
                            B.mm_group(bx[:, 0:n], bxb, [wx[:, kt, ms * 128:(ms + 1) * 128] for kt in range(NCH)], rhs, [wxb] + rb)
                            bc, bcb = B.bank()
                            B.mm_group(bc[:, 0:n], bcb, [wc[:, kt, ms * 128:(ms + 1) * 128] for kt in range(NCH)], rhs, [wcb] + rb)
                            bg, bgb = B.bank()
                            B.mm_group(bg[:, 0:n], bgb, [wg[:, kt, ms * 128:(ms + 1) * 128] for kt in range(NCH)], rhs, [wgb] + rb)
                            q = cnt[0] % 2
                            cnt[0] += 1
                            B.cp("act", tmpf[q][:, 0:n], bx[:, 0:n], [bxb], [tmpb[q]])
                            B.tt("dve", cin[m][:, off:off + n], tmpf[q][:, 0:n], bc[:, 0:n], ALU.mult, [tmpb[q], bcb], [cinb[m]])
                            w0 = wsc[:, i, m * 3 + 0:m * 3 + 1]
                            w1 = wsc[:, i, m * 3 + 1:m * 3 + 2]
                            w2 = wsc[:, i, m * 3 + 2:m * 3 + 3]
                            B.act(accf[q][:, 0:n], cin[m][:, off - 2 * sh:off - 2 * sh + n], AF.Copy, [cinb[m], cb], [accb[q]], scale=w0)
                            B.stt(accf[q][:, 0:n], cin[m][:, off - sh:off - sh + n], w1, accf[q][:, 0:n], ALU.mult, ALU.add, [cinb[m], accb[q], cb], [accb[q]])
                            B.stt(accf[q][:, 0:n], cin[m][:, off:off + n], w2, accf[q][:, 0:n], ALU.mult, ALU.add, [cinb[m], accb[q], cb], [accb[q]])
                            B.tt("dve", ycat[4 + m][:, a:b_], accf[q][:, 0:n], bg[:, 0:n], ALU.mult, [accb[q], bgb], [ycb[4 + m][nt]])
                        if tile == 0:
                            B.cp("dve", car_s[:, i, m * 2:m * 2 + 2], cin[m][:, TP:TP + 2], [cinb[m]], [carb])
                        else:
                            B.cp("dve", sco[:, m * 34:m * 34 + 2], cin[m][:, TP:TP + 2], [cinb[m]], [scob])
                            s0 = 2 + TP + 32 + 32
                            B.cp("dve", sco[:, m * 34 + 2:m * 34 + 34], cin[m][:, s0:s0 + 32], [cinb[m]], [scob])
                if tile == 1:
                    B.dma("sp", sco_d[i], sco[:, :], osem["sco"], reads=[scob])
                _chk(f"sconv{tile}{l}")
                B.arena_reset()
                B.aoff = mark
                XW = 1 + TP + 16 * 5
                S0 = 1 + TP
                X = [B.alloc(XW, BF16) for _ in range(8)]
                Xb = [Buf() for _ in range(8)]
                Am = B.alloc(8 * 11 * 128, BF16)
                Amb2 = [[[Buf(), Buf()] for _ in range(11)] for _ in range(8)]
                Amb = None
                Bm = B.alloc(8 * 128, BF16)
                Cm = B.alloc(8 * 128, BF16)
                Bmb = [Buf() for _ in range(8)]
                Cmb = [Buf() for _ in range(8)]
                yf = [B.alloc(344, F32) for _ in range(2)]
                yfb = [Buf() for _ in range(2)]
                t1 = [B.alloc(344, F32) for _ in range(2)]
                t1b = [Buf() for _ in range(2)]
                B.memset("dve", Cm[:, :], 0.0, Cmb)
                ev = [0]
                ytails = []

                def build_A(c, g8, k, Am=Am, Amb2=Amb2):
                    g = 8 * c + g8
                    row = i * 11 + k
                    o = (g8 * 11 + k) * 128
                    B.tt("dve", Am[:, o:o + 128].rearrange("p (a b) -> p a b", a=2), emat[:, :].rearrange("p (a b) -> p a b", a=2),
                         vt[:, row, g, :].unsqueeze(2).to_broadcast([128, 2, 64]), ALU.mult, [cb, pb], [Amb2[g8][k][0]])
                def prologue_builds(c):
                    for g8 in range(8):
                        g = 8 * c + g8
                        build_A(c, g8, 0)
                        B.ts("dve", Bm[:, g8 * 128:(g8 + 1) * 128], bmch[:, i * 4 + c, :], gmask[:, g8:g8 + 1], ALU.mult, [cb, pb], [Bmb[g8]])
                        B.cp("act", X[g8][:, 0:1], hfin[:, i, g * 17:g * 17 + 1], [hfb], [Xb[g8]])
                        if has_s:
                            B.cp("act", X[g8][:, S0:S0 + 16], h0[:, i, g * NSEQ:(g + 1) * NSEQ], [cb], [Xb[g8]])
                prologue_builds(0)
                for c in range(4):
                    for g8 in range(8):
                        g = 8 * c + g8
                        B.cp("act", Cm[:, g8 * 128 + g8 * 16:g8 * 128 + g8 * 16 + 16], cmch[:, i * 4 + c, g8 * 16:g8 * 16 + 16], [cb, pb], [Cmb[g8]])
                        for nt, (a, b_) in enumerate(ntiles):
                            n = b_ - a
                            bk, bb = B.bank()
                            B.mm_group(bk[:, 0:n], bb, [Bm[:, g8 * 128:(g8 + 1) * 128]], [ub[c][:, a:b_]], [Bmb[g8], ubb[c][nt]])
                            dst = X[g8][:, 1 + a:1 + b_] if a < TP else X[g8][:, S0 + 16:S0 + 16 + TS]
                            ev[0] += 1
                            B.cp("act" if ev[0] % 3 else "dve", dst, bk[:, 0:n], [bb], [Xb[g8]])
                    NX = 1 + TP
                    steps = []
                    d = 1
                    while d < NX:
                        first = 2 * d - 1
                        if first < NX:
                            steps.append((d, first, (NX - 1 - first) // (2 * d) + 1))
                        d *= 2
                    d //= 2
                    while d >= 1:
                        first = 3 * d - 1
                        if first < NX:
                            steps.append((d, first, (NX - 1 - first) // (2 * d) + 1))
                        d //= 2
                    built = {0}
                    for (d, first, cntp) in steps:
                        k = POWS.index(d)
                        for kk in (k, k + 1):
                            if kk <= 10 and kk not in built:
                                built.add(kk)
                                for g8 in range(8):
                                    build_A(c, g8, kk)
                        for p0 in range(0, cntp, 512):
                            pc = min(512, cntp - p0)
                            f0 = first + 2 * d * p0
                            for g8 in range(8):
                                g = 8 * c + g8
                                o = (g8 * 11 + k) * 128
                                dstv = X[g8][:, f0:f0 + 2 * d * (pc - 1) + 1:2 * d]
                                srcv = X[g8][:, f0 - d:f0 - d + 2 * d * (pc - 1) + 1:2 * d]
                                bk, bb = B.bank()
                                B.mm_group(bk[:, 0:pc], bb, [identb[:], Am[:, o:o + 128]], [dstv, srcv], [Xb[g8], Amb2[g8][k][0], cb])
                                ev[0] += 1
                                last = f0 + 2 * d * (pc - 1)
                                en = "dve" if (last == NX - 1 or ev[0] % 3 == 0) else "act"
                                B.cp(en, dstv, bk[:, 0:pc], [bb], [Xb[g8]])
                                if last == NX - 1:
                                    B.cp("dve", hfin[:, i, g * 17:g * 17 + 1], bk[:, pc - 1:pc], [bb], [hfb])
                    if has_s:
                        for t in range(4):
                            for g8 in range(8):
                                g = 8 * c + g8
                                o = (g8 * 11 + 0) * 128
                                dstv = X[g8][:, S0 + 16 * (t + 1):S0 + 16 * (t + 2)]
                                srcv = X[g8][:, S0 + 16 * t:S0 + 16 * (t + 1)]
                                bk, bb = B.bank()
                                B.mm_group(bk[:, 0:16], bb, [identb[:], Am[:, o:o + 128]], [dstv, srcv], [Xb[g8], Amb2[g8][0][0], cb])
                                ev[0] += 1
                                B.cp("dve" if (t == 3 or ev[0] % 2 == 0) else "act", dstv, bk[:, 0:16], [bb], [Xb[g8]])
                                if t == 3:
                                    B.cp("dve", hfin[:, i, g * 17 + 1:g * 17 + 17], bk[:, 0:16], [bb], [hfb])
                    if c + 1 < 4:
                        prologue_builds(c + 1)
                    for nt, (a, b_) in enumerate(ntiles):
                        n = b_ - a
                        bk, bb = B.bank()
                        rhs = [(X[g8][:, 1 + a:1 + b_] if a < TP else X[g8][:, S0 + 16:S0 + 16 + TS]) for g8 in range(8)]
                        B.mm_group(bk[:, 0:n], bb, [Cm[:, g8 * 128:(g8 + 1) * 128] for g8 in range(8)], rhs, Xb + Cmb)
                        q = nt % 2
                        B.stt(yf[q][:, 0:n], uf[c][:, a:b_], ssmd[:, i, c:c + 1], bk[:, 0:n], ALU.mult, ALU.add, [ufb[c][nt], bb, cb], [yfb[q]])
                        B.tt("dve", t1[q][:, 0:n], yf[q][:, 0:n], yf[q][:, 0:n], ALU.mult, [yfb[q]], [t1b[q]])
                        B.ts("dve", t1[q][:, 0:n], t1[q][:, 0:n], 0.044715, ALU.mult, [t1b[q]], [t1b[q]], s2=1.0, op1=ALU.add)
                        B.tt("dve", t1[q][:, 0:n], t1[q][:, 0:n], yf[q][:, 0:n], ALU.mult, [t1b[q], yfb[q]], [t1b[q]])
                        B.act(t1[q][:, 0:n], t1[q][:, 0:n], AF.Sigmoid, [t1b[q]], [t1b[q]], scale=1.5957691216057308)

                        def _tail(q=q, n=n, a=a, b_=b_, c=c, nt=nt):
                            B.tt("dve", uf[c][:, a:b_], yf[q][:, 0:n], t1[q][:, 0:n], ALU.mult, [yfb[q], t1b[q]], [ufb[c][nt]])
                            B.cp("act", ub[c][:, a:b_], uf[c][:, a:b_], [ufb[c][nt]], [ubb[c][nt]])
                        if ytails:
                            ytails.pop(0)()
                        ytails.append(_tail)
                    while ytails:
                        ytails.pop(0)()
                _chk(f"scan{tile}{l}")
                if tile == 1:
                    B.dma("sp", hfin_d[i], hfin[:, i, :], osem["hfo"], reads=[hfb])
                sg = [B.alloc(344, F32) for _ in range(2)]
                sgb = [Buf() for _ in range(2)]

                def ev_glu(si, ms, nt, bk, bb):
                    m = si * 2 + ms
                    a, b_ = ntiles[nt]
                    n = b_ - a
                    q = nt % 2
                    B.act(sg[q][:, 0:n], bk, AF.Sigmoid, [bb, cb], [sgb[q]], bias=bglu[:, i, m:m + 1])
                    B.tt("dve", ycat[m][:, a:b_], uf[m][:, a:b_], sg[q][:, 0:n], ALU.mult, [ufb[m][nt], sgb[q]], [ycb[m][nt]])
                proj(w_glu_d[i], 512, [0, 256], ub, ubb, ev_glu)

                def ev_out(si, ms, nt, bk, bb):
                    m = si * 2 + ms
                    a, b_ = ntiles[nt]
                    B.tt("dve", xs[m][:, a:b_], bk, xs[m][:, a:b_], ALU.add, [bb, xb[m][nt]], [xb[m][nt]])
                proj(w_out_d[i], D, [0, 256, 512, 768], ycat, ycb, ev_out)
            else:
                VW = 30 + TP + 30 * NSEQ + TS
                VS = 30 + TP
                vb_ = [B.alloc(VW, BF16) for _ in range(NCH)]
                vbb = [Buf() for _ in range(NCH)]
                cco = B.alloc(NCH * 94, F32)
                ccob = Buf()
                sgm = [B.alloc(344, F32) for _ in range(3)]
                sgmb = [Buf() for _ in range(3)]
                vf = [B.alloc(344, F32) for _ in range(3)]
                vfb = [Buf() for _ in range(3)]
                gtails = []
                for c in range(NCH):
                    B.cp("dve", vb_[c][:, 0:30], car_c[:, i, c * 30:(c + 1) * 30], [carb], [vbb[c]])
                    if has_s:
                        B.dma("pool", vb_[c][:, VS:VS + 30 * NSEQ], cch_d[i, c * 128:(c + 1) * 128, :], osem["hstp"], writes=[vbb[c]])
                cnt = [0]
                for j in range(4):
                    wa, wab = B.slab(w_pw1_d[i], D, 256 * j)
                    wg, wgb = B.slab(w_pw1_d[i], D, 1024 + 256 * j)
                    for ms in range(2):
                        m = 2 * j + ms
                        for nt, (a, b_) in enumerate(ntiles):
                            n = b_ - a
                            off = 30 + a if a < TP else VS + 30 * NSEQ
                            rhs = [xn[kt][:, a:b_] for kt in range(NCH)]
                            rb = [xnb[kt][nt] for kt in range(NCH)]
                            ba, bab = B.bank()
                            B.mm_group(ba[:, 0:n], bab, [wa[:, kt, ms * 128:(ms + 1) * 128] for kt in range(NCH)], rhs, [wab] + rb)
                            bg, bgb = B.bank()
                            B.mm_group(bg[:, 0:n], bgb, [wg[:, kt, ms * 128:(ms + 1) * 128] for kt in range(NCH)], rhs, [wgb] + rb)
                            q = cnt[0] % 3
                            cnt[0] += 1
                            B.act(sgm[q][:, 0:n], bg[:, 0:n], AF.Sigmoid, [bgb, cb], [sgmb[q]], bias=bpw1[:, i, 8 + m:8 + m + 1])
                            B.stt(vf[q][:, 0:n], ba[:, 0:n], bpw1[:, i, m:m + 1], sgm[q][:, 0:n], ALU.add, ALU.mult, [bab, sgmb[q], cb], [vfb[q]])
                            def _tail(q=q, n=n, m=m, off=off, nt=nt):
                                B.cp("act", vb_[m][:, off:off + n], vf[q][:, 0:n], [vfb[q]], [vbb[m]])
                                if tile == 1 and nt == 2:
                                    B.cp("dve", cco[:, m * 94:m * 94 + 30], vf[q][:, n - 30:n], [vfb[q]], [ccob])
                                if tile == 1 and nt == 3:
                                    B.cp("dve", cco[:, m * 94 + 30:m * 94 + 94], vf[q][:, 0:n], [vfb[q]], [ccob])
                            if gtails:
                                gtails.pop(0)()
                            gtails.append(_tail)
                        while gtails:
                            gtails.pop(0)()
                        if tile == 0:
                            B.cp("dve", car_c[:, i, m * 30:(m + 1) * 30], vb_[m][:, TP:TP + 30], [vbb[m]], [carb])
                if tile == 1:
                    B.dma("sp", cco2_d[i], cco[:, :], osem["cco"], reads=[ccob])
                cvf = [B.alloc(W, F32) for _ in range(NCH)]
                cvfb = [[Buf() for _ in range(NT)] for _ in range(NCH)]
                cvb = xn
                cvbb = xnb
                dg = [B.alloc(31 * 128, BF16) for _ in range(2)]
                dgb2 = [[Buf() for _ in range(31)] for _ in range(2)]
                def build_dg(c, dg=dg, dgb2=dgb2):
                    q = c % 2
                    for k in range(31):
                        if k % 2 == 0:
                            B.act(dg[q][:, k * 128:(k + 1) * 128], identf[:], AF.Copy, [cb], [dgb2[q][k]], scale=wdw[:, i, c * 31 + k:c * 31 + k + 1])
                        else:
                            B.ts("dve", dg[q][:, k * 128:(k + 1) * 128], identf[:], wdw[:, i, c * 31 + k:c * 31 + k + 1], ALU.mult, [cb], [dgb2[q][k]])
                build_dg(0)
                for c in range(NCH):
                    q = c % 2
                    if c + 1 < NCH:
                        build_dg(c + 1)
                    for nt, (a, b_) in enumerate(ntiles):
                        n = b_ - a
                        if a < TP:
                            rhs = [vb_[c][:, a + k:a + k + n] for k in range(31)]
                        else:
                            rhs = [vb_[c][:, VS + k * NSEQ:VS + k * NSEQ + n] for k in range(31)]
                        bk, bb = B.bank()
                        B.mm_group(bk[:, 0:n], bb, [dg[q][:, k * 128:(k + 1) * 128] for k in range(31)], rhs, dgb2[q] + [vbb[c]])
                        B.cp("dve", cvf[c][:, a:b_], bk[:, 0:n], [bb], [cvfb[c][nt]])
                        B.cp("act", cvb[c][:, a:b_], cvf[c][:, a:b_], [cvfb[c][nt]], [cvbb[c][nt]])
                mus = [B.alloc(344, F32) for _ in range(2)]
                rsds = [B.alloc(344, F32) for _ in range(2)]
                msqs = [B.alloc(344, F32) for _ in range(2)]
                stbs = [Buf(), Buf()]
                sact = cvb
                sactb = cvbb
                ltails = []
                for nt, (a, b_) in enumerate(ntiles):
                    n = b_ - a
                    pp = nt % 2
                    sq8, sq8b = B.nrm_tmp[0][pp], B.nrm_tmp[1][pp]
                    b1, b1b = B.bank()
                    B.mm_group(b1[:, 0:n], b1b, [onesb[:]] * NCH, [cvb[c][:, a:b_] for c in range(NCH)], [cvbb[c][nt] for c in range(NCH)] + [cb])
                    for c in range(NCH):
                        B.act(sq8[c][:, 0:n], cvf[c][:, a:b_], AF.Square, [cvfb[c][nt]], [sq8b[c]])
                    b2, b2b = B.bank()
                    B.mm_group(b2[:, 0:n], b2b, [onesb[:]] * NCH, [sq8[c][:, 0:n] for c in range(NCH)], sq8b + [cb])

                    def _tail(nt=nt, n=n, a=a, b_=b_, b1=b1, b1b=b1b, b2=b2, b2b=b2b, mu=mus[pp], rsd=rsds[pp], msq=msqs[pp], stb=stbs[pp]):
                        B.ts("dve", mu[:, 0:n], b1[:, 0:n], float(1.0 / D), ALU.mult, [b1b], [stb])
                        B.tt("dve", msq[:, 0:n], mu[:, 0:n], mu[:, 0:n], ALU.mult, [stb], [stb])
                        B.stt(msq[:, 0:n], b2[:, 0:n], float(1.0 / D), msq[:, 0:n], ALU.mult, ALU.subtract, [b2b, stb], [stb])
                        B.act(rsd[:, 0:n], msq[:, 0:n], AF.Sqrt, [stb], [stb], bias=float(EPS), scale=1.0)
                        P.op("dve", lambda e, n=n, rsd=rsd: e.reciprocal(out=rsd[:, 0:n], in_=rsd[:, 0:n]), [stb], [stb])
                        for c in range(NCH):
                            B.tt("dve", cvf[c][:, a:b_], cvf[c][:, a:b_], mu[:, 0:n], ALU.subtract, [cvfb[c][nt], stb], [cvfb[c][nt]])
                            B.tt("dve", cvf[c][:, a:b_], cvf[c][:, a:b_], rsd[:, 0:n], ALU.mult, [cvfb[c][nt], stb], [cvfb[c][nt]])
                            B.act(sact[c][:, a:b_], cvf[c][:, a:b_], AF.Silu, [cvfb[c][nt], cb], [sactb[c][nt]],
                                  bias=lnb[:, i, c:c + 1], scale=lng[:, i, c:c + 1])
                    if ltails:
                        ltails.pop(0)()
                    ltails.append(_tail)
                while ltails:
                    ltails.pop(0)()

                def ev_pw2(si, ms, nt, bk, bb):
                    m = si * 2 + ms
                    a, b_ = ntiles[nt]
                    B.stt(xs[m][:, a:b_], bk, bpw2[:, i, m:m + 1], xs[m][:, a:b_], ALU.add, ALU.add, [bb, xb[m][nt], cb], [xb[m][nt]])
                proj(w_pw2_d[i], D, [0, 256, 512, 768], sact, sactb, ev_pw2)

            if KD == f"mixer{l}":
                for c in range(NCH):
                    B.dma("sp", dbg_d[c * 128:(c + 1) * 128, tile * TP:(tile + 1) * TP], xs[c][:, 0:TP], dbgsem, reads=xb[c][:3])
                    if has_s:
                        B.dma("sp", dbg_d[c * 128:(c + 1) * 128, 2 * TP:2 * TP + TS], xs[c][:, TP:TP + TS], dbgsem, reads=[xb[c][3]])
            _chk(f"mixer{tile}{l}")
            B.arena_reset()
            xn = [B.alloc(W, BF16) for _ in range(NCH)]
            xnb = [[Buf() for _ in range(NT)] for _ in range(NCH)]
            B.nrm_tmp = ([[B.alloc(344, BF16) for _ in range(NCH)] for _ in range(2)], [[Buf() for _ in range(NCH)] for _ in range(2)], [B.alloc(344, F32) for _ in range(2)], [Buf(), Buf()])
            rmsnorm(lambda c: nffn[:, l, c:c + 1], lambda c: xn[c], xnb)
            hh = [B.alloc(W, BF16) for _ in range(NFF)]
            hb = [[Buf() for _ in range(NT)] for _ in range(NFF)]
            AW = 2 + TP + 32 + TS
            asb = [B.alloc(AW, F32) for _ in range(2)]
            asbb = [[Buf() for _ in range(NT + 2)] for _ in range(2)]
            acc = [B.alloc(344, F32) for _ in range(3)]
            accb = [Buf() for _ in range(3)]
            tails = []
            ffo = B.alloc(NFF * 34, F32)
            ffob = Buf()
            fhist = B.alloc(NFF * 32, F32)
            fhb = Buf()
            if has_s:
                B.dma("sp", fhist[:, :].rearrange("p (m k) -> p m k", k=32), ffh_d[l].rearrange("(m p) k -> p m k", p=128), osem["hst"], writes=[fhb])
            cnt = [0]
            for j in range(11):
                wa, wab = B.slab(w_up_d[l], D, 256 * j)
                wg, wgb = B.slab(w_up_d[l], D, DFF + 256 * j)
                for ms in range(2):
                    m = 2 * j + ms
                    s = m % 2
                    B.cp("dve", asb[s][:, 0:2], car_f[:, l, m * 2:m * 2 + 2], [carb], [asbb[s][0]])
                    if has_s:
                        B.cp("dve", asb[s][:, 2 + TP:2 + TP + 32], fhist[:, m * 32:(m + 1) * 32], [fhb], [asbb[s][NT + 1]])
                    for nt, (a, b_) in enumerate(ntiles):
                        n = b_ - a
                        off = 2 + a if a < TP else 2 + TP + 32
                        sh = 1 if a < TP else NSEQ
                        rhs = [xn[kt][:, a:b_] for kt in range(NCH)]
                        rb = [xnb[kt][nt] for kt in range(NCH)]
                        ba, bab = B.bank()
                        B.mm_group(ba[:, 0:n], bab, [wa[:, kt, ms * 128:(ms + 1) * 128] for kt in range(NCH)], rhs, [wab] + rb)
                        bg, bgb = B.bank()
                        B.mm_group(bg[:, 0:n], bgb, [wg[:, kt, ms * 128:(ms + 1) * 128] for kt in range(NCH)], rhs, [wgb] + rb)
                        q = cnt[0] % 3
                        cnt[0] += 1
                        prevb = asbb[s][NT + 1] if a >= TP else asbb[s][nt]
                        curb = asbb[s][1 + nt]
                        B.cp("act", asb[s][:, off:off + n], ba[:, 0:n], [bab], [curb])
                        w0 = wfc[:, l, m * 3 + 0:m * 3 + 1]
                        w1 = wfc[:, l, m * 3 + 1:m * 3 + 2]
                        w2 = wfc[:, l, m * 3 + 2:m * 3 + 3]
                        B.act(acc[q][:, 0:n], asb[s][:, off - 2 * sh:off - 2 * sh + n], AF.Copy, [prevb, curb, cb], [accb[q]], scale=w0)
                        B.stt(acc[q][:, 0:n], asb[s][:, off - sh:off - sh + n], w1, acc[q][:, 0:n], ALU.mult, ALU.add, [prevb, curb, accb[q], cb], [accb[q]])
                        B.stt(acc[q][:, 0:n], asb[s][:, off:off + n], w2, acc[q][:, 0:n], ALU.mult, ALU.add, [curb, accb[q], cb], [accb[q]])
                        def _tail(q=q, n=n, m=m, a=a, b_=b_, bg=bg, bgb=bgb, nt=nt):
                            B.act(acc[q][:, 0:n], acc[q][:, 0:n], AF.Silu, [accb[q]], [accb[q]])
                            B.tt("dve", hh[m][:, a:b_], acc[q][:, 0:n], bg[:, 0:n], ALU.mult, [accb[q], bgb], [hb[m][nt]])
                        if tails:
                            tails.pop(0)()
                        tails.append(_tail)
                    if tile == 0:
                        B.cp("dve", car_f[:, l, m * 2:m * 2 + 2], asb[s][:, TP:TP + 2], [asbb[s][3]], [carb])
                    else:
                        B.cp("dve", ffo[:, m * 34:m * 34 + 2], asb[s][:, TP:TP + 2], [asbb[s][3]], [ffob])
                        s0 = 2 + TP + 32 + 32
                        B.cp("dve", ffo[:, m * 34 + 2:m * 34 + 34], asb[s][:, s0:s0 + 32], [asbb[s][4]], [ffob])
            while tails:
                tails.pop(0)()
            if tile == 1:
                B.dma("sp", ffo_d[l], ffo[:, :], osem["ffo"], reads=[ffob])

            def ev_dn(si, ms, nt, bk, bb):
                m = si
                a, b_ = ntiles[nt]
                B.tt("dve", xs[m][:, a:b_], bk, xs[m][:, a:b_], ALU.add, [bb, xb[m][nt]], [xb[m][nt]])
            proj(w_dn_d[l], DFF, [128 * k for k in range(8)], hh, hb, ev_dn, ncols=128)

            if KD == f"ffn{l}":
                for c in range(NCH):
                    B.dma("sp", dbg_d[c * 128:(c + 1) * 128, tile * TP:(tile + 1) * TP], xs[c][:, 0:TP], dbgsem, reads=xb[c][:3])
                    if has_s:
                        B.dma("sp", dbg_d[c * 128:(c + 1) * 128, 2 * TP:2 * TP + TS], xs[c][:, TP:TP + TS], dbgsem, reads=[xb[c][3]])
            _chk(f"ffn{tile}{l}")
        B.arena_reset()
        yo = [B.alloc(W, F32) for _ in range(NCH)]
        yob = [[Buf() for _ in range(NT)] for _ in range(NCH)]
        B.nrm_tmp = ([[B.alloc(344, BF16) for _ in range(NCH)] for _ in range(2)], [[Buf() for _ in range(NCH)] for _ in range(2)], [B.alloc(344, F32) for _ in range(2)], [Buf(), Buf()])
        rmsnorm(lambda c: nfin[:, c:c + 1], lambda c: yo[c], yob)
        for c in range(NCH):
            B.dma("sp", y_d[c * 128:(c + 1) * 128, tile * TP:(tile + 1) * TP], yo[c][:, 0:TP], ysem, reads=yob[c][:3])
            if has_s:
                B.dma("sp", y_d[c * 128:(c + 1) * 128, 2 * TP:2 * TP + TS], yo[c][:, TP:TP + TS], ysem, reads=[yob[c][3]])
        _chk(f"tile{tile}")

    return B


def _prep_inputs(inp):
    f = lambda a: np.ascontiguousarray(np.asarray(a, dtype=np.float32))
    shared = {}
    vec = lambda a, n: f(np.asarray(a).reshape(-1, n, 128).transpose(2, 0, 1))
    shared["nmix"] = vec(inp["norm_mix"], NCH)
    shared["nffn"] = vec(inp["norm_ffn"], NCH)
    shared["nfin"] = f(np.asarray(inp["norm_final"]).reshape(NCH, 128).T)
    lam_t = lambda a: f(np.concatenate([np.asarray(a).transpose(0, 2, 1)] * 2, axis=1))
    shared["lamre"] = lam_t(inp["ssm_lam_re"])
    shared["lamim"] = lam_t(inp["ssm_lam_im"])
    shared["logdt"] = f(np.broadcast_to(np.asarray(inp["ssm_log_dt"])[:, None, :], (2, 128, G)))
    shared["bre"] = f(np.asarray(inp["ssm_b_re"]).transpose(0, 2, 1, 3).reshape(2, 64, G * 16))
    shared["bim"] = f(np.asarray(inp["ssm_b_im"]).transpose(0, 2, 1, 3).reshape(2, 64, G * 16))
    cl = lambda a: np.asarray(a).reshape(2, 4, 8, 16, 64).transpose(0, 2, 3, 1, 4).reshape(2, 128, 4, 64)
    shared["cn"] = f(np.concatenate([cl(inp["ssm_c_re"]), cl(inp["ssm_c_im"])], axis=3).reshape(2, 128, 512))
    pv = lambda a, n: f(np.asarray(a).reshape(2, n, 128).transpose(0, 2, 1))
    shared["ssmd"] = pv(inp["ssm_d"], 4)
    shared["bglu"] = pv(inp["b_glu"], 4)
    shared["wsc"] = f(np.asarray(inp["w_sconv"]).reshape(2, 3, 4, 128).transpose(0, 3, 2, 1).reshape(2, 128, 12))
    shared["bpw1"] = pv(inp["b_conf_pw1"], 16)
    shared["wdw"] = f(np.asarray(inp["w_conf_dw"]).reshape(2, 31, NCH, 128).transpose(0, 3, 2, 1).reshape(2, 128, NCH * 31))
    shared["lng"] = pv(inp["conf_ln_g"], NCH)
    shared["lnb"] = pv(inp["conf_ln_b"], NCH)
    shared["bpw2"] = pv(inp["b_conf_pw2"], NCH)
    shared["wfc"] = f(np.asarray(inp["w_ffn_conv"]).reshape(DEPTH, 3, NFF, 128).transpose(0, 3, 2, 1).reshape(DEPTH, 128, NFF * 3))
    for k in ("w_in_even", "w_glu", "w_out_even", "w_conf_pw1", "w_conf_pw2", "w_ffn_up", "w_ffn_down"):
        shared[k] = f(inp[k])
    xp = np.asarray(inp["x_prompt"], dtype=np.float32)
    xsmp = np.asarray(inp["x_sample"], dtype=np.float32)
    meta = np.asarray(inp["meta_tokens"], dtype=np.float32)
    maps = []
    for c in range(NCORES):
        m = dict(shared)
        sl = slice(c * NSEQ, (c + 1) * NSEQ)
        xpc = np.concatenate([meta, xp[c]], axis=0)
        xsc = xsmp[sl].transpose(1, 0, 2).reshape(TS, D)
        m["x"] = f(np.concatenate([xpc, xsc], axis=0).T)
        hre = np.asarray(inp["state_ssm_re"])[:, sl].transpose(0, 3, 2, 1)
        him = np.asarray(inp["state_ssm_im"])[:, sl].transpose(0, 3, 2, 1)
        m["h0"] = f(np.concatenate([hre, him], axis=1))
        m["sconv_h"] = f(np.asarray(inp["state_sconv"])[:, sl].transpose(0, 3, 2, 1).reshape(2, 512, 2 * NSEQ))
        m["cconv_h"] = f(np.asarray(inp["state_cconv"])[:, sl].transpose(0, 3, 2, 1).reshape(2, D, 30 * NSEQ))
        m["ffn_h"] = f(np.asarray(inp["state_ffn"])[:, sl].transpose(0, 3, 2, 1).reshape(DEPTH, DFF, 2 * NSEQ))
        maps.append(m)
    return maps


def _unpack_states(r):
    sc = np.asarray(r["sconv_o"]).reshape(2, 128, 4, 34).transpose(0, 2, 1, 3).reshape(2, 512, 34)
    c2 = np.asarray(r["cconv_o2"]).reshape(2, 128, NCH, 94).transpose(0, 2, 1, 3).reshape(2, D, 94)
    hist = np.asarray(r["cconv_o"])
    cc = np.concatenate([c2[:, :, 0:30], hist, c2[:, :, 30:94]], axis=2)
    ff = np.asarray(r["ffn_o"]).reshape(DEPTH, 128, NFF, 34).transpose(0, 2, 1, 3).reshape(DEPTH, DFF, 34)
    return sc, cc, ff


_CACHE = {}


def kernel(**inputs):
    if "B" not in _CACHE:
        _CACHE["B"] = build_program()
    B = _CACHE["B"]
    maps = _prep_inputs(inputs)
    res = run_bass_kernel_spmd(B.nc, maps, core_ids=list(range(NCORES)))
    R = res.results
    BATCH = NCORES
    y_prompt = np.zeros((BATCH, 2048, D), np.float32)
    y_sample = np.zeros((NCORES * NSEQ, 4, D), np.float32)
    p_re = np.zeros((2, BATCH, G, 64), np.float32)
    p_im = np.zeros((2, BATCH, G, 64), np.float32)
    p_sc = np.zeros((2, BATCH, 2, 512), np.float32)
    p_cc = np.zeros((2, BATCH, 30, D), np.float32)
    p_ff = np.zeros((DEPTH, BATCH, 2, DFF), np.float32)
    s_re = np.zeros((2, NCORES * NSEQ, G, 64), np.float32)
    s_im = np.zeros((2, NCORES * NSEQ, G, 64), np.float32)
    s_sc = np.zeros((2, NCORES * NSEQ, 2, 512), np.float32)
    s_cc = np.zeros((2, NCORES * NSEQ, 30, D), np.float32)
    s_ff = np.zeros((DEPTH, NCORES * NSEQ, 2, DFF), np.float32)
    for c in range(NCORES):
        r = R[c]
        sl = slice(c * NSEQ, (c + 1) * NSEQ)
        y = np.asarray(r["y"])
        y_prompt[c] = y[:, 16:2 * TP].T
        y_sample[sl] = y[:, 2 * TP:].T.reshape(4, NSEQ, D).transpose(1, 0, 2)
        hf = np.asarray(r["hfin"]).reshape(2, 128, G, 17)
        p_re[:, c] = hf[:, 0:64, :, 0].transpose(0, 2, 1)
        p_im[:, c] = hf[:, 64:128, :, 0].transpose(0, 2, 1)
        s_re[:, sl] = hf[:, 0:64, :, 1:].transpose(0, 3, 2, 1)
        s_im[:, sl] = hf[:, 64:128, :, 1:].transpose(0, 3, 2, 1)
        sc, cc, ff = _unpack_states(r)
        p_sc[:, c] = sc[:, :, 0:2].transpose(0, 2, 1)
        s_sc[:, sl] = sc[:, :, 2:].reshape(2, 512, 2, NSEQ).transpose(0, 3, 2, 1)
        p_cc[:, c] = cc[:, :, 0:30].transpose(0, 2, 1)
        s_cc[:, sl] = cc[:, :, 30:].reshape(2, D, 30, NSEQ).transpose(0, 3, 2, 1)
        p_ff[:, c] = ff[:, :, 0:2].transpose(0, 2, 1)
        s_ff[:, sl] = ff[:, :, 2:].reshape(DEPTH, DFF, 2, NSEQ).transpose(0, 3, 2, 1)
    return (y_prompt, y_sample, p_re, p_im, p_sc, p_cc, p_ff, s_re, s_im, s_sc, s_cc, s_ff)
```
